# Optimizing a Trainium2 kernel written in Bass

```python
import numpy as np
import jax
import jax.numpy as jnp
from jax import lax

D_MODEL = 1024
BATCH = 8
SEQ = 4096
DEPTH = 4

GRID_W = 64
ROPE_THETA = 10000.0
NORM_EPS = 1e-6
Q_BLOCK = 128
MIN_FORGET = 1e-6

N_ATTN_LAYERS = (DEPTH + 1) // 2
N_REC_LAYERS = DEPTH // 2

GQA_HEADS = 8
GQA_KV_HEADS = 2
GQA_HEAD_DIM = 64
MLA_HEADS = 8
MLA_Q_RANK = 256
MLA_KV_RANK = 128
MLA_NOPE_DIM = 64
MLA_ROPE_DIM = 32
MLA_V_DIM = 64
ATTN_IN_SPLITS = (GQA_HEADS * GQA_HEAD_DIM, GQA_KV_HEADS * GQA_HEAD_DIM, GQA_KV_HEADS * GQA_HEAD_DIM, MLA_Q_RANK, MLA_KV_RANK, MLA_ROPE_DIM)
ATTN_IN_DIM = sum(ATTN_IN_SPLITS)
ATTN_MIX_DIM = GQA_HEADS * GQA_HEAD_DIM + MLA_HEADS * MLA_V_DIM

HGRN_EXPAND = 128
HGRN_HEADS = D_MODEL // HGRN_EXPAND
HGRN_FORGET_DIM = HGRN_HEADS * HGRN_EXPAND
HGRN_HEAD_V = D_MODEL // HGRN_HEADS
HGRN_CHUNK = 64
REC_IN_DIM = 3 * HGRN_FORGET_DIM + 2 * HGRN_HEADS * HGRN_HEAD_V

FFN_HIDDEN = (((8 * D_MODEL + 2) // 3 + 255) // 256) * 256

kernel_name = 'hybrid_gqa_mla_hgrn2_encoder'


def _rmsnorm(x, gain):
    xf = x.astype(jnp.float32)
    y = xf * lax.rsqrt(jnp.mean(xf * xf, axis=-1, keepdims=True) + NORM_EPS)
    return (y * gain.astype(jnp.float32)).astype(x.dtype)


def _axial_rope_tables(seq_len, rot_dim):
    n_rows = seq_len // GRID_W
    row = jnp.repeat(jnp.arange(n_rows, dtype=jnp.float32), GRID_W)
    col = jnp.tile(jnp.arange(GRID_W, dtype=jnp.float32), n_rows)
    sec = rot_dim // 2
    inv = ROPE_THETA ** (-jnp.arange(0, sec, 2, dtype=jnp.float32) / sec)
    ang = jnp.concatenate([row[:, None] * inv, col[:, None] * inv], axis=-1)
    return jnp.cos(ang), jnp.sin(ang)


def _apply_axial_rope(x, cos, sin):
    B, S, H, Dr = x.shape
    xs = x.astype(jnp.float32).reshape(B, S, H, 2, 2, Dr // 4)
    c = cos.reshape(S, 2, Dr // 4)[None, :, None]
    s = sin.reshape(S, 2, Dr // 4)[None, :, None]
    x1 = xs[..., 0, :]
    x2 = xs[..., 1, :]
    out = jnp.stack([x1 * c - x2 * s, x1 * s + x2 * c], axis=-2)
    return out.reshape(B, S, H, Dr).astype(x.dtype)


def _block_attention(q, k, v, scale):
    B, S, H, D = q.shape
    Hk = k.shape[2]
    G = H // Hk
    Dv = v.shape[-1]
    nb = S // Q_BLOCK
    qb = q.reshape(B, nb, Q_BLOCK, Hk, G, D).transpose(1, 0, 2, 3, 4, 5)

    def one_block(q_blk):
        s = jnp.einsum('bqkgd,bskd->bkgqs', q_blk, k, preferred_element_type=jnp.float32) * scale
        p = jax.nn.softmax(s, axis=-1).astype(v.dtype)
        return jnp.einsum('bkgqs,bskd->bqkgd', p, v)

    o = lax.map(one_block, qb)
    return o.transpose(1, 0, 2, 3, 4, 5).reshape(B, S, H, Dv)


def _parallel_attention(h, w_in, gqa_qn, gqa_kn, cq_norm, ckv_norm, w_uq, w_ukv, mla_qn, mla_kn, w_out, rope_a, rope_b):
    B, S, _ = h.shape
    idx = np.cumsum(ATTN_IN_SPLITS)[:-1].tolist()
    q_a, k_a, v_a, c_q, c_kv, k_r = jnp.split(h @ w_in, idx, axis=-1)
    q_a = _apply_axial_rope(_rmsnorm(q_a.reshape(B, S, GQA_HEADS, GQA_HEAD_DIM), gqa_qn), *rope_a)
    k_a = _apply_axial_rope(_rmsnorm(k_a.reshape(B, S, GQA_KV_HEADS, GQA_HEAD_DIM), gqa_kn), *rope_a)
    v_a = v_a.reshape(B, S, GQA_KV_HEADS, GQA_HEAD_DIM)
    o_a = _block_attention(q_a, k_a, v_a, GQA_HEAD_DIM ** -0.5)
    q_b = (_rmsnorm(c_q, cq_norm) @ w_uq).reshape(B, S, MLA_HEADS, MLA_NOPE_DIM + MLA_ROPE_DIM)
    kv_b = (_rmsnorm(c_kv, ckv_norm) @ w_ukv).reshape(B, S, MLA_HEADS, MLA_NOPE_DIM + MLA_V_DIM)
    k_nope = kv_b[..., :MLA_NOPE_DIM]
    v_b = kv_b[..., MLA_NOPE_DIM:]
    k_rope = jnp.broadcast_to(k_r[:, :, None, :], (B, S, MLA_HEADS, MLA_ROPE_DIM))
    k_b = jnp.concatenate([k_nope, k_rope], axis=-1)
    q_b = _rmsnorm(q_b, mla_qn)
    k_b = _rmsnorm(k_b, mla_kn)
    q_b = jnp.concatenate([q_b[..., :MLA_NOPE_DIM], _apply_axial_rope(q_b[..., MLA_NOPE_DIM:], *rope_b)], axis=-1)
    k_b = jnp.concatenate([k_b[..., :MLA_NOPE_DIM], _apply_axial_rope(k_b[..., MLA_NOPE_DIM:], *rope_b)], axis=-1)
    o_b = _block_attention(q_b, k_b, v_b, (MLA_NOPE_DIM + MLA_ROPE_DIM) ** -0.5)
    o = jnp.concatenate([o_a.reshape(B, S, -1), o_b.reshape(B, S, -1)], axis=-1)
    return o @ w_out


def _chunk_gated_recurrence(q, k, v, log_f):
    N, S, H, K = q.shape
    V = v.shape[-1]
    L = HGRN_CHUNK
    nc = S // L

    def chunks(a):
        return a.reshape(N, nc, L, H, a.shape[-1]).transpose(1, 0, 3, 2, 4)

    qc, kc, vc, gc = chunks(q), chunks(k), chunks(v), chunks(log_f)
    bc = jnp.cumsum(gc, axis=3)
    causal = jnp.tril(jnp.ones((L, L), dtype=bool))[:, :, None]

    def step(state, inp):
        q_, k_, v_, b_ = inp
        o_inter = jnp.einsum('nhtk,nhkv->nhtv', q_ * jnp.exp(b_), state)
        diff = b_[:, :, :, None, :] - b_[:, :, None, :, :]
        decay = jnp.where(causal, jnp.exp(jnp.minimum(diff, 0.0)), 0.0)
        scores = jnp.einsum('nhtk,nhtsk,nhsk->nhts', q_, decay, k_)
        o_intra = jnp.einsum('nhts,nhsv->nhtv', scores, v_)
        b_last = b_[:, :, -1, :]
        k_dec = k_ * jnp.exp(jnp.minimum(b_last[:, :, None, :] - b_, 0.0))
        state = state * jnp.exp(b_last)[..., None] + jnp.einsum('nhsk,nhsv->nhkv', k_dec, v_)
        return state, o_inter + o_intra

    init = jnp.zeros((N, H, K, V), jnp.float32)
    _, o = lax.scan(step, init, (qc, kc, vc, bc))
    return o.transpose(1, 0, 3, 2, 4).reshape(N, S, H, V)


def _hgrn2_bidirectional(h, w_in, lower, out_norm, w_out):
    B, S, _ = h.shape
    F = HGRN_FORGET_DIM
    q, z_f, z_b, i, g = jnp.split(h @ w_in, [F, 2 * F, 3 * F, 3 * F + HGRN_HEADS * HGRN_HEAD_V], axis=-1)
    z = jnp.stack([z_f, jnp.flip(z_b, axis=1)]).astype(jnp.float32)
    lb = lower.astype(jnp.float32)[:, None, None, :]
    f = jnp.clip(lb + (1.0 - lb) * jax.nn.sigmoid(z), MIN_FORGET, 1.0 - MIN_FORGET)
    log_f = jnp.log(f)
    k = 1.0 - f
    qf = jax.nn.silu(q.astype(jnp.float32))
    vf = i.astype(jnp.float32)
    qs = jnp.stack([qf, jnp.flip(qf, axis=1)])
    vs = jnp.stack([vf, jnp.flip(vf, axis=1)])
    shp_k = (2 * B, S, HGRN_HEADS, HGRN_EXPAND)
    o = _chunk_gated_recurrence(qs.reshape(shp_k), k.reshape(shp_k), vs.reshape(2 * B, S, HGRN_HEADS, HGRN_HEAD_V), log_f.reshape(shp_k))
    o = o.reshape(2, B, S, HGRN_HEADS, HGRN_HEAD_V)
    o = o[0] + jnp.flip(o[1], axis=1)
    gate = jax.nn.silu(g.astype(jnp.float32)).reshape(B, S, HGRN_HEADS, HGRN_HEAD_V)
    o = _rmsnorm(o, out_norm) * gate
    return o.reshape(B, S, HGRN_HEADS * HGRN_HEAD_V).astype(h.dtype) @ w_out


def _swiglu(h, w_gate, w_up, w_down):
    return (jax.nn.silu(h @ w_gate) * (h @ w_up)) @ w_down


def setup_inputs(seed: int = 0) -> dict:
    key = jax.random.key(seed)
    ks = jax.random.split(key, 21)

    def w(k, shape, fan_in):
        return jax.random.normal(k, shape, jnp.float32) * (fan_in ** -0.5)

    def gain(k, shape):
        return 1.0 + 0.02 * jax.random.normal(k, shape, jnp.float32)

    NA, NR = N_ATTN_LAYERS, N_REC_LAYERS
    return {
        'x': jax.random.normal(ks[0], (BATCH, SEQ, D_MODEL), jnp.float32),
        'attn_norm': gain(ks[1], (NA, D_MODEL)),
        'attn_w_in': w(ks[2], (NA, D_MODEL, ATTN_IN_DIM), D_MODEL),
        'gqa_q_norm': gain(ks[3], (NA, GQA_HEAD_DIM)),
        'gqa_k_norm': gain(ks[4], (NA, GQA_HEAD_DIM)),
        'mla_cq_norm': gain(ks[5], (NA, MLA_Q_RANK)),
        'mla_ckv_norm': gain(ks[6], (NA, MLA_KV_RANK)),
        'mla_w_uq': w(ks[7], (NA, MLA_Q_RANK, MLA_HEADS * (MLA_NOPE_DIM + MLA_ROPE_DIM)), MLA_Q_RANK),
        'mla_w_ukv': w(ks[8], (NA, MLA_KV_RANK, MLA_HEADS * (MLA_NOPE_DIM + MLA_V_DIM)), MLA_KV_RANK),
        'mla_q_norm': gain(ks[9], (NA, MLA_NOPE_DIM + MLA_ROPE_DIM)),
        'mla_k_norm': gain(ks[10], (NA, MLA_NOPE_DIM + MLA_ROPE_DIM)),
        'attn_w_out': w(ks[11], (NA, ATTN_MIX_DIM, D_MODEL), ATTN_MIX_DIM),
        'rec_norm': gain(ks[12], (NR, D_MODEL)),
        'rec_w_in': w(ks[13], (NR, D_MODEL, REC_IN_DIM), D_MODEL),
        'rec_lower_bounds': 1.0 + 0.1 * jax.random.normal(ks[14], (2, NR, HGRN_FORGET_DIM), jnp.float32),
        'rec_out_norm': gain(ks[15], (NR, HGRN_HEAD_V)),
        'rec_w_out': w(ks[16], (NR, HGRN_HEADS * HGRN_HEAD_V, D_MODEL), HGRN_HEADS * HGRN_HEAD_V),
        'ffn_norm': gain(ks[17], (DEPTH, D_MODEL)),
        'ffn_w_gate': w(ks[18], (DEPTH, D_MODEL, FFN_HIDDEN), D_MODEL),
        'ffn_w_up': w(ks[19], (DEPTH, D_MODEL, FFN_HIDDEN), D_MODEL),
        'ffn_w_down': w(ks[20], (DEPTH, FFN_HIDDEN, D_MODEL), FFN_HIDDEN),
    }


def reference(x, attn_norm, attn_w_in, gqa_q_norm, gqa_k_norm, mla_cq_norm, mla_ckv_norm, mla_w_uq, mla_w_ukv, mla_q_norm, mla_k_norm, attn_w_out, rec_norm, rec_w_in, rec_lower_bounds, rec_out_norm, rec_w_out, ffn_norm, ffn_w_gate, ffn_w_up, ffn_w_down):
    S = x.shape[1]
    rope_a = _axial_rope_tables(S, GQA_HEAD_DIM)
    rope_b = _axial_rope_tables(S, MLA_ROPE_DIM)
    p = jax.nn.softmax(rec_lower_bounds.astype(jnp.float32), axis=1)
    lower = jnp.cumsum(p, axis=1) - p[:, :1]
    h = x
    for layer in range(DEPTH):
        j = layer // 2
        if layer % 2 == 0:
            h = h + _parallel_attention(_rmsnorm(h, attn_norm[j]), attn_w_in[j], gqa_q_norm[j], gqa_k_norm[j], mla_cq_norm[j], mla_ckv_norm[j], mla_w_uq[j], mla_w_ukv[j], mla_q_norm[j], mla_k_norm[j], attn_w_out[j], rope_a, rope_b)
        else:
            h = h + _hgrn2_bidirectional(_rmsnorm(h, rec_norm[j]), rec_w_in[j], lower[:, j], rec_out_norm[j], rec_w_out[j])
        h = h + _swiglu(_rmsnorm(h, ffn_norm[layer]), ffn_w_gate[layer], ffn_w_up[layer], ffn_w_down[layer])
    return h
```

```python
import numpy as np
from contextlib import ExitStack

import concourse.bass as bass
import concourse.mybir as mybir
from concourse.alu_op_type import AluOpType as ALU
from concourse.bass_utils import run_bass_kernel_spmd

F32 = mybir.dt.float32
BF16 = mybir.dt.bfloat16
AF = mybir.ActivationFunctionType
AX = mybir.AxisListType

S = 4096
D = 1024
DC = 8
FH = 2816
FC = 22
EPS = 1e-6
N_CORES = 8
DEPTH = 4
ARENA_BYTES = 204 * 1024


class Trk:
    __slots__ = ("name", "w", "r", "ap")

    def __init__(self, name="", ap=None):
        self.name = name
        self.w = None
        self.r = {}
        self.ap = ap

    def __getitem__(self, k):
        return self.ap[k]


class EngQ:
    def __init__(self, name):
        self.name = name
        self.n = 0
        self.waited = {}
        self.ops = []


ENGS = ("tensor", "vector", "scalar", "gpsimd", "sync")


class Em:
    def __init__(self):
        self.q = {e: EngQ(e) for e in ENGS}
        self.dval = {}
        self.groups = {}
        self.n_inst = 0

    def _wait(self, q, key, val):
        if q.waited.get(key, 0) < val:
            q.ops.append((0, key, val))
            q.waited[key] = val

    def _sync(self, q, reads, writes):
        own = q.name
        for t in reads:
            if t.w is not None:
                k, v = t.w
                if k == own and own == "tensor":
                    continue
                self._wait(q, k, v)
        for t in writes:
            if t.w is not None:
                k, v = t.w
                if not (k == own and own == "tensor"):
                    self._wait(q, k, v)
            for k, v in t.r.items():
                if k == own:
                    continue
                self._wait(q, k, v)

    @staticmethod
    def _mark(ev, reads, writes):
        k, v = ev
        for t in reads:
            if t.r.get(k, 0) < v:
                t.r[k] = v
        for t in writes:
            t.w = ev
            t.r = {}

    def op(self, eng, fn, reads=(), writes=()):
        q = self.q[eng]
        self._sync(q, reads, writes)
        q.n += 1
        q.ops.append((1, fn, eng, 1))
        self.n_inst += 1
        self._mark((eng, q.n), reads, writes)

    def dma(self, eng, pairs, reads=(), writes=(), sem=None, group=False, **kw):
        q = self.q[eng]
        self._sync(q, reads, writes)
        for (o, i) in pairs:
            self.dval[sem] = self.dval.get(sem, 0) + 16
            q.ops.append((1, I("dma_start", out=o, in_=i, **kw), sem, 16))
            self.n_inst += 1
        self._mark((sem, self.dval[sem]), reads, writes)
        if group:
            self.groups.setdefault(sem, []).extend(writes)

    def group_end(self, sem):
        for t in self.groups.pop(sem, []):
            if t.w is not None and t.w[0] == sem:
                t.w = (sem, self.dval[sem])

    def barrier(self):
        for q in self.q.values():
            for p in self.q.values():
                if p is not q and p.n > 0:
                    self._wait(q, p.name, p.n)
            for k, v in self.dval.items():
                self._wait(q, k, v)

    def final_wait(self):
        q = self.q["sync"]
        for k, v in self.dval.items():
            self._wait(q, k, v)
        for p in self.q.values():
            if p is not q and p.n > 0:
                self._wait(q, p.name, p.n)

    def sem_keys(self):
        return list(ENGS) + sorted(self.dval.keys())

    def replay(self, block, sems):
        for eng in ENGS:
            q = self.q[eng]

            def body(e, q=q):
                for o in q.ops:
                    if o[0] == 0:
                        e.wait_ge(sems[o[1]], o[2])
                    else:
                        o[1](e).then_inc(sems[o[2]], o[3])

            getattr(block, eng)(body)


class Arena:
    def __init__(self, handle, nbytes):
        self.h = handle
        self.cap = nbytes
        self.top = 0

    def reset(self, to=0):
        self.top = to

    def alloc(self, name, shape, dtype):
        n = 1
        for s in shape[1:]:
            n *= s
        esz = 4 if dtype == F32 else 2
        nb = (n * esz + 31) // 32 * 32
        off = self.top
        self.top += nb
        assert self.top <= self.cap, f"SBUF arena overflow at {name}: {self.top} > {self.cap}"
        ap = self.h[:, off // 4:(off + nb) // 4]
        if dtype != F32:
            ap = ap.bitcast(dtype)
        ap = ap[0:shape[0], 0:n]
        if len(shape) == 3:
            ap = ap.rearrange("p (a b) -> p a b", a=shape[1])
        elif len(shape) == 4:
            ap = ap.rearrange("p (a b c) -> p a b c", a=shape[1], b=shape[2])
        return Trk(name, ap)


class Ctx:
    pass


def I(method, *args, **kw):
    return lambda e: getattr(e, method)(*args, **kw)


def wload(E, dst, src, sem, eng="gpsimd"):
    C = dst.ap.shape[1]
    N = dst.ap.shape[2]
    srcv = src.rearrange("(c p) n -> p c n", p=128)
    pairs = []
    npieces = (N + 2047) // 2048
    step = (N + npieces - 1) // npieces
    for n0 in range(0, N, step):
        n1 = min(N, n0 + step)
        pairs.append((dst.ap[:, :, n0:n1], srcv[:, :, n0:n1]))
    E.dma(eng, pairs, writes=[dst], sem=sem)


def emit_rmsnorm_T(E, C, hbuf, gain, sq, ss_ps, lnv, rstd, xn, TN):
    E.op("scalar", I("activation", out=sq.ap, in_=hbuf.ap, func=AF.Square),
         reads=[hbuf], writes=[sq])
    for c in range(DC):
        E.op("tensor", I("matmul", ss_ps.ap[:, 0:TN], lhsT=C.ones.ap, rhs=sq.ap[:, c, :],
                                               start=(c == 0), stop=(c == DC - 1)),
             reads=[sq, C.ones], writes=[ss_ps])
    E.op("scalar", I("activation", out=lnv.ap, in_=ss_ps.ap[:, 0:TN], func=AF.Ln,
                                          scale=1.0 / D, bias=C.eps.ap[:, 0:1]),
         reads=[ss_ps, C.eps], writes=[lnv])
    E.op("scalar", I("activation", out=rstd.ap, in_=lnv.ap, func=AF.Exp, scale=-0.5),
         reads=[lnv], writes=[rstd])
    for c in range(DC):
        E.op("vector", I("scalar_tensor_tensor", out=xn.ap[:, c, :], in0=hbuf.ap[:, c, :], scalar=gain.ap[:, c:c + 1], in1=rstd.ap,
            op0=ALU.mult, op1=ALU.mult),
            reads=[hbuf, gain, rstd], writes=[xn])


def phase_ffn(E, C, layer, src, TN=256):
    A = C.arena
    A.reset(C.arena_base)
    NT = S // TN
    per = TN // 256
    wg = A.alloc("wg", [128, DC, FH], BF16)
    wu = A.alloc("wu", [128, DC, FH], BF16)
    wd = A.alloc("wd", [128, FC, D], BF16)
    gain = A.alloc("fgain", [128, DC], F32)
    hb = [A.alloc(f"hb{i}", [128, DC, TN], F32) for i in range(2)]
    sq = A.alloc("sq", [128, DC, TN], BF16)
    xn = A.alloc("xn", [128, DC, TN], BF16)
    lnv = A.alloc("lnv", [128, TN], F32)
    rstd = A.alloc("rstd", [128, TN], F32)
    sg = [A.alloc(f"sg{i}", [128, TN], F32) for i in range(2)]
    act = A.alloc("act", [128, FC, TN], BF16)
    ps = [Trk(f"ps{i}", C.psum[i].ap) for i in range(8)]
    ss_ps, g_ps, u_ps, y_ps = ps[0], ps[1:3], ps[3:5], ps[5:7]

    E.dma("sync", [(gain.ap, C.dram["ffn_norm"][layer])], writes=[gain], sem="d_small")
    wload(E, wg, C.dram["ffn_w_gate"][layer], "d_w0")
    wload(E, wu, C.dram["ffn_w_up"][layer], "d_w1")
    wload(E, wd, C.dram["ffn_w_down"][layer], "d_w2")

    srcv = src.rearrange("(c p) s -> p c s", p=128)
    dstv = C.hT.rearrange("(c p) s -> p c s", p=128)

    def htrk(t):
        return C.h_trk[t * per:(t + 1) * per]

    def load(t):
        b = hb[t % 2]
        E.dma("sync", [(b.ap, srcv[:, :, t * TN:(t + 1) * TN])], reads=htrk(t), writes=[b],
              sem=f"d_ld{t % 2}")

    def prologue(t):
        emit_rmsnorm_T(E, C, hb[t % 2], gain, sq, ss_ps, lnv, rstd, xn, TN)

    def gateup(t):
        for f in range(FC):
            gp, up, sgt = g_ps[f % 2], u_ps[f % 2], sg[f % 2]
            for c in range(DC):
                E.op("tensor", I("matmul", gp.ap[:, 0:TN], lhsT=wg.ap[:, c, f * 128:(f + 1) * 128], rhs=xn.ap[:, c, :],
                    start=(c == 0), stop=(c == DC - 1)), reads=[wg, xn], writes=[gp])
            for c in range(DC):
                E.op("tensor", I("matmul", up.ap[:, 0:TN], lhsT=wu.ap[:, c, f * 128:(f + 1) * 128], rhs=xn.ap[:, c, :],
                    start=(c == 0), stop=(c == DC - 1)), reads=[wu, xn], writes=[up])
            E.op("scalar", I("activation", out=sgt.ap, in_=gp.ap[:, 0:TN],
                                                                  func=AF.Silu),
                 reads=[gp], writes=[sgt])
            E.op("vector", I("tensor_tensor", out=act.ap[:, f, :], in0=sgt.ap, in1=up.ap[:, 0:TN], op=ALU.mult),
                reads=[sgt, up], writes=[act])

    def down(t):
        b = hb[t % 2]
        for dm in range(DC):
            yp = y_ps[dm % 2]
            for f in range(FC):
                E.op("tensor", I("matmul", yp.ap[:, 0:TN], lhsT=wd.ap[:, f, dm * 128:(dm + 1) * 128], rhs=act.ap[:, f, :],
                    start=(f == 0), stop=(f == FC - 1)), reads=[wd, act], writes=[yp])
            E.op("vector", I("tensor_tensor", out=b.ap[:, dm, :], in0=b.ap[:, dm, :], in1=yp.ap[:, 0:TN], op=ALU.add),
                reads=[b, yp], writes=[b])
        E.dma("sync", [(dstv[:, :, t * TN:(t + 1) * TN], b.ap)], reads=[b], writes=htrk(t),
              sem=f"d_st{t % 2}")

    load(0)
    prologue(0)
    for t in range(NT):
        if t + 1 < NT:
            load(t + 1)
        gateup(t)
        if t + 1 < NT:
            prologue(t + 1)
        down(t)
    E.barrier()


def phase_attn_proj(E, C, j, src):
    A = C.arena
    A.reset(C.arena_base)
    TN = 512
    NT = S // TN
    w_in = A.alloc("w_in", [128, DC, 1184], BF16)
    w_uq = A.alloc("w_uq", [128, 2, 768], BF16)
    w_ukv = A.alloc("w_ukv", [128, 1, 1024], BF16)
    wkr = A.alloc("wkr", [128, DC, 96], BF16)
    wuk = A.alloc("wuk", [128, 8, 96], BF16)
    gain = A.alloc("again", [128, DC], F32)
    g_qa = A.alloc("g_qa", [64, 1], F32)
    g_ka = A.alloc("g_ka", [64, 1], F32)
    g_cq = A.alloc("g_cq", [128, 2], F32)
    g_ckv = A.alloc("g_ckv", [128, 1], F32)
    g_qb = A.alloc("g_qb", [96, 1], F32)
    g_kb = A.alloc("g_kb", [96, 1], F32)
    tabs = [[A.alloc(f"tab{i}_{k}", [96, TN], F32) for k in range(4)] for i in range(2)]
    hb = [A.alloc(f"hb{i}", [128, DC, TN], F32) for i in range(2)]
    sq = A.alloc("sq", [128, DC, TN], BF16)
    xn = A.alloc("xn", [128, DC, TN], BF16)
    lnv = A.alloc("lnv", [128, TN], F32)
    rstd = A.alloc("rstd", [128, TN], F32)
    cq_raw = A.alloc("cq_raw", [128, 2, TN], F32)
    sq2 = A.alloc("sq2", [128, 2, TN], BF16)
    ln2 = A.alloc("ln2", [128, TN], F32)
    rstd2 = A.alloc("rstd2", [128, TN], F32)
    cqn = A.alloc("cqn", [128, 2, TN], BF16)
    ckvn = A.alloc("ckvn", [128, TN], BF16)
    va = A.alloc("va", [128, 4, 128], BF16)
    vb = A.alloc("vb", [128, 4, 512], BF16)
    ws = []
    for i in range(4):
        w = Ctx()
        w.usq = A.alloc(f"usq{i}", [96, TN], BF16)
        w.uln = A.alloc(f"uln{i}", [96, TN], F32)
        w.urstd = A.alloc(f"urstd{i}", [96, TN], F32)
        w.uxn = A.alloc(f"uxn{i}", [96, TN], BF16)
        w.ut1 = A.alloc(f"ut1{i}", [96, TN], F32)
        w.ut2 = A.alloc(f"ut2{i}", [96, TN], F32)
        w.uout = A.alloc(f"uout{i}", [96, TN], BF16)
        ws.append(w)
    ps = [Trk(f"ps{i}", C.psum[i].ap) for i in range(8)]
    ss_ps = ps[0]
    pj, uss, upx, pv = ps[1:4], ps[4:6], ps[6:8], ps[0]

    sm = "d_small"
    dbg = getattr(C, "dbg", 0)
    if dbg == 12:
        wload(E, w_in, C.dram["attn_w_in"][j], "d_w0")
        wload(E, w_uq, C.dram["mla_w_uq"][j], "d_w1")
        wload(E, w_ukv, C.dram["mla_w_ukv"][j], "d_w2")
        E.barrier()
        return
    if dbg == 13:
        E.op("vector", I("memset", wkr.ap, 0.0), writes=[wkr])
        E.op("vector", I("memset", wuk.ap, 0.0), writes=[wuk])
        E.op("vector", I("tensor_copy", out=wkr.ap[:, :, 64:96], in_=w_in.ap[:, :, 1152:1184]),
             reads=[w_in], writes=[wkr])
        E.barrier()
        return
    E.dma("sync", [(gain.ap, C.dram["attn_norm"][j])], writes=[gain], sem=sm, group=True)
    E.dma("sync", [(g_qa.ap, C.dram["gqa_q_norm"][j])], writes=[g_qa], sem=sm, group=True)
    E.dma("sync", [(g_ka.ap, C.dram["gqa_k_norm"][j])], writes=[g_ka], sem=sm, group=True)
    E.dma("sync", [(g_cq.ap, C.dram["mla_cq_norm"][j])], writes=[g_cq], sem=sm, group=True)
    E.dma("sync", [(g_ckv.ap, C.dram["mla_ckv_norm"][j])], writes=[g_ckv], sem=sm, group=True)
    E.dma("sync", [(g_qb.ap, C.dram["mla_q_norm"][j])], writes=[g_qb], sem=sm, group=True)
    E.dma("sync", [(g_kb.ap, C.dram["mla_k_norm"][j])], writes=[g_kb], sem=sm, group=True)
    E.group_end(sm)
    if dbg == 11:
        E.barrier()
        return
    wload(E, w_in, C.dram["attn_w_in"][j], "d_w0")
    wload(E, w_uq, C.dram["mla_w_uq"][j], "d_w1")
    wload(E, w_ukv, C.dram["mla_w_ukv"][j], "d_w2")
    E.op("vector", I("memset", wkr.ap, 0.0), writes=[wkr])
    E.op("vector", I("memset", wuk.ap, 0.0), writes=[wuk])
    E.op("vector", I("tensor_copy", out=wkr.ap[:, :, 64:96], in_=w_in.ap[:, :, 1152:1184]),
         reads=[w_in], writes=[wkr])
    ukv_h = w_ukv.ap[:, 0, :].rearrange("p (h x) -> p h x", h=8)
    if dbg != 14:
        E.op("vector", I("tensor_copy", out=wuk.ap[:, :, 0:64], in_=ukv_h[:, :, 0:64]),
             reads=[w_ukv], writes=[wuk])
    if dbg == 14 or dbg == 15:
        E.barrier()
        return

    srcv = src.rearrange("(c p) s -> p c s", p=128)
    dbg = getattr(C, "dbg", 0)
    if dbg == 1:
        E.barrier()
        return

    def load(t):
        b = hb[t % 2]
        sl = slice(t * TN, (t + 1) * TN)
        E.dma("sync", [(b.ap, srcv[:, :, sl])], reads=C.h_trk[2 * t:2 * t + 2], writes=[b], sem=f"d_ld{t % 2}")
        tb = tabs[t % 2]
        E.dma("sync", [(tb[0].ap[0:64, :], C.dram["c_CA"][:, sl])], writes=[tb[0]], sem=f"d_tb{t % 2}", group=True)
        E.dma("sync", [(tb[1].ap[0:64, :], C.dram["c_SA"][:, sl])], writes=[tb[1]], sem=f"d_tb{t % 2}", group=True)
        E.dma("sync", [(tb[2].ap, C.dram["c_CB"][:, sl])], writes=[tb[2]], sem=f"d_tb{t % 2}", group=True)
        E.dma("sync", [(tb[3].ap, C.dram["c_SB"][:, sl])], writes=[tb[3]], sem=f"d_tb{t % 2}", group=True)
        E.group_end(f"d_tb{t % 2}")

    ucount = [0]
    active = []

    def tick():
        for g in list(active):
            try:
                next(g)
            except StopIteration:
                active.remove(g)

    def drain():
        while active:
            tick()

    def push(g):
        active.append(g)
        for _ in range(3):
            tick()

    def unit(projfn, Dh, g, Pm, Ct, St, dst_ap, dst_trk):
        k = ucount[0]
        ucount[0] += 1
        w = ws[k % 4]
        pst = pj[k % 3]
        ssp, pxp = uss[k % 2], upx[k % 2]
        projfn(pst)
        yield
        E.op("scalar", I("activation", out=w.usq.ap[0:Dh, :], in_=pst.ap[0:Dh, :], func=AF.Square),
             reads=[pst], writes=[w.usq])
        yield
        E.op("tensor", I("matmul", ssp.ap[0:Dh, :], lhsT=C.ones.ap[0:Dh, 0:Dh], rhs=w.usq.ap[0:Dh, :],
                         start=True, stop=True), reads=[w.usq, C.ones], writes=[ssp])
        yield
        E.op("scalar", I("activation", out=w.uln.ap[0:Dh, :], in_=ssp.ap[0:Dh, :], func=AF.Ln,
                         scale=1.0 / Dh, bias=C.eps.ap[0:Dh, 0:1]),
             reads=[ssp, C.eps], writes=[w.uln])
        yield
        E.op("scalar", I("activation", out=w.urstd.ap[0:Dh, :], in_=w.uln.ap[0:Dh, :], func=AF.Exp,
                         scale=-0.5), reads=[w.uln], writes=[w.urstd])
        yield
        E.op("vector", I("scalar_tensor_tensor", out=w.uxn.ap[0:Dh, :], in0=pst.ap[0:Dh, :], scalar=g.ap[0:Dh, 0:1],
                         in1=w.urstd.ap[0:Dh, :], op0=ALU.mult, op1=ALU.mult),
             reads=[pst, g, w.urstd], writes=[w.uxn])
        yield
        E.op("tensor", I("matmul", pxp.ap[0:Dh, :], lhsT=Pm.ap[0:Dh, 0:Dh], rhs=w.uxn.ap[0:Dh, :],
                         start=True, stop=True), reads=[w.uxn, Pm], writes=[pxp])
        E.op("gpsimd", I("tensor_tensor", out=w.ut1.ap[0:Dh, :], in0=w.uxn.ap[0:Dh, :],
                         in1=Ct.ap[0:Dh, :], op=ALU.mult),
             reads=[w.uxn, Ct], writes=[w.ut1])
        yield
        E.op("vector", I("tensor_tensor", out=w.ut2.ap[0:Dh, :], in0=pxp.ap[0:Dh, :],
                         in1=St.ap[0:Dh, :], op=ALU.mult),
             reads=[pxp, St], writes=[w.ut2])
        yield
        E.op("gpsimd", I("tensor_tensor", out=w.uout.ap[0:Dh, :], in0=w.ut1.ap[0:Dh, :],
                         in1=w.ut2.ap[0:Dh, :], op=ALU.add),
             reads=[w.ut1, w.ut2], writes=[w.uout])
        yield
        E.dma("sync", [(dst_ap, w.uout.ap[0:Dh, :])], reads=[w.uout], writes=[dst_trk], sem=f"d_u{k % 4}")

    def win_proj(M, col0):
        def f(pst):
            for c in range(DC):
                E.op("tensor", I("matmul", pst.ap[0:M, :], lhsT=w_in.ap[:, c, col0:col0 + M], rhs=xn.ap[:, c, :],
                                 start=(c == 0), stop=(c == DC - 1)), reads=[w_in, xn], writes=[pst])
        return f

    pcount = [0]

    def proj(M, col0, cols=None):
        pst = pj[pcount[0] % 2]
        pcount[0] += 1
        for c in range(DC):
            E.op("tensor", I("matmul", pst.ap[0:M, :], lhsT=w_in.ap[:, c, col0:col0 + M],
                                                   rhs=xn.ap[:, c, :], start=(c == 0), stop=(c == DC - 1)),
                 reads=[w_in, xn], writes=[pst])
        return pst

    load(0)
    for t in range(NT):
        if t + 1 < NT:
            load(t + 1)
        sl = slice(t * TN, (t + 1) * TN)
        tb = tabs[t % 2]
        emit_rmsnorm_T(E, C, hb[t % 2], gain, sq, ss_ps, lnv, rstd, xn, TN)
        for h in range(8):
            push(unit(win_proj(64, h * 64), 64, g_qa, C.PA, tb[0], tb[1], C.QT[h, 0:64, sl], C.qt_trk))
        for g in range(2):
            push(unit(win_proj(64, 512 + g * 64), 64, g_ka, C.PA, tb[0], tb[1], C.KT[g, 0:64, sl], C.kt_trk))
        drain()
        if dbg == 5:
            E.barrier()
            return
        for sub in range(4):
            for c in range(DC):
                E.op("tensor", I("matmul", pv.ap[:, 0:128], lhsT=xn.ap[:, c, sub * 128:(sub + 1) * 128], rhs=w_in.ap[:, c, 640:768],
                    start=(c == 0), stop=(c == DC - 1)), reads=[xn, w_in], writes=[pv])
            E.op("vector", I("tensor_copy", out=va.ap[:, sub, :], in_=pv.ap[:, 0:128]),
                 reads=[pv], writes=[va])
        E.dma("sync", [(C.V[sl, 0:128].rearrange("(s p) n -> p s n", p=128), va.ap)],
              reads=[va], writes=[C.v_trk], sem="d_va")
        if dbg == 6:
            E.barrier()
            return
        cq_ps = []
        for jj in range(2):
            pst = proj(128, 768 + jj * 128)
            cq_ps.append(pst)
            E.op("scalar", I("activation", out=sq2.ap[:, jj, :], in_=pst.ap, func=AF.Square),
                 reads=[pst], writes=[sq2])
        if dbg == 71:
            E.barrier()
            return
        ssp = uss[0]
        for jj in range(2):
            E.op("tensor", I("matmul", ssp.ap, lhsT=C.ones.ap, rhs=sq2.ap[:, jj, :],
                                                     start=(jj == 0), stop=(jj == 1)),
                 reads=[sq2, C.ones], writes=[ssp])
        E.op("scalar", I("activation", out=ln2.ap, in_=ssp.ap, func=AF.Ln, scale=1.0 / 256,
                                              bias=C.eps.ap[:, 0:1]), reads=[ssp, C.eps], writes=[ln2])
        E.op("scalar", I("activation", out=rstd2.ap, in_=ln2.ap, func=AF.Exp, scale=-0.5),
             reads=[ln2], writes=[rstd2])
        if dbg == 72:
            E.barrier()
            return
        for jj in range(2):
            E.op("vector", I("scalar_tensor_tensor", out=cqn.ap[:, jj, :], in0=cq_ps[jj].ap, scalar=g_cq.ap[:, jj:jj + 1], in1=rstd2.ap,
                op0=ALU.mult, op1=ALU.mult), reads=[cq_ps[jj], g_cq, rstd2], writes=[cqn])
        if dbg == 7:
            E.barrier()
            return
        pst = proj(128, 1024)
        E.op("scalar", I("activation", out=sq2.ap[:, 0, :], in_=pst.ap, func=AF.Square),
             reads=[pst], writes=[sq2])
        ssp = uss[1]
        E.op("tensor", I("matmul", ssp.ap, lhsT=C.ones.ap, rhs=sq2.ap[:, 0, :], start=True, stop=True),
             reads=[sq2, C.ones], writes=[ssp])
        E.op("scalar", I("activation", out=ln2.ap, in_=ssp.ap, func=AF.Ln, scale=1.0 / 128,
                                              bias=C.eps.ap[:, 0:1]), reads=[ssp, C.eps], writes=[ln2])
        E.op("scalar", I("activation", out=rstd2.ap, in_=ln2.ap, func=AF.Exp, scale=-0.5),
             reads=[ln2], writes=[rstd2])
        E.op("vector", I("scalar_tensor_tensor", out=ckvn.ap, in0=pst.ap, scalar=g_ckv.ap[:, 0:1], in1=rstd2.ap, op0=ALU.mult, op1=ALU.mult),
            reads=[pst, g_ckv, rstd2], writes=[ckvn])
        if dbg == 8:
            E.barrier()
            return
        def uq_proj(h):
            def f(pst):
                for jj in range(2):
                    E.op("tensor", I("matmul", pst.ap[0:96, :], lhsT=w_uq.ap[:, jj, h * 96:(h + 1) * 96], rhs=cqn.ap[:, jj, :],
                                     start=(jj == 0), stop=(jj == 1)), reads=[w_uq, cqn], writes=[pst])
            return f

        def kb_proj(h):
            def f(pst):
                E.op("tensor", I("matmul", pst.ap[0:96, :], lhsT=wuk.ap[:, h, :], rhs=ckvn.ap, start=True, stop=False),
                     reads=[wuk, ckvn], writes=[pst])
                for c in range(DC):
                    E.op("tensor", I("matmul", pst.ap[0:96, :], lhsT=wkr.ap[:, c, :], rhs=xn.ap[:, c, :],
                                     start=False, stop=(c == DC - 1)), reads=[wkr, xn], writes=[pst])
            return f

        for h in range(8):
            push(unit(uq_proj(h), 96, g_qb, C.PB, tb[2], tb[3], C.QT[8 + h, 0:96, sl], C.qt_trk))
        for h in range(8):
            push(unit(kb_proj(h), 96, g_kb, C.PB, tb[2], tb[3], C.KT[2 + h, 0:96, sl], C.kt_trk))
        drain()
        for sub in range(4):
            E.op("tensor", I("matmul", pv.ap, lhsT=ckvn.ap[:, sub * 128:(sub + 1) * 128],
                                                       rhs=ukv_h[:, :, 64:128], start=True, stop=True),
                 reads=[ckvn, w_ukv], writes=[pv])
            E.op("vector", I("tensor_copy", out=vb.ap[:, sub, :], in_=pv.ap),
                 reads=[pv], writes=[vb])
        E.dma("sync", [(C.V[sl, 128:640].rearrange("(s p) n -> p s n", p=128), vb.ap)],
              reads=[vb], writes=[C.v_trk], sem="d_vb")
    E.barrier()


def phase_attn_core(E, C, j, src):
    A = C.arena
    A.reset(C.arena_base)
    oT_all = A.alloc("oT_all", [128, 8, S], BF16)
    w_out = A.alloc("w_out", [128, DC, D], BF16)
    ktb = [A.alloc(f"ktb{i}", [128, S], BF16) for i in range(2)]
    qtb = [A.alloc(f"qtb{i}", [128, S], BF16) for i in range(2)]
    vtb = [A.alloc(f"vtb{i}", [128, 32, 128], BF16) for i in range(2)]
    pt = [A.alloc(f"pt{i}", [128, 2, 512], BF16) for i in range(2)]
    rd = [A.alloc(f"rd{i}", [128, 512], F32) for i in range(2)]
    mark = A.top
    s_ps = [Trk(f"s_ps{i}", C.psum_all[:, i * 1024:(i + 1) * 1024]) for i in range(2)]
    o_ps = [Trk(f"o_ps{i}", C.psum[4 + i].ap) for i in range(2)]
    wload(E, w_out, C.dram["attn_w_out"][j], "d_w0")
    for i in range(2):
        E.op("vector", I("memset", vtb[i].ap[:, :, 64:128], 1.0), writes=[vtb[i]])
        E.op("vector", I("memset", ktb[i].ap[64:128, :], 0.0), writes=[ktb[i]])
        E.op("gpsimd", I("memset", qtb[i].ap[64:128, :], 0.0), writes=[qtb[i]])
    kv_loaded = [None, None]
    fin = 0
    for hh in range(16):
        b = hh % 2
        if hh < 8:
            Dh, kv, vc0 = 64, hh // 4, (hh // 4) * 64
        else:
            Dh, kv, vc0 = 96, 2 + (hh - 8), 128 + (hh - 8) * 64
        scale = float(Dh) ** -0.5
        E.dma("sync", [(qtb[b].ap[0:Dh, :], C.QT[hh, 0:Dh, :])], reads=[C.qt_trk], writes=[qtb[b]], sem=f"d_q{b}")
        if kv_loaded[b] != kv:
            E.dma("sync", [(ktb[b].ap[0:Dh, :], C.KT[kv, 0:Dh, :])], reads=[C.kt_trk], writes=[ktb[b]],
                  sem=f"d_k{b}")
            vsrc = C.V[:, vc0:vc0 + 64].rearrange("(k p) n -> p k n", p=128)
            E.dma("sync", [(vtb[b].ap[:, k0:k0 + 8, 0:64], vsrc[:, k0:k0 + 8, :]) for k0 in range(0, 32, 8)],
                  reads=[C.v_trk], writes=[vtb[b]], sem=f"d_v{b}")
            kv_loaded[b] = kv
        kt_, qt_, vt_ = ktb[b], qtb[b], vtb[b]
        pb = (hh % 2) * 64
        for qb in range(8):
            def smm(g):
                sp = s_ps[g % 2]
                for jj in range(2):
                    kt = 2 * g + jj
                    E.op("tensor", I("matmul", sp.ap[:, jj * 512:(jj + 1) * 512], lhsT=kt_.ap[:, kt * 128:(kt + 1) * 128],
                                     rhs=qt_.ap[:, qb * 512:(qb + 1) * 512], start=True, stop=True),
                         reads=[kt_, qt_], writes=[sp])
            op_ = o_ps[fin % 2]
            rdt = rd[fin % 2]
            fin += 1
            smm(0)
            for g in range(16):
                if g + 1 < 16:
                    smm(g + 1)
                sp, pp = s_ps[g % 2], pt[g % 2]
                E.op("scalar", I("activation", out=pp.ap.rearrange("p a b -> p (a b)"), in_=sp.ap, func=AF.Exp, scale=scale),
                     reads=[sp], writes=[pp])
                for jj in range(2):
                    kt = 2 * g + jj
                    E.op("tensor", I("matmul", op_.ap, lhsT=vt_.ap[:, kt, :], rhs=pp.ap[:, jj, :],
                                     start=(kt == 0), stop=(kt == 31)), reads=[pp, vt_], writes=[op_])
            E.op("vector", I("reciprocal", out=rdt.ap[64:128, :], in_=op_.ap[64:128, :]), reads=[op_], writes=[rdt])
            E.op("vector", I("tensor_tensor", out=oT_all.ap[pb:pb + 64, hh // 2, qb * 512:(qb + 1) * 512],
                             in0=op_.ap[0:64, :], in1=rdt.ap[64:128, :], op=ALU.mult),
                 reads=[op_, rdt], writes=[oT_all])
    E.barrier()

    A.reset(mark)
    TN = 512
    hb = [A.alloc(f"hb{i}", [128, DC, TN], F32) for i in range(2)]
    ps = [Trk(f"ps{i}", C.psum[i].ap) for i in range(8)]
    srcv = src.rearrange("(c p) s -> p c s", p=128)
    dstv = C.hT.rearrange("(c p) s -> p c s", p=128)

    def load(t):
        E.dma("sync", [(hb[t % 2].ap, srcv[:, :, t * TN:(t + 1) * TN])], reads=C.h_trk[2 * t:2 * t + 2],
              writes=[hb[t % 2]], sem=f"d_ld{t % 2}")

    load(0)
    for t in range(S // TN):
        if t + 1 < S // TN:
            load(t + 1)
        b = hb[t % 2]
        for dm in range(DC):
            yp = ps[dm % 4]
            for c in range(DC):
                E.op("tensor", I("matmul", yp.ap, lhsT=w_out.ap[:, c, dm * 128:(dm + 1) * 128],
                                 rhs=oT_all.ap[:, c, t * TN:(t + 1) * TN], start=(c == 0), stop=(c == DC - 1)),
                     reads=[w_out, oT_all], writes=[yp])
            E.op("vector", I("tensor_tensor", out=b.ap[:, dm, :], in0=b.ap[:, dm, :], in1=yp.ap, op=ALU.add),
                 reads=[b, yp], writes=[b])
        E.dma("sync", [(dstv[:, :, t * TN:(t + 1) * TN], b.ap)], reads=[b], writes=C.h_trk[2 * t:2 * t + 2],
              sem=f"d_st{t % 2}")
    E.barrier()


LN_LO = float(np.log(np.float32(1e-6)))
LN_HI = float(np.log(np.float32(1.0) - np.float32(1e-6)))


def _interleave(*gens):
    act = [g for g in gens if g is not None]
    while act:
        for g in list(act):
            try:
                next(g)
            except StopIteration:
                act.remove(g)


def phase_rec_dir(E, C, j, src, dirn):
    A = C.arena
    A.reset(C.arena_base)
    TN = 512
    NT = S // TN
    bwd = dirn == 1
    win = C.dram["rec_w_in"][j]
    wq = A.alloc("wq", [128, DC, 1024], BF16)
    wz = A.alloc("wz", [128, DC, 1024], BF16)
    wi = A.alloc("wi", [128, DC, 1024], BF16)
    if bwd:
        wg = A.alloc("wg", [128, DC, 1024], BF16)
        w_out = A.alloc("w_out", [128, DC, D], BF16)
        g_o = A.alloc("g_o", [128, 1], F32)
    gain = A.alloc("rgain", [128, DC], F32)
    lbv = A.alloc("lbv", [128, 8], F32)
    l0 = A.alloc("l0", [128, 8], F32)
    l1 = A.alloc("l1", [128, 8], F32)
    one = A.alloc("one", [128, 1], F32)
    mask = A.alloc("mask", [128, 128], F32)
    rst = A.alloc("rst", [128, TN], F32)
    hb = [A.alloc(f"hb{i}", [128, DC, TN], F32) for i in range(2 if not bwd else 1)]
    sq = A.alloc("sq", [128, DC, TN], BF16)
    xn = A.alloc("xn", [128, DC, TN], BF16)
    lnv = A.alloc("lnv", [128, TN], F32)
    rstd = A.alloc("rstd", [128, TN], F32)
    vtok = A.alloc("vtok", [128, 4, 1024], BF16)
    Ws = [[A.alloc(f"W{k}_{i}", [128, TN], F32) for i in range(6)] for k in range(2)]
    ek = A.alloc("ek", [128, TN], F32)
    eqs = [A.alloc(f"eq{i}", [128, TN], F32) for i in range(2)]
    qtls = [A.alloc(f"qtl{i}", [128, TN], BF16) for i in range(2)]
    ktl = A.alloc("ktl", [128, TN], BF16)
    ams = [A.alloc(f"am{i}", [128, 4, 128], BF16) for i in range(2)]
    ktokAs = [A.alloc(f"ktokA{i}", [128, 4, 128], BF16) for i in range(2)]
    ktokBs = [A.alloc(f"ktokB{i}", [128, 4, 128], BF16) for i in range(2)]
    state = A.alloc("state", [128, 8, 128], F32)
    smid = A.alloc("smid", [128, 128], BF16)
    tmp = A.alloc("tmp", [128, 128], F32)
    d1s = [A.alloc(f"d1{i}", [128, 8], F32) for i in range(2)]
    emids = [A.alloc(f"emid{i}", [128, 8], F32) for i in range(2)]
    if bwd:
        ofw = A.alloc("ofw", [128, TN], F32)
        osum = A.alloc("osum", [128, TN], F32)
        sqo = A.alloc("sqo", [128, TN], BF16)
        lno = A.alloc("lno", [128, TN], F32)
        rso = A.alloc("rso", [128, TN], F32)
        sgt = A.alloc("sgt", [128, TN], F32)
        ogT = A.alloc("ogT", [128, 8, TN], BF16)
    else:
        ofs = [A.alloc(f"ofs{i}", [128, TN], F32) for i in range(2)]
    ps = [Trk(f"ps{i}", C.psum[i].ap) for i in range(8)]
    ss_ps, q_ps, z_ps, a_ps, t_ps, o_ps = ps[0], ps[1], ps[2], ps[4], ps[5], ps[6]
    v_ps = ps[0]
    kv_ps = [ps[3], ps[7]]

    sm = "d_small"
    E.dma("sync", [(gain.ap, C.dram["rec_norm"][j])], writes=[gain], sem=sm, group=True)
    E.dma("sync", [(l0.ap, C.dram["rec_lb"][dirn, 0])], writes=[l0], sem=sm, group=True)
    E.dma("sync", [(l1.ap, C.dram["rec_lb"][dirn, 1])], writes=[l1], sem=sm, group=True)
    E.dma("sync", [(mask.ap, C.dram["c_maskb" if bwd else "c_maskf"])], writes=[mask], sem=sm, group=True)
    E.dma("sync", [(rst.ap, C.dram["c_rst"])], writes=[rst], sem=sm, group=True)
    wload(E, wq, win[:, 0:1024], "d_w0")
    wload(E, wz, win[:, (2048 if bwd else 1024):(3072 if bwd else 2048)], "d_w1")
    wload(E, wi, win[:, 3072:4096], "d_w2")
    if bwd:
        wload(E, wg, win[:, 4096:5120], "d_w3")
        wload(E, w_out, C.dram["rec_w_out"][j], "d_w4")
        E.dma("sync", [(g_o.ap, C.dram["rec_out_norm"][j])], writes=[g_o], sem=sm, group=True)
    E.group_end(sm)
    E.op("vector", I("memset", one.ap, 1.0), writes=[one])
    E.op("vector", I("memset", state.ap, 0.0), writes=[state])
    for i in range(2):
        E.op("vector", I("memset", ktokAs[i].ap, 0.0), writes=[ktokAs[i]])
        E.op("gpsimd", I("memset", ktokBs[i].ap, 0.0), writes=[ktokBs[i]])
    if j == 0:
        E.op("vector", I("memset", lbv.ap, 0.0), writes=[lbv])
    else:
        E.op("vector", I("tensor_tensor", out=l0.ap, in0=l0.ap, in1=l1.ap, op=ALU.subtract), reads=[l0, l1], writes=[l0])
        E.op("scalar", I("activation", out=l1.ap, in_=l0.ap, func=AF.Exp), reads=[l0], writes=[l1])
        E.op("scalar", I("activation", out=l0.ap, in_=l1.ap, func=AF.Ln, bias=one.ap[:, 0:1]), reads=[l1, one], writes=[l0])
        E.op("scalar", I("activation", out=lbv.ap, in_=l0.ap, func=AF.Exp, scale=-1.0), reads=[l0], writes=[lbv])

    srcv = src.rearrange("(c p) s -> p c s", p=128)
    dstv = C.hT.rearrange("(c p) s -> p c s", p=128)
    torder = list(range(NT - 1, -1, -1)) if bwd else list(range(NT))
    corder = list(range(7, -1, -1)) if bwd else list(range(8))
    ridx = 32 if bwd else 31
    lidx = 0 if bwd else 63
    mask3 = mask.ap.rearrange("p (a b) -> p a b", a=1).to_broadcast([128, 4, 128])
    porder = list(range(3, -1, -1)) if bwd else list(range(4))

    def v3(t):
        return t.ap.rearrange("p (c l) -> p c l", c=8)

    def load(i):
        t = torder[i]
        b = hb[i % len(hb)]
        E.dma("sync", [(b.ap, srcv[:, :, t * TN:(t + 1) * TN])], reads=C.h_trk[2 * t:2 * t + 2], writes=[b],
              sem=f"d_ld{i % len(hb)}")

    cnt = {"kv": 0, "cp": 0}

    def stage1a(hd):
        par = hd % 2
        hs = slice(hd * 128, (hd + 1) * 128)
        qs, u, la, lb_, Bt, Bm = Ws[par]
        for dc in range(DC):
            E.op("tensor", I("matmul", q_ps.ap, lhsT=wq.ap[:, dc, hs], rhs=xn.ap[:, dc, :],
                             start=(dc == 0), stop=(dc == DC - 1)), reads=[wq, xn], writes=[q_ps])
        for dc in range(DC):
            E.op("tensor", I("matmul", z_ps.ap, lhsT=wz.ap[:, dc, hs], rhs=xn.ap[:, dc, :],
                             start=(dc == 0), stop=(dc == DC - 1)), reads=[wz, xn], writes=[z_ps])
        yield
        E.op("scalar", I("activation", out=u.ap, in_=z_ps.ap, func=AF.Exp, scale=-1.0), reads=[z_ps], writes=[u])
        E.op("scalar", I("activation", out=la.ap, in_=u.ap, func=AF.Ln, bias=one.ap[:, 0:1]),
             reads=[u, one], writes=[la])
        E.op("scalar", I("activation", out=lb_.ap, in_=u.ap, func=AF.Ln, bias=one.ap[:, 0:1],
                         scale=lbv.ap[:, hd:hd + 1]), reads=[u, one, lbv], writes=[lb_])
        yield
        E.op("vector", I("tensor_tensor", out=lb_.ap, in0=lb_.ap, in1=la.ap, op=ALU.subtract),
             reads=[lb_, la], writes=[lb_])
        E.op("vector", I("tensor_scalar", out=lb_.ap, in0=lb_.ap, scalar1=LN_HI, scalar2=LN_LO,
                         op0=ALU.min, op1=ALU.max), reads=[lb_], writes=[lb_])
        E.op("scalar", I("activation", out=u.ap, in_=lb_.ap, func=AF.Exp), reads=[lb_], writes=[u])
        yield
        E.op("vector", I("tensor_tensor_scan", out=la.ap, data0=rst.ap, data1=lb_.ap, initial=0.0,
                         op0=ALU.mult, op1=ALU.add), reads=[rst, lb_], writes=[la])
        E.op("vector", I("tensor_scalar", out=u.ap, in0=u.ap, scalar1=-1.0, scalar2=1.0,
                         op0=ALU.mult, op1=ALU.add), reads=[u], writes=[u])
        yield
        if bwd:
            E.op("vector", I("tensor_tensor", out=Bt.ap, in0=lb_.ap, in1=la.ap, op=ALU.subtract),
                 reads=[lb_, la], writes=[Bt])
            E.op("vector", I("tensor_tensor", out=v3(Bt), in0=v3(Bt),
                             in1=v3(la)[:, :, 63:64].to_broadcast([128, 8, 64]), op=ALU.add),
                 reads=[Bt, la], writes=[Bt])
            Bsrc = Bt
        else:
            Bsrc = la
        E.op("vector", I("tensor_tensor", out=v3(Bm), in0=v3(Bsrc),
                         in1=v3(Bsrc)[:, :, ridx:ridx + 1].to_broadcast([128, 8, 64]), op=ALU.subtract),
             reads=[Bsrc], writes=[Bm])
        yield
        E.op("scalar", I("activation", out=qs.ap, in_=q_ps.ap, func=AF.Silu), reads=[q_ps], writes=[qs])
        yield

    def stage1b(hd):
        par = hd % 2
        eq, qtl, am, d1, emid = eqs[par], qtls[par], ams[par], d1s[par], emids[par]
        ktokA, ktokB = ktokAs[par], ktokBs[par]
        qs, u, la, lb_, Bt, Bm = Ws[par]
        Bsrc = Bt if bwd else la
        E.op("scalar", I("activation", out=eq.ap, in_=Bm.ap, func=AF.Exp), reads=[Bm], writes=[eq])
        E.op("scalar", I("activation", out=ek.ap, in_=Bm.ap, func=AF.Exp, scale=-1.0), reads=[Bm], writes=[ek])
        E.op("scalar", I("activation", out=d1.ap, in_=v3(Bsrc)[:, :, lidx], func=AF.Exp), reads=[Bsrc], writes=[d1])
        E.op("scalar", I("activation", out=emid.ap, in_=v3(Bsrc)[:, :, ridx], func=AF.Exp), reads=[Bsrc], writes=[emid])
        yield
        E.op("gpsimd", I("tensor_tensor", out=ktl.ap, in0=u.ap, in1=ek.ap, op=ALU.mult),
             reads=[u, ek], writes=[ktl])
        E.op("gpsimd", I("tensor_tensor", out=qtl.ap, in0=qs.ap, in1=eq.ap, op=ALU.mult),
             reads=[qs, eq], writes=[qtl])
        yield
        tpb = t_ps.ap.bitcast(BF16)
        for p in range(4):
            E.op("tensor", I("transpose", out=tpb[:, p * 128:(p + 1) * 128], in_=ktl.ap[:, p * 128:(p + 1) * 128],
                             identity=C.ident.ap), reads=[ktl, C.ident], writes=[t_ps])
        E.op("scalar", I("copy", out=ktokA.ap[0:64].rearrange("p a b -> p (a b)"), in_=tpb[0:64, 0:512]),
             reads=[t_ps], writes=[ktokA])
        E.op("scalar", I("copy", out=ktokB.ap[64:128].rearrange("p a b -> p (a b)"), in_=tpb[64:128, 0:512]),
             reads=[t_ps], writes=[ktokB])
        yield
        for p in range(4):
            ps_ = slice(p * 128, (p + 1) * 128)
            E.op("tensor", I("matmul", a_ps.ap[:, ps_], lhsT=ktl.ap[:, ps_], rhs=qtl.ap[:, ps_],
                             start=True, stop=True), reads=[ktl, qtl], writes=[a_ps])
        E.op("vector", I("tensor_tensor", out=am.ap, in0=a_ps.ap.rearrange("p (c l) -> p c l", c=4),
                         in1=mask3, op=ALU.mult), reads=[a_ps, mask], writes=[am])
        yield

    def stage2(hd, t, b):
        par = hd % 2
        hs = slice(hd * 128, (hd + 1) * 128)
        sl = slice(t * TN, (t + 1) * TN)
        eq, qtl, am, d1, emid = eqs[par], qtls[par], ams[par], d1s[par], emids[par]
        ktokA, ktokB = ktokAs[par], ktokBs[par]
        if bwd:
            E.dma("sync", [(ofw.ap, C.OB[hd, :, sl])], reads=[C.ob_trk[t]], writes=[ofw], sem="d_ofw")
        for p in porder:
            E.op("tensor", I("matmul", o_ps.ap[:, p * 128:(p + 1) * 128], lhsT=vtok.ap[:, p, hs], rhs=am.ap[:, p, :],
                             start=True, stop=False), reads=[vtok, am], writes=[o_ps])
            pair = [2 * p + 1, 2 * p] if bwd else [2 * p, 2 * p + 1]
            for ci, c in enumerate(pair):
                cs_ = slice(c * 64, (c + 1) * 64)
                kvp = kv_ps[cnt["kv"] % 2]
                cnt["kv"] += 1
                ktk = ktokA if c % 2 == 0 else ktokB
                E.op("tensor", I("matmul", kvp.ap[:, 0:128], lhsT=ktk.ap[:, p, :], rhs=vtok.ap[:, p, hs], start=True, stop=True),
                     reads=[ktk, vtok], writes=[kvp])
                E.op("vector", I("tensor_scalar", out=smid.ap, in0=state.ap[:, hd, :], scalar1=emid.ap[:, c:c + 1],
                                 scalar2=None, op0=ALU.mult), reads=[state, emid], writes=[smid])
                E.op("tensor", I("matmul", o_ps.ap[:, cs_], lhsT=smid.ap, rhs=qtl.ap[:, cs_], start=False, stop=(ci == 1)),
                     reads=[smid, qtl], writes=[o_ps])
                E.op("vector", I("tensor_scalar", out=tmp.ap, in0=kvp.ap[:, 0:128], scalar1=eq.ap[:, c * 64 + lidx:c * 64 + lidx + 1],
                                 scalar2=None, op0=ALU.mult), reads=[kvp, eq], writes=[tmp])
                E.op("vector", I("scalar_tensor_tensor", out=state.ap[:, hd, :], in0=state.ap[:, hd, :],
                                 scalar=d1.ap[:, c:c + 1], in1=tmp.ap, op0=ALU.mult, op1=ALU.add),
                     reads=[state, d1, tmp], writes=[state])
                yield
        if bwd:
            E.op("vector", I("tensor_tensor", out=osum.ap, in0=ofw.ap, in1=o_ps.ap, op=ALU.add),
                 reads=[ofw, o_ps], writes=[osum])
            yield
            return
        if not bwd:
            of = ofs[hd % 2]
            E.op("scalar", I("activation", out=of.ap, in_=o_ps.ap, func=AF.Identity), reads=[o_ps], writes=[of])
            E.dma("sync", [(C.OB[hd, :, sl], of.ap)], reads=[of], writes=[C.ob_trk[t]], sem=f"d_of{hd % 2}")
        yield

    def stage3(hd):
        hs = slice(hd * 128, (hd + 1) * 128)
        if True:
            E.op("scalar", I("activation", out=sqo.ap, in_=osum.ap, func=AF.Square), reads=[osum], writes=[sqo])
            E.op("tensor", I("matmul", v_ps.ap, lhsT=C.ones.ap, rhs=sqo.ap, start=True, stop=True),
                 reads=[sqo, C.ones], writes=[v_ps])
            yield
            E.op("scalar", I("activation", out=lno.ap, in_=v_ps.ap, func=AF.Ln, scale=1.0 / 128,
                             bias=C.eps.ap[:, 0:1]), reads=[v_ps, C.eps], writes=[lno])
            E.op("scalar", I("activation", out=rso.ap, in_=lno.ap, func=AF.Exp, scale=-0.5), reads=[lno], writes=[rso])
            E.op("vector", I("scalar_tensor_tensor", out=osum.ap, in0=osum.ap, scalar=g_o.ap[:, 0:1], in1=rso.ap,
                             op0=ALU.mult, op1=ALU.mult), reads=[osum, g_o, rso], writes=[osum])
            for dc in range(DC):
                E.op("tensor", I("matmul", v_ps.ap, lhsT=wg.ap[:, dc, hs], rhs=xn.ap[:, dc, :],
                                 start=(dc == 0), stop=(dc == DC - 1)), reads=[wg, xn], writes=[v_ps])
            yield
            E.op("scalar", I("activation", out=sgt.ap, in_=v_ps.ap, func=AF.Silu), reads=[v_ps], writes=[sgt])
            E.op("vector", I("tensor_tensor", out=ogT.ap[:, hd, :], in0=osum.ap, in1=sgt.ap, op=ALU.mult),
                 reads=[osum, sgt], writes=[ogT])
        yield

    load(0)
    for i, t in enumerate(torder):
        sl = slice(t * TN, (t + 1) * TN)
        if i + 1 < NT and len(hb) == 2:
            load(i + 1)
        b = hb[i % len(hb)]
        emit_rmsnorm_T(E, C, b, gain, sq, ss_ps, lnv, rstd, xn, TN)
        for p in range(4):
            for hg in range(2):
                for dc in range(DC):
                    E.op("tensor", I("matmul", v_ps.ap, lhsT=xn.ap[:, dc, p * 128:(p + 1) * 128],
                                     rhs=wi.ap[:, dc, hg * 512:(hg + 1) * 512], start=(dc == 0), stop=(dc == DC - 1)),
                         reads=[xn, wi], writes=[v_ps])
                if cnt["cp"] % 2 == 0:
                    E.op("vector", I("tensor_copy", out=vtok.ap[:, p, hg * 512:(hg + 1) * 512], in_=v_ps.ap),
                         reads=[v_ps], writes=[vtok])
                else:
                    E.op("scalar", I("copy", out=vtok.ap[:, p, hg * 512:(hg + 1) * 512], in_=v_ps.ap),
                         reads=[v_ps], writes=[vtok])
                cnt["cp"] += 1
        _interleave(stage1a(0))
        _interleave(stage1a(1), stage1b(0))
        for hd in range(8):
            _interleave(stage1a(hd + 2) if hd + 2 < 8 else None, stage1b(hd + 1) if hd + 1 < 8 else None,
                        stage2(hd, t, b), stage3(hd - 1) if (bwd and hd >= 1) else None)
        if bwd:
            _interleave(stage3(7))
        if bwd:
            for dm in range(DC):
                yp = ps[1 + dm % 2]
                for hd in range(8):
                    E.op("tensor", I("matmul", yp.ap, lhsT=w_out.ap[:, hd, dm * 128:(dm + 1) * 128], rhs=ogT.ap[:, hd, :],
                                     start=(hd == 0), stop=(hd == 7)), reads=[w_out, ogT], writes=[yp])
                E.op("vector", I("tensor_tensor", out=b.ap[:, dm, :], in0=b.ap[:, dm, :], in1=yp.ap, op=ALU.add),
                     reads=[b, yp], writes=[b])
            E.dma("sync", [(dstv[:, :, sl], b.ap)], reads=[b], writes=C.h_trk[2 * t:2 * t + 2], sem="d_st0")
            if i + 1 < NT:
                load(i + 1)
        elif len(hb) == 1 and i + 1 < NT:
            load(i + 1)
    E.barrier()


SMALL_SPECS = {
    "ffn_norm": [DEPTH, 128, DC],
    "attn_norm": [2, 128, DC],
    "gqa_q_norm": [2, 64, 1],
    "gqa_k_norm": [2, 64, 1],
    "mla_cq_norm": [2, 128, 2],
    "mla_ckv_norm": [2, 128, 1],
    "mla_q_norm": [2, 96, 1],
    "mla_k_norm": [2, 96, 1],
    "rec_norm": [2, 128, DC],
    "rec_lb": [2, 2, 128, DC],
    "rec_out_norm": [2, 128, 1],
    "c_ident": [128, 128],
    "c_PA": [64, 64],
    "c_PB": [96, 96],
    "c_CA": [64, S],
    "c_SA": [64, S],
    "c_CB": [96, S],
    "c_SB": [96, S],
    "c_maskf": [128, 128],
    "c_maskb": [128, 128],
    "c_rst": [128, 512],
}
WEIGHT_SPECS = {
    "ffn_w_gate": [DEPTH, D, FH],
    "ffn_w_up": [DEPTH, D, FH],
    "ffn_w_down": [DEPTH, FH, D],
    "attn_w_in": [2, D, 1184],
    "mla_w_uq": [2, 256, 768],
    "mla_w_ukv": [2, 128, 1024],
    "attn_w_out": [2, D, D],
    "rec_w_in": [2, D, 5120],
    "rec_w_out": [2, D, D],
}


def build_program(plan, debug=False, dbg=0):
    nc = bass.Bass("TRN2", target_bir_lowering=False)
    sck = "ExternalOutput" if debug else "Internal"
    E = Em()
    C = Ctx()
    C.dbg = dbg
    C.dram = {}
    xT = nc.dram_tensor("xT", [D, S], F32, kind="ExternalInput").ap()
    for name, shp in list(SMALL_SPECS.items()) + list(WEIGHT_SPECS.items()):
        C.dram[name] = nc.dram_tensor(name, shp, F32, kind="ExternalInput").ap()
    C.hT = nc.dram_tensor("yT", [D, S], F32, kind="ExternalOutput").ap()
    C.h_trk = [Trk(f"h{t}") for t in range(S // 256)]

    with ExitStack() as es:
        arena_h = es.enter_context(nc.sbuf_tensor("arena", [128, ARENA_BYTES // 4], F32))
        C.arena = Arena(arena_h, ARENA_BYTES)
        C.psum_all = es.enter_context(nc.psum_tensor("psall", [128, 4096], F32))[:, :]
        C.psum = [Trk(f"psb{i}", C.psum_all[:, i * 512:(i + 1) * 512]) for i in range(8)]
        C.ones = C.arena.alloc("ones", [128, 128], BF16)
        C.eps = C.arena.alloc("eps", [128, 1], F32)
        C.ident = C.arena.alloc("ident", [128, 128], BF16)
        C.PA = C.arena.alloc("PA", [64, 64], BF16)
        C.PB = C.arena.alloc("PB", [96, 96], BF16)
        C.arena_base = C.arena.top
        E.op("vector", I("memset", C.ones.ap, 1.0), writes=[C.ones])
        E.op("vector", I("memset", C.eps.ap, EPS), writes=[C.eps])
        E.dma("gpsimd", [(C.ident.ap, C.dram["c_ident"])], writes=[C.ident], sem="d_c0")
        E.dma("gpsimd", [(C.PA.ap, C.dram["c_PA"])], writes=[C.PA], sem="d_c1")
        E.dma("gpsimd", [(C.PB.ap, C.dram["c_PB"])], writes=[C.PB], sem="d_c2")
        C.QT = nc.dram_tensor("sc_QT", [16, 96, S], BF16, kind=sck).ap()
        C.KT = nc.dram_tensor("sc_KT", [10, 96, S], BF16, kind=sck).ap()
        C.V = nc.dram_tensor("sc_V", [S, 640], BF16, kind=sck).ap()
        C.OB = nc.dram_tensor("sc_OB", [8, 128, S], F32, kind=sck).ap()
        C.qt_trk = Trk("QT")
        C.kt_trk = Trk("KT")
        C.v_trk = Trk("V")
        C.ob_trk = [Trk(f"OB{t}") for t in range(8)]

        src = xT
        for (kind, layer) in plan:
            if kind == "ffn":
                phase_ffn(E, C, layer, src)
            elif kind == "attn":
                phase_attn_proj(E, C, layer, src)
                phase_attn_core(E, C, layer, src)
            elif kind == "attn_proj":
                phase_attn_proj(E, C, layer, src)
                continue
            elif kind == "attn_core":
                phase_attn_core(E, C, layer, src)
            elif kind == "rec_f":
                phase_rec_dir(E, C, layer, src, 0)
                continue
            elif kind == "rec_b":
                phase_rec_dir(E, C, layer, src, 1)
            elif kind == "rec":
                phase_rec_dir(E, C, layer, src, 0)
                phase_rec_dir(E, C, layer, src, 1)
            src = C.hT
        E.final_wait()

        sems = {}
        for k in E.sem_keys():
            sems[k] = es.enter_context(nc.semaphore(k))
        with nc.allow_low_precision("bf16 matmul operands, fp32 accumulation"):
            block = es.enter_context(nc.Block())
            E.replay(block, sems)
    return nc, E


def _rope_tables(rot_dim, lead):
    n_rows = S // 64
    row = np.repeat(np.arange(n_rows, dtype=np.float32), 64)
    col = np.tile(np.arange(64, dtype=np.float32), n_rows)
    sec = rot_dim // 2
    inv = (np.float32(10000.0) ** (-np.arange(0, sec, 2, dtype=np.float32) / np.float32(sec))).astype(np.float32)
    ang = np.concatenate([row[:, None] * inv, col[:, None] * inv], axis=-1).astype(np.float32)
    nf = rot_dim // 4
    Dh = lead + rot_dim
    Ct = np.ones((Dh, S), np.float32)
    St = np.zeros((Dh, S), np.float32)
    P = np.zeros((Dh, Dh), np.float32)
    for a in range(2):
        for j in range(nf):
            c = np.cos(ang[:, a * nf + j]).astype(np.float32)
            sn = np.sin(ang[:, a * nf + j]).astype(np.float32)
            i1 = lead + a * 2 * nf + j
            i2 = i1 + nf
            Ct[i1] = c
            Ct[i2] = c
            St[i1] = sn
            St[i2] = sn
            P[i2, i1] = -1.0
            P[i1, i2] = 1.0
    return Ct, St, P


def host_consts():
    out = {}
    out["c_ident"] = np.eye(128, dtype=np.float32)
    out["c_CA"], out["c_SA"], out["c_PA"] = _rope_tables(64, 0)
    out["c_CB"], out["c_SB"], out["c_PB"] = _rope_tables(32, 64)
    s_i = np.arange(128)[:, None]
    t_i = np.arange(128)[None, :]
    same = (s_i // 64) == (t_i // 64)
    out["c_maskf"] = ((s_i <= t_i) & same).astype(np.float32)
    out["c_maskb"] = ((s_i >= t_i) & same).astype(np.float32)
    rst = np.ones((128, 512), np.float32)
    rst[:, ::64] = 0.0
    out["c_rst"] = rst
    return out


def host_layout(inputs):
    out = dict(host_consts())
    f = lambda k: np.asarray(inputs[k], np.float32)
    out["ffn_norm"] = np.ascontiguousarray(f("ffn_norm").reshape(DEPTH, DC, 128).transpose(0, 2, 1))
    out["attn_norm"] = np.ascontiguousarray(f("attn_norm").reshape(2, DC, 128).transpose(0, 2, 1))
    out["rec_norm"] = np.ascontiguousarray(f("rec_norm").reshape(2, DC, 128).transpose(0, 2, 1))
    out["gqa_q_norm"] = np.ascontiguousarray(f("gqa_q_norm").reshape(2, 64, 1))
    out["gqa_k_norm"] = np.ascontiguousarray(f("gqa_k_norm").reshape(2, 64, 1))
    out["mla_cq_norm"] = np.ascontiguousarray(f("mla_cq_norm").reshape(2, 2, 128).transpose(0, 2, 1))
    out["mla_ckv_norm"] = np.ascontiguousarray(f("mla_ckv_norm").reshape(2, 128, 1))
    out["mla_q_norm"] = np.ascontiguousarray(f("mla_q_norm").reshape(2, 96, 1))
    out["mla_k_norm"] = np.ascontiguousarray(f("mla_k_norm").reshape(2, 96, 1))
    out["rec_lb"] = np.ascontiguousarray(f("rec_lower_bounds").reshape(2, 2, DC, 128).transpose(0, 1, 3, 2))
    out["rec_out_norm"] = np.ascontiguousarray(f("rec_out_norm").reshape(2, 128, 1))
    for k in WEIGHT_SPECS:
        out[k] = np.ascontiguousarray(np.asarray(inputs[k], np.float32))
    return out


FULL_PLAN = [("attn", 0), ("ffn", 0), ("rec", 0), ("ffn", 1), ("attn", 1), ("ffn", 2), ("rec", 1), ("ffn", 3)]


def kernel(**inputs):
    x = np.asarray(inputs["x"], np.float32)
    shared = host_layout(inputs)
    nc, _ = build_program(FULL_PLAN)
    in_maps = []
    for b in range(N_CORES):
        m = dict(shared)
        m["xT"] = np.ascontiguousarray(x[b].T)
        in_maps.append(m)
    res = run_bass_kernel_spmd(nc, in_maps, core_ids=list(range(N_CORES)))
    out = np.stack([np.asarray(res.results[b]["yT"]).T for b in range(N_CORES)])
    return np.ascontiguousarray(out.astype(np.float32))
```

```python
import numpy as np
from contextlib import ExitStack

import concourse.bass as bass
import concourse.mybir as mybir
from concourse.alu_op_type import AluOpType as ALU
from concourse.bass_utils import run_bass_kernel_spmd

F32 = mybir.dt.float32
BF16 = mybir.dt.bfloat16
AF = mybir.ActivationFunctionType
AX = mybir.AxisListType

S = 4096
D = 1024
DC = 8
FH = 2816
FC = 22
EPS = 1e-6
N_CORES = 8
DEPTH = 4
ARENA_BYTES = 204 * 1024


class Trk:
    __slots__ = ("name", "w", "r", "ap")

    def __init__(self, name="", ap=None):
        self.name = name
        self.w = None
        self.r = {}
        self.ap = ap

    def __getitem__(self, k):
        return self.ap[k]


class EngQ:
    def __init__(self, name):
        self.name = name
        self.n = 0
        self.waited = {}
        self.ops = []


ENGS = ("tensor", "vector", "scalar", "gpsimd", "sync")


class Em:
    def __init__(self):
        self.q = {e: EngQ(e) for e in ENGS}
        self.dval = {}
        self.groups = {}
        self.n_inst = 0

    def _wait(self, q, key, val):
        if q.waited.get(key, 0) < val:
            q.ops.append((0, key, val))
            q.waited[key] = val

    def _sync(self, q, reads, writes):
        own = q.name
        for t in reads:
            if t.w is not None:
                k, v = t.w
                if k == own and own == "tensor":
                    continue
                self._wait(q, k, v)
        for t in writes:
            if t.w is not None:
                k, v = t.w
                if not (k == own and own == "tensor"):
                    self._wait(q, k, v)
            for k, v in t.r.items():
                if k == own:
                    continue
                self._wait(q, k, v)

    @staticmethod
    def _mark(ev, reads, writes):
        k, v = ev
        for t in reads:
            if t.r.get(k, 0) < v:
                t.r[k] = v
        for t in writes:
            t.w = ev
            t.r = {}

    def op(self, eng, fn, reads=(), writes=()):
        q = self.q[eng]
        self._sync(q, reads, writes)
        q.n += 1
        q.ops.append((1, fn, eng, 1))
        self.n_inst += 1
        self._mark((eng, q.n), reads, writes)

    def dma(self, eng, pairs, reads=(), writes=(), sem=None, group=False, **kw):
        q = self.q[eng]
        self._sync(q, reads, writes)
        for (o, i) in pairs:
            self.dval[sem] = self.dval.get(sem, 0) + 16
            q.ops.append((1, I("dma_start", out=o, in_=i, **kw), sem, 16))
            self.n_inst += 1
        self._mark((sem, self.dval[sem]), reads, writes)
        if group:
            self.groups.setdefault(sem, []).extend(writes)

    def group_end(self, sem):
        for t in self.groups.pop(sem, []):
            if t.w is not None and t.w[0] == sem:
                t.w = (sem, self.dval[sem])

    def barrier(self):
        for q in self.q.values():
            for p in self.q.values():
                if p is not q and p.n > 0:
                    self._wait(q, p.name, p.n)
            for k, v in self.dval.items():
                self._wait(q, k, v)

    def final_wait(self):
        q = self.q["sync"]
        for k, v in self.dval.items():
            self._wait(q, k, v)
        for p in self.q.values():
            if p is not q and p.n > 0:
                self._wait(q, p.name, p.n)

    def sem_keys(self):
        return list(ENGS) + sorted(self.dval.keys())

    def replay(self, block, sems):
        for eng in ENGS:
            q = self.q[eng]

            def body(e, q=q):
                for o in q.ops:
                    if o[0] == 0:
                        e.wait_ge(sems[o[1]], o[2])
                    else:
                        o[1](e).then_inc(sems[o[2]], o[3])

            getattr(block, eng)(body)


class Arena:
    def __init__(self, handle, nbytes):
        self.h = handle
        self.cap = nbytes
        self.top = 0

    def reset(self, to=0):
        self.top = to

    def alloc(self, name, shape, dtype):
        n = 1
        for s in shape[1:]:
            n *= s
        esz = 4 if dtype == F32 else 2
        nb = (n * esz + 31) // 32 * 32
        off = self.top
        self.top += nb
        assert self.top <= self.cap, f"SBUF arena overflow at {name}: {self.top} > {self.cap}"
        ap = self.h[:, off // 4:(off + nb) // 4]
        if dtype != F32:
            ap = ap.bitcast(dtype)
        ap = ap[0:shape[0], 0:n]
        if len(shape) == 3:
            ap = ap.rearrange("p (a b) -> p a b", a=shape[1])
        elif len(shape) == 4:
            ap = ap.rearrange("p (a b c) -> p a b c", a=shape[1], b=shape[2])
        return Trk(name, ap)


class Ctx:
    pass


def I(method, *args, **kw):
    return lambda e: getattr(e, method)(*args, **kw)


def wload(E, dst, src, sem, eng="gpsimd"):
    C = dst.ap.shape[1]
    N = dst.ap.shape[2]
    srcv = src.rearrange("(c p) n -> p c n", p=128)
    pairs = []
    npieces = (N + 2047) // 2048
    step = (N + npieces - 1) // npieces
    for n0 in range(0, N, step):
        n1 = min(N, n0 + step)
        pairs.append((dst.ap[:, :, n0:n1], srcv[:, :, n0:n1]))
    E.dma(eng, pairs, writes=[dst], sem=sem)


def emit_rmsnorm_T(E, C, hbuf, gain, sq, ss_ps, lnv, rstd, xn, TN):
    E.op("scalar", I("activation", out=sq.ap, in_=hbuf.ap, func=AF.Square),
         reads=[hbuf], writes=[sq])
    for c in range(DC):
        E.op("tensor", I("matmul", ss_ps.ap[:, 0:TN], lhsT=C.ones.ap, rhs=sq.ap[:, c, :],
                                               start=(c == 0), stop=(c == DC - 1)),
             reads=[sq, C.ones], writes=[ss_ps])
    E.op("scalar", I("activation", out=lnv.ap, in_=ss_ps.ap[:, 0:TN], func=AF.Ln,
                                          scale=1.0 / D, bias=C.eps.ap[:, 0:1]),
         reads=[ss_ps, C.eps], writes=[lnv])
    E.op("scalar", I("activation", out=rstd.ap, in_=lnv.ap, func=AF.Exp, scale=-0.5),
         reads=[lnv], writes=[rstd])
    for c in range(DC):
        E.op("vector", I("scalar_tensor_tensor", out=xn.ap[:, c, :], in0=hbuf.ap[:, c, :], scalar=gain.ap[:, c:c + 1], in1=rstd.ap,
            op0=ALU.mult, op1=ALU.mult),
            reads=[hbuf, gain, rstd], writes=[xn])


def phase_ffn(E, C, layer, src, TN=256):
    A = C.arena
    A.reset(C.arena_base)
    NT = S // TN
    per = TN // 256
    wg = A.alloc("wg", [128, DC, FH], BF16)
    wu = A.alloc("wu", [128, DC, FH], BF16)
    wd = A.alloc("wd", [128, FC, D], BF16)
    gain = A.alloc("fgain", [128, DC], F32)
    hb = [A.alloc(f"hb{i}", [128, DC, TN], F32) for i in range(2)]
    sq = A.alloc("sq", [128, DC, TN], BF16)
    xn = A.alloc("xn", [128, DC, TN], BF16)
    lnv = A.alloc("lnv", [128, TN], F32)
    rstd = A.alloc("rstd", [128, TN], F32)
    sg = [A.alloc(f"sg{i}", [128, TN], F32) for i in range(2)]
    act = A.alloc("act", [128, FC, TN], BF16)
    ps = [Trk(f"ps{i}", C.psum[i].ap) for i in range(8)]
    ss_ps, g_ps, u_ps, y_ps = ps[0], ps[1:3], ps[3:5], ps[5:7]

    E.dma("sync", [(gain.ap, C.dram["ffn_norm"][layer])], writes=[gain], sem="d_small")
    wload(E, wg, C.dram["ffn_w_gate"][layer], "d_w0")
    wload(E, wu, C.dram["ffn_w_up"][layer], "d_w1")
    wload(E, wd, C.dram["ffn_w_down"][layer], "d_w2")

    srcv = src.rearrange("(c p) s -> p c s", p=128)
    dstv = C.hT.rearrange("(c p) s -> p c s", p=128)

    def htrk(t):
        return C.h_trk[t * per:(t + 1) * per]

    def load(t):
        b = hb[t % 2]
        E.dma("sync", [(b.ap, srcv[:, :, t * TN:(t + 1) * TN])], reads=htrk(t), writes=[b],
              sem=f"d_ld{t % 2}")

    def prologue(t):
        emit_rmsnorm_T(E, C, hb[t % 2], gain, sq, ss_ps, lnv, rstd, xn, TN)

    def gateup(t):
        for f in range(FC):
            gp, up, sgt = g_ps[f % 2], u_ps[f % 2], sg[f % 2]
            for c in range(DC):
                E.op("tensor", I("matmul", gp.ap[:, 0:TN], lhsT=wg.ap[:, c, f * 128:(f + 1) * 128], rhs=xn.ap[:, c, :],
                    start=(c == 0), stop=(c == DC - 1)), reads=[wg, xn], writes=[gp])
            for c in range(DC):
                E.op("tensor", I("matmul", up.ap[:, 0:TN], lhsT=wu.ap[:, c, f * 128:(f + 1) * 128], rhs=xn.ap[:, c, :],
                    start=(c == 0), stop=(c == DC - 1)), reads=[wu, xn], writes=[up])
            E.op("scalar", I("activation", out=sgt.ap, in_=gp.ap[:, 0:TN],
                                                                  func=AF.Silu),
                 reads=[gp], writes=[sgt])
            E.op("vector", I("tensor_tensor", out=act.ap[:, f, :], in0=sgt.ap, in1=up.ap[:, 0:TN], op=ALU.mult),
                reads=[sgt, up], writes=[act])

    def down(t):
        b = hb[t % 2]
        for dm in range(DC):
            yp = y_ps[dm % 2]
            for f in range(FC):
                E.op("tensor", I("matmul", yp.ap[:, 0:TN], lhsT=wd.ap[:, f, dm * 128:(dm + 1) * 128], rhs=act.ap[:, f, :],
                    start=(f == 0), stop=(f == FC - 1)), reads=[wd, act], writes=[yp])
            E.op("vector", I("tensor_tensor", out=b.ap[:, dm, :], in0=b.ap[:, dm, :], in1=yp.ap[:, 0:TN], op=ALU.add),
                reads=[b, yp], writes=[b])
        E.dma("sync", [(dstv[:, :, t * TN:(t + 1) * TN], b.ap)], reads=[b], writes=htrk(t),
              sem=f"d_st{t % 2}")

    load(0)
    prologue(0)
    for t in range(NT):
        if t + 1 < NT:
            load(t + 1)
        gateup(t)
        if t + 1 < NT:
            prologue(t + 1)
        down(t)
    E.barrier()


def phase_attn_proj(E, C, j, src):
    A = C.arena
    A.reset(C.arena_base)
    TN = 512
    NT = S // TN
    w_in = A.alloc("w_in", [128, DC, 1184], BF16)
    w_uq = A.alloc("w_uq", [128, 2, 768], BF16)
    w_ukv = A.alloc("w_ukv", [128, 1, 1024], BF16)
    wkr = A.alloc("wkr", [128, DC, 96], BF16)
    wuk = A.alloc("wuk", [128, 8, 96], BF16)
    gain = A.alloc("again", [128, DC], F32)
    g_qa = A.alloc("g_qa", [64, 1], F32)
    g_ka = A.alloc("g_ka", [64, 1], F32)
    g_cq = A.alloc("g_cq", [128, 2], F32)
    g_ckv = A.alloc("g_ckv", [128, 1], F32)
    g_qb = A.alloc("g_qb", [96, 1], F32)
    g_kb = A.alloc("g_kb", [96, 1], F32)
    tabs = [[A.alloc(f"tab{i}_{k}", [96, TN], F32) for k in range(4)] for i in range(2)]
    hb = [A.alloc(f"hb{i}", [128, DC, TN], F32) for i in range(2)]
    sq = A.alloc("sq", [128, DC, TN], BF16)
    xn = A.alloc("xn", [128, DC, TN], BF16)
    lnv = A.alloc("lnv", [128, TN], F32)
    rstd = A.alloc("rstd", [128, TN], F32)
    cq_raw = A.alloc("cq_raw", [128, 2, TN], F32)
    sq2 = A.alloc("sq2", [128, 2, TN], BF16)
    ln2 = A.alloc("ln2", [128, TN], F32)
    rstd2 = A.alloc("rstd2", [128, TN], F32)
    cqn = A.alloc("cqn", [128, 2, TN], BF16)
    ckvn = A.alloc("ckvn", [128, TN], BF16)
    va = A.alloc("va", [128, 4, 128], BF16)
    vb = A.alloc("vb", [128, 4, 512], BF16)
    ws = []
    for i in range(4):
        w = Ctx()
        w.usq = A.alloc(f"usq{i}", [96, TN], BF16)
        w.uln = A.alloc(f"uln{i}", [96, TN], F32)
        w.urstd = A.alloc(f"urstd{i}", [96, TN], F32)
        w.uxn = A.alloc(f"uxn{i}", [96, TN], BF16)
        w.ut1 = A.alloc(f"ut1{i}", [96, TN], F32)
        w.ut2 = A.alloc(f"ut2{i}", [96, TN], F32)
        w.uout = A.alloc(f"uout{i}", [96, TN], BF16)
        ws.append(w)
    ps = [Trk(f"ps{i}", C.psum[i].ap) for i in range(8)]
    ss_ps = ps[0]
    pj, uss, upx, pv = ps[1:4], ps[4:6], ps[6:8], ps[0]

    sm = "d_small"
    dbg = getattr(C, "dbg", 0)
    if dbg == 12:
        wload(E, w_in, C.dram["attn_w_in"][j], "d_w0")
        wload(E, w_uq, C.dram["mla_w_uq"][j], "d_w1")
        wload(E, w_ukv, C.dram["mla_w_ukv"][j], "d_w2")
        E.barrier()
        return
    if dbg == 13:
        E.op("vector", I("memset", wkr.ap, 0.0), writes=[wkr])
        E.op("vector", I("memset", wuk.ap, 0.0), writes=[wuk])
        E.op("vector", I("tensor_copy", out=wkr.ap[:, :, 64:96], in_=w_in.ap[:, :, 1152:1184]),
             reads=[w_in], writes=[wkr])
        E.barrier()
        return
    E.dma("sync", [(gain.ap, C.dram["attn_norm"][j])], writes=[gain], sem=sm, group=True)
    E.dma("sync", [(g_qa.ap, C.dram["gqa_q_norm"][j])], writes=[g_qa], sem=sm, group=True)
    E.dma("sync", [(g_ka.ap, C.dram["gqa_k_norm"][j])], writes=[g_ka], sem=sm, group=True)
    E.dma("sync", [(g_cq.ap, C.dram["mla_cq_norm"][j])], writes=[g_cq], sem=sm, group=True)
    E.dma("sync", [(g_ckv.ap, C.dram["mla_ckv_norm"][j])], writes=[g_ckv], sem=sm, group=True)
    E.dma("sync", [(g_qb.ap, C.dram["mla_q_norm"][j])], writes=[g_qb], sem=sm, group=True)
    E.dma("sync", [(g_kb.ap, C.dram["mla_k_norm"][j])], writes=[g_kb], sem=sm, group=True)
    E.group_end(sm)
    if dbg == 11:
        E.barrier()
        return
    wload(E, w_in, C.dram["attn_w_in"][j], "d_w0")
    wload(E, w_uq, C.dram["mla_w_uq"][j], "d_w1")
    wload(E, w_ukv, C.dram["mla_w_ukv"][j], "d_w2")
    E.op("vector", I("memset", wkr.ap, 0.0), writes=[wkr])
    E.op("vector", I("memset", wuk.ap, 0.0), writes=[wuk])
    E.op("vector", I("tensor_copy", out=wkr.ap[:, :, 64:96], in_=w_in.ap[:, :, 1152:1184]),
         reads=[w_in], writes=[wkr])
    ukv_h = w_ukv.ap[:, 0, :].rearrange("p (h x) -> p h x", h=8)
    if dbg != 14:
        E.op("vector", I("tensor_copy", out=wuk.ap[:, :, 0:64], in_=ukv_h[:, :, 0:64]),
             reads=[w_ukv], writes=[wuk])
    if dbg == 14 or dbg == 15:
        E.barrier()
        return

    srcv = src.rearrange("(c p) s -> p c s", p=128)
    dbg = getattr(C, "dbg", 0)
    if dbg == 1:
        E.barrier()
        return

    def load(t):
        b = hb[t % 2]
        sl = slice(t * TN, (t + 1) * TN)
        E.dma("sync", [(b.ap, srcv[:, :, sl])], reads=C.h_trk[2 * t:2 * t + 2], writes=[b], sem=f"d_ld{t % 2}")
        tb = tabs[t % 2]
        E.dma("sync", [(tb[0].ap[0:64, :], C.dram["c_CA"][:, sl])], writes=[tb[0]], sem=f"d_tb{t % 2}", group=True)
        E.dma("sync", [(tb[1].ap[0:64, :], C.dram["c_SA"][:, sl])], writes=[tb[1]], sem=f"d_tb{t % 2}", group=True)
        E.dma("sync", [(tb[2].ap, C.dram["c_CB"][:, sl])], writes=[tb[2]], sem=f"d_tb{t % 2}", group=True)
        E.dma("sync", [(tb[3].ap, C.dram["c_SB"][:, sl])], writes=[tb[3]], sem=f"d_tb{t % 2}", group=True)
        E.group_end(f"d_tb{t % 2}")

    ucount = [0]
    active = []

    def tick():
        for g in list(active):
            try:
                next(g)
            except StopIteration:
                active.remove(g)

    def drain():
        while active:
            tick()

    def push(g):
        active.append(g)
        for _ in range(3):
            tick()

    def unit(projfn, Dh, g, Pm, Ct, St, dst_ap, dst_trk):
        k = ucount[0]
        ucount[0] += 1
        w = ws[k % 4]
        pst = pj[k % 3]
        ssp, pxp = uss[k % 2], upx[k % 2]
        projfn(pst)
        yield
        E.op("scalar", I("activation", out=w.usq.ap[0:Dh, :], in_=pst.ap[0:Dh, :], func=AF.Square),
             reads=[pst], writes=[w.usq])
        yield
        E.op("tensor", I("matmul", ssp.ap[0:Dh, :], lhsT=C.ones.ap[0:Dh, 0:Dh], rhs=w.usq.ap[0:Dh, :],
                         start=True, stop=True), reads=[w.usq, C.ones], writes=[ssp])
        yield
        E.op("scalar", I("activation", out=w.uln.ap[0:Dh, :], in_=ssp.ap[0:Dh, :], func=AF.Ln,
                         scale=1.0 / Dh, bias=C.eps.ap[0:Dh, 0:1]),
             reads=[ssp, C.eps], writes=[w.uln])
        yield
        E.op("scalar", I("activation", out=w.urstd.ap[0:Dh, :], in_=w.uln.ap[0:Dh, :], func=AF.Exp,
                         scale=-0.5), reads=[w.uln], writes=[w.urstd])
        yield
        E.op("vector", I("scalar_tensor_tensor", out=w.uxn.ap[0:Dh, :], in0=pst.ap[0:Dh, :], scalar=g.ap[0:Dh, 0:1],
                         in1=w.urstd.ap[0:Dh, :], op0=ALU.mult, op1=ALU.mult),
             reads=[pst, g, w.urstd], writes=[w.uxn])
        yield
        E.op("tensor", I("matmul", pxp.ap[0:Dh, :], lhsT=Pm.ap[0:Dh, 0:Dh], rhs=w.uxn.ap[0:Dh, :],
                         start=True, stop=True), reads=[w.uxn, Pm], writes=[pxp])
        E.op("gpsimd", I("tensor_tensor", out=w.ut1.ap[0:Dh, :], in0=w.uxn.ap[0:Dh, :],
                         in1=Ct.ap[0:Dh, :], op=ALU.mult),
             reads=[w.uxn, Ct], writes=[w.ut1])
        yield
        E.op("vector", I("tensor_tensor", out=w.ut2.ap[0:Dh, :], in0=pxp.ap[0:Dh, :],
                         in1=St.ap[0:Dh, :], op=ALU.mult),
             reads=[pxp, St], writes=[w.ut2])
        yield
        E.op("gpsimd", I("tensor_tensor", out=w.uout.ap[0:Dh, :], in0=w.ut1.ap[0:Dh, :],
                         in1=w.ut2.ap[0:Dh, :], op=ALU.add),
             reads=[w.ut1, w.ut2], writes=[w.uout])
        yield
        E.dma("sync", [(dst_ap, w.uout.ap[0:Dh, :])], reads=[w.uout], writes=[dst_trk], sem=f"d_u{k % 4}")

    def win_proj(M, col0):
        def f(pst):
            for c in range(DC):
                E.op("tensor", I("matmul", pst.ap[0:M, :], lhsT=w_in.ap[:, c, col0:col0 + M], rhs=xn.ap[:, c, :],
                                 start=(c == 0), stop=(c == DC - 1)), reads=[w_in, xn], writes=[pst])
        return f

    pcount = [0]

    def proj(M, col0, cols=None):
        pst = pj[pcount[0] % 2]
        pcount[0] += 1
        for c in range(DC):
            E.op("tensor", I("matmul", pst.ap[0:M, :], lhsT=w_in.ap[:, c, col0:col0 + M],
                                                   rhs=xn.ap[:, c, :], start=(c == 0), stop=(c == DC - 1)),
                 reads=[w_in, xn], writes=[pst])
        return pst

    load(0)
    for t in range(NT):
        if t + 1 < NT:
            load(t + 1)
        sl = slice(t * TN, (t + 1) * TN)
        tb = tabs[t % 2]
        emit_rmsnorm_T(E, C, hb[t % 2], gain, sq, ss_ps, lnv, rstd, xn, TN)
        for h in range(8):
            push(unit(win_proj(64, h * 64), 64, g_qa, C.PA, tb[0], tb[1], C.QT[h, 0:64, sl], C.qt_trk))
        for g in range(2):
            push(unit(win_proj(64, 512 + g * 64), 64, g_ka, C.PA, tb[0], tb[1], C.KT[g, 0:64, sl], C.kt_trk))
        drain()
        if dbg == 5:
            E.barrier()
            return
        for sub in range(4):
            for c in range(DC):
                E.op("tensor", I("matmul", pv.ap[:, 0:128], lhsT=xn.ap[:, c, sub * 128:(sub + 1) * 128], rhs=w_in.ap[:, c, 640:768],
                    start=(c == 0), stop=(c == DC - 1)), reads=[xn, w_in], writes=[pv])
            E.op("vector", I("tensor_copy", out=va.ap[:, sub, :], in_=pv.ap[:, 0:128]),
                 reads=[pv], writes=[va])
        E.dma("sync", [(C.V[sl, 0:128].rearrange("(s p) n -> p s n", p=128), va.ap)],
              reads=[va], writes=[C.v_trk], sem="d_va")
        if dbg == 6:
            E.barrier()
            return
        cq_ps = []
        for jj in range(2):
            pst = proj(128, 768 + jj * 128)
            cq_ps.append(pst)
            E.op("scalar", I("activation", out=sq2.ap[:, jj, :], in_=pst.ap, func=AF.Square),
                 reads=[pst], writes=[sq2])
        if dbg == 71:
            E.barrier()
            return
        ssp = uss[0]
        for jj in range(2):
            E.op("tensor", I("matmul", ssp.ap, lhsT=C.ones.ap, rhs=sq2.ap[:, jj, :],
                                                     start=(jj == 0), stop=(jj == 1)),
                 reads=[sq2, C.ones], writes=[ssp])
        E.op("scalar", I("activation", out=ln2.ap, in_=ssp.ap, func=AF.Ln, scale=1.0 / 256,
                                              bias=C.eps.ap[:, 0:1]), reads=[ssp, C.eps], writes=[ln2])
        E.op("scalar", I("activation", out=rstd2.ap, in_=ln2.ap, func=AF.Exp, scale=-0.5),
             reads=[ln2], writes=[rstd2])
        if dbg == 72:
            E.barrier()
            return
        for jj in range(2):
            E.op("vector", I("scalar_tensor_tensor", out=cqn.ap[:, jj, :], in0=cq_ps[jj].ap, scalar=g_cq.ap[:, jj:jj + 1], in1=rstd2.ap,
                op0=ALU.mult, op1=ALU.mult), reads=[cq_ps[jj], g_cq, rstd2], writes=[cqn])
        if dbg == 7:
            E.barrier()
            return
        pst = proj(128, 1024)
        E.op("scalar", I("activation", out=sq2.ap[:, 0, :], in_=pst.ap, func=AF.Square),
             reads=[pst], writes=[sq2])
        ssp = uss[1]
        E.op("tensor", I("matmul", ssp.ap, lhsT=C.ones.ap, rhs=sq2.ap[:, 0, :], start=True, stop=True),
             reads=[sq2, C.ones], writes=[ssp])
        E.op("scalar", I("activation", out=ln2.ap, in_=ssp.ap, func=AF.Ln, scale=1.0 / 128,
                                              bias=C.eps.ap[:, 0:1]), reads=[ssp, C.eps], writes=[ln2])
        E.op("scalar", I("activation", out=rstd2.ap, in_=ln2.ap, func=AF.Exp, scale=-0.5),
             reads=[ln2], writes=[rstd2])
        E.op("vector", I("scalar_tensor_tensor", out=ckvn.ap, in0=pst.ap, scalar=g_ckv.ap[:, 0:1], in1=rstd2.ap, op0=ALU.mult, op1=ALU.mult),
            reads=[pst, g_ckv, rstd2], writes=[ckvn])
        if dbg == 8:
            E.barrier()
            return
        def uq_proj(h):
            def f(pst):
                for jj in range(2):
                    E.op("tensor", I("matmul", pst.ap[0:96, :], lhsT=w_uq.ap[:, jj, h * 96:(h + 1) * 96], rhs=cqn.ap[:, jj, :],
                                     start=(jj == 0), stop=(jj == 1)), reads=[w_uq, cqn], writes=[pst])
            return f

        def kb_proj(h):
            def f(pst):
                E.op("tensor", I("matmul", pst.ap[0:96, :], lhsT=wuk.ap[:, h, :], rhs=ckvn.ap, start=True, stop=False),
                     reads=[wuk, ckvn], writes=[pst])
                for c in range(DC):
                    E.op("tensor", I("matmul", pst.ap[0:96, :], lhsT=wkr.ap[:, c, :], rhs=xn.ap[:, c, :],
                                     start=False, stop=(c == DC - 1)), reads=[wkr, xn], writes=[pst])
            return f

        for h in range(8):
            push(unit(uq_proj(h), 96, g_qb, C.PB, tb[2], tb[3], C.QT[8 + h, 0:96, sl], C.qt_trk))
        for h in range(8):
            push(unit(kb_proj(h), 96, g_kb, C.PB, tb[2], tb[3], C.KT[2 + h, 0:96, sl], C.kt_trk))
        drain()
        for sub in range(4):
            E.op("tensor", I("matmul", pv.ap, lhsT=ckvn.ap[:, sub * 128:(sub + 1) * 128],
                                                       rhs=ukv_h[:, :, 64:128], start=True, stop=True),
                 reads=[ckvn, w_ukv], writes=[pv])
            E.op("vector", I("tensor_copy", out=vb.ap[:, sub, :], in_=pv.ap),
                 reads=[pv], writes=[vb])
        E.dma("sync", [(C.V[sl, 128:640].rearrange("(s p) n -> p s n", p=128), vb.ap)],
              reads=[vb], writes=[C.v_trk], sem="d_vb")
    E.barrier()


def phase_attn_core(E, C, j, src):
    A = C.arena
    A.reset(C.arena_base)
    oT_all = A.alloc("oT_all", [128, 8, S], BF16)
    w_out = A.alloc("w_out", [128, DC, D], BF16)
    ktb = [A.alloc(f"ktb{i}", [128, S], BF16) for i in range(2)]
    qtb = [A.alloc(f"qtb{i}", [128, S], BF16) for i in range(2)]
    vtb = [A.alloc(f"vtb{i}", [128, 32, 128], BF16) for i in range(2)]
    pt = [A.alloc(f"pt{i}", [128, 3, 512], BF16) for i in range(2)]
    rd = [A.alloc(f"rd{i}", [128, 512], F32) for i in range(2)]
    mark = A.top
    s_ps = [Trk(f"s_ps{i}", C.psum_all[:, i * 1536:(i + 1) * 1536]) for i in range(2)]
    o_ps = [Trk(f"o_ps{i}", C.psum[6 + i].ap) for i in range(2)]
    groups = [(k0, 3) for k0 in range(0, 30, 3)] + [(30, 2)]
    wload(E, w_out, C.dram["attn_w_out"][j], "d_w0")
    for i in range(2):
        E.op("vector", I("memset", vtb[i].ap[:, :, 64:128], 1.0), writes=[vtb[i]])
        E.op("vector", I("memset", ktb[i].ap[64:128, :], 0.0), writes=[ktb[i]])
        E.op("gpsimd", I("memset", qtb[i].ap[64:128, :], 0.0), writes=[qtb[i]])
    kv_loaded = [None, None]
    fin = 0
    for hh in range(16):
        b = hh % 2
        if hh < 8:
            Dh, kv, vc0 = 64, hh // 4, (hh // 4) * 64
        else:
            Dh, kv, vc0 = 96, 2 + (hh - 8), 128 + (hh - 8) * 64
        scale = float(Dh) ** -0.5
        E.dma("sync", [(qtb[b].ap[0:Dh, :], C.QT[hh, 0:Dh, :])], reads=[C.qt_trk], writes=[qtb[b]], sem=f"d_q{b}")
        if kv_loaded[b] != kv:
            E.dma("sync", [(ktb[b].ap[0:Dh, :], C.KT[kv, 0:Dh, :])], reads=[C.kt_trk], writes=[ktb[b]],
                  sem=f"d_k{b}")
            vsrc = C.V[:, vc0:vc0 + 64].rearrange("(k p) n -> p k n", p=128)
            E.dma("sync", [(vtb[b].ap[:, k0:k0 + 8, 0:64], vsrc[:, k0:k0 + 8, :]) for k0 in range(0, 32, 8)],
                  reads=[C.v_trk], writes=[vtb[b]], sem=f"d_v{b}")
            kv_loaded[b] = kv
        kt_, qt_, vt_ = ktb[b], qtb[b], vtb[b]
        pb = (hh % 2) * 64
        for qb in range(8):
            def smm(gi):
                sp = s_ps[gi % 2]
                k0, n = groups[gi]
                for jj in range(n):
                    kt = k0 + jj
                    E.op("tensor", I("matmul", sp.ap[:, jj * 512:(jj + 1) * 512], lhsT=kt_.ap[:, kt * 128:(kt + 1) * 128],
                                     rhs=qt_.ap[:, qb * 512:(qb + 1) * 512], start=True, stop=True),
                         reads=[kt_, qt_], writes=[sp])
            op_ = o_ps[fin % 2]
            rdt = rd[fin % 2]
            fin += 1
            smm(0)
            for gi, (k0, n) in enumerate(groups):
                if gi + 1 < len(groups):
                    smm(gi + 1)
                sp, pp = s_ps[gi % 2], pt[gi % 2]
                E.op("scalar", I("activation", out=pp.ap.rearrange("p a b -> p (a b)")[:, 0:n * 512], in_=sp.ap[:, 0:n * 512],
                                 func=AF.Exp, scale=scale), reads=[sp], writes=[pp])
                for jj in range(n):
                    kt = k0 + jj
                    E.op("tensor", I("matmul", op_.ap, lhsT=vt_.ap[:, kt, :], rhs=pp.ap[:, jj, :],
                                     start=(kt == 0), stop=(kt == 31)), reads=[pp, vt_], writes=[op_])
            E.op("vector", I("reciprocal", out=rdt.ap[64:128, :], in_=op_.ap[64:128, :]), reads=[op_], writes=[rdt])
            E.op("vector", I("tensor_tensor", out=oT_all.ap[pb:pb + 64, hh // 2, qb * 512:(qb + 1) * 512],
                             in0=op_.ap[0:64, :], in1=rdt.ap[64:128, :], op=ALU.mult),
                 reads=[op_, rdt], writes=[oT_all])
    E.barrier()

    A.reset(mark)
    TN = 512
    hb = [A.alloc(f"hb{i}", [128, DC, TN], F32) for i in range(2)]
    ps = [Trk(f"ps{i}", C.psum[i].ap) for i in range(8)]
    srcv = src.rearrange("(c p) s -> p c s", p=128)
    dstv = C.hT.rearrange("(c p) s -> p c s", p=128)

    def load(t):
        E.dma("sync", [(hb[t % 2].ap, srcv[:, :, t * TN:(t + 1) * TN])], reads=C.h_trk[2 * t:2 * t + 2],
              writes=[hb[t % 2]], sem=f"d_ld{t % 2}")

    load(0)
    for t in range(S // TN):
        if t + 1 < S // TN:
            load(t + 1)
        b = hb[t % 2]
        for dm in range(DC):
            yp = ps[dm % 4]
            for c in range(DC):
                E.op("tensor", I("matmul", yp.ap, lhsT=w_out.ap[:, c, dm * 128:(dm + 1) * 128],
                                 rhs=oT_all.ap[:, c, t * TN:(t + 1) * TN], start=(c == 0), stop=(c == DC - 1)),
                     reads=[w_out, oT_all], writes=[yp])
            E.op("vector", I("tensor_tensor", out=b.ap[:, dm, :], in0=b.ap[:, dm, :], in1=yp.ap, op=ALU.add),
                 reads=[b, yp], writes=[b])
        E.dma("sync", [(dstv[:, :, t * TN:(t + 1) * TN], b.ap)], reads=[b], writes=C.h_trk[2 * t:2 * t + 2],
              sem=f"d_st{t % 2}")
    E.barrier()


LN_LO = float(np.log(np.float32(1e-6)))
LN_HI = float(np.log(np.float32(1.0) - np.float32(1e-6)))


def _interleave(*gens):
    act = [g for g in gens if g is not None]
    while act:
        for g in list(act):
            try:
                next(g)
            except StopIteration:
                act.remove(g)


def phase_rec_dir(E, C, j, src, dirn):
    A = C.arena
    A.reset(C.arena_base)
    TN = 512
    NT = S // TN
    bwd = dirn == 1
    win = C.dram["rec_w_in"][j]
    wq = A.alloc("wq", [128, DC, 1024], BF16)
    wz = A.alloc("wz", [128, DC, 1024], BF16)
    wi = A.alloc("wi", [128, DC, 1024], BF16)
    if bwd:
        wg = A.alloc("wg", [128, DC, 1024], BF16)
        w_out = A.alloc("w_out", [128, DC, D], BF16)
        g_o = A.alloc("g_o", [128, 1], F32)
    gain = A.alloc("rgain", [128, DC], F32)
    lbv = A.alloc("lbv", [128, 8], F32)
    l0 = A.alloc("l0", [128, 8], F32)
    l1 = A.alloc("l1", [128, 8], F32)
    one = A.alloc("one", [128, 1], F32)
    mask = A.alloc("mask", [128, 128], F32)
    rst = A.alloc("rst", [128, TN], F32)
    hb = [A.alloc(f"hb{i}", [128, DC, TN], F32) for i in range(2 if not bwd else 1)]
    sq = A.alloc("sq", [128, DC, TN], BF16)
    xn = A.alloc("xn", [128, DC, TN], BF16)
    lnv = A.alloc("lnv", [128, TN], F32)
    rstd = A.alloc("rstd", [128, TN], F32)
    vtok = A.alloc("vtok", [128, 4, 1024], BF16)
    Ws = [[A.alloc(f"W{k}_{i}", [128, TN], F32) for i in range(6)] for k in range(2)]
    ek = A.alloc("ek", [128, TN], F32)
    eqs = [A.alloc(f"eq{i}", [128, TN], F32) for i in range(2)]
    qtls = [A.alloc(f"qtl{i}", [128, TN], BF16) for i in range(2)]
    ktl = A.alloc("ktl", [128, TN], BF16)
    ams = [A.alloc(f"am{i}", [128, 4, 128], BF16) for i in range(2)]
    ktokAs = [A.alloc(f"ktokA{i}", [128, 4, 128], BF16) for i in range(2)]
    ktokBs = [A.alloc(f"ktokB{i}", [128, 4, 128], BF16) for i in range(2)]
    state = A.alloc("state", [128, 8, 128], F32)
    smid = A.alloc("smid", [128, 128], BF16)
    tmp = A.alloc("tmp", [128, 128], F32)
    d1s = [A.alloc(f"d1{i}", [128, 8], F32) for i in range(2)]
    emids = [A.alloc(f"emid{i}", [128, 8], F32) for i in range(2)]
    if bwd:
        ofw = A.alloc("ofw", [128, TN], F32)
        osum = A.alloc("osum", [128, TN], F32)
        sqo = A.alloc("sqo", [128, TN], BF16)
        lno = A.alloc("lno", [128, TN], F32)
        rso = A.alloc("rso", [128, TN], F32)
        sgt = A.alloc("sgt", [128, TN], F32)
        ogT = A.alloc("ogT", [128, 8, TN], BF16)
    else:
        ofs = [A.alloc(f"ofs{i}", [128, TN], F32) for i in range(2)]
    ps = [Trk(f"ps{i}", C.psum[i].ap) for i in range(8)]
    ss_ps, q_ps, z_ps, a_ps, t_ps, o_ps = ps[0], ps[1], ps[2], ps[4], ps[5], ps[6]
    v_ps = ps[0]
    kv_ps = [ps[3], ps[7]]

    sm = "d_small"
    E.dma("sync", [(gain.ap, C.dram["rec_norm"][j])], writes=[gain], sem=sm, group=True)
    E.dma("sync", [(l0.ap, C.dram["rec_lb"][dirn, 0])], writes=[l0], sem=sm, group=True)
    E.dma("sync", [(l1.ap, C.dram["rec_lb"][dirn, 1])], writes=[l1], sem=sm, group=True)
    E.dma("sync", [(mask.ap, C.dram["c_maskb" if bwd else "c_maskf"])], writes=[mask], sem=sm, group=True)
    E.dma("sync", [(rst.ap, C.dram["c_rst"])], writes=[rst], sem=sm, group=True)
    wload(E, wq, win[:, 0:1024], "d_w0")
    wload(E, wz, win[:, (2048 if bwd else 1024):(3072 if bwd else 2048)], "d_w1")
    wload(E, wi, win[:, 3072:4096], "d_w2")
    if bwd:
        wload(E, wg, win[:, 4096:5120], "d_w3")
        wload(E, w_out, C.dram["rec_w_out"][j], "d_w4")
        E.dma("sync", [(g_o.ap, C.dram["rec_out_norm"][j])], writes=[g_o], sem=sm, group=True)
    E.group_end(sm)
    E.op("vector", I("memset", one.ap, 1.0), writes=[one])
    E.op("vector", I("memset", state.ap, 0.0), writes=[state])
    for i in range(2):
        E.op("vector", I("memset", ktokAs[i].ap, 0.0), writes=[ktokAs[i]])
        E.op("gpsimd", I("memset", ktokBs[i].ap, 0.0), writes=[ktokBs[i]])
    if j == 0:
        E.op("vector", I("memset", lbv.ap, 0.0), writes=[lbv])
    else:
        E.op("vector", I("tensor_tensor", out=l0.ap, in0=l0.ap, in1=l1.ap, op=ALU.subtract), reads=[l0, l1], writes=[l0])
        E.op("scalar", I("activation", out=l1.ap, in_=l0.ap, func=AF.Exp), reads=[l0], writes=[l1])
        E.op("scalar", I("activation", out=l0.ap, in_=l1.ap, func=AF.Ln, bias=one.ap[:, 0:1]), reads=[l1, one], writes=[l0])
        E.op("scalar", I("activation", out=lbv.ap, in_=l0.ap, func=AF.Exp, scale=-1.0), reads=[l0], writes=[lbv])

    srcv = src.rearrange("(c p) s -> p c s", p=128)
    dstv = C.hT.rearrange("(c p) s -> p c s", p=128)
    torder = list(range(NT - 1, -1, -1)) if bwd else list(range(NT))
    corder = list(range(7, -1, -1)) if bwd else list(range(8))
    ridx = 32 if bwd else 31
    lidx = 0 if bwd else 63
    mask3 = mask.ap.rearrange("p (a b) -> p a b", a=1).to_broadcast([128, 4, 128])
    porder = list(range(3, -1, -1)) if bwd else list(range(4))

    def v3(t):
        return t.ap.rearrange("p (c l) -> p c l", c=8)

    def load(i):
        t = torder[i]
        b = hb[i % len(hb)]
        E.dma("sync", [(b.ap, srcv[:, :, t * TN:(t + 1) * TN])], reads=C.h_trk[2 * t:2 * t + 2], writes=[b],
              sem=f"d_ld{i % len(hb)}")

    cnt = {"kv": 0, "cp": 0}

    def stage1a(hd):
        par = hd % 2
        hs = slice(hd * 128, (hd + 1) * 128)
        qs, u, la, lb_, Bt, Bm = Ws[par]
        for dc in range(DC):
            E.op("tensor", I("matmul", q_ps.ap, lhsT=wq.ap[:, dc, hs], rhs=xn.ap[:, dc, :],
                             start=(dc == 0), stop=(dc == DC - 1)), reads=[wq, xn], writes=[q_ps])
        for dc in range(DC):
            E.op("tensor", I("matmul", z_ps.ap, lhsT=wz.ap[:, dc, hs], rhs=xn.ap[:, dc, :],
                             start=(dc == 0), stop=(dc == DC - 1)), reads=[wz, xn], writes=[z_ps])
        yield
        E.op("scalar", I("activation", out=u.ap, in_=z_ps.ap, func=AF.Exp, scale=-1.0), reads=[z_ps], writes=[u])
        E.op("scalar", I("activation", out=la.ap, in_=u.ap, func=AF.Ln, bias=one.ap[:, 0:1]),
             reads=[u, one], writes=[la])
        E.op("scalar", I("activation", out=lb_.ap, in_=u.ap, func=AF.Ln, bias=one.ap[:, 0:1],
                         scale=lbv.ap[:, hd:hd + 1]), reads=[u, one, lbv], writes=[lb_])
        yield
        E.op("vector", I("tensor_tensor", out=lb_.ap, in0=lb_.ap, in1=la.ap, op=ALU.subtract),
             reads=[lb_, la], writes=[lb_])
        E.op("vector", I("tensor_scalar", out=lb_.ap, in0=lb_.ap, scalar1=LN_HI, scalar2=LN_LO,
                         op0=ALU.min, op1=ALU.max), reads=[lb_], writes=[lb_])
        E.op("scalar", I("activation", out=u.ap, in_=lb_.ap, func=AF.Exp), reads=[lb_], writes=[u])
        yield
        E.op("vector", I("tensor_tensor_scan", out=la.ap, data0=rst.ap, data1=lb_.ap, initial=0.0,
                         op0=ALU.mult, op1=ALU.add), reads=[rst, lb_], writes=[la])
        E.op("vector", I("tensor_scalar", out=u.ap, in0=u.ap, scalar1=-1.0, scalar2=1.0,
                         op0=ALU.mult, op1=ALU.add), reads=[u], writes=[u])
        yield
        if bwd:
            E.op("vector", I("tensor_tensor", out=Bt.ap, in0=lb_.ap, in1=la.ap, op=ALU.subtract),
                 reads=[lb_, la], writes=[Bt])
            E.op("vector", I("tensor_tensor", out=v3(Bt), in0=v3(Bt),
                             in1=v3(la)[:, :, 63:64].to_broadcast([128, 8, 64]), op=ALU.add),
                 reads=[Bt, la], writes=[Bt])
            Bsrc = Bt
        else:
            Bsrc = la
        E.op("vector", I("tensor_tensor", out=v3(Bm), in0=v3(Bsrc),
                         in1=v3(Bsrc)[:, :, ridx:ridx + 1].to_broadcast([128, 8, 64]), op=ALU.subtract),
             reads=[Bsrc], writes=[Bm])
        yield
        E.op("scalar", I("activation", out=qs.ap, in_=q_ps.ap, func=AF.Silu), reads=[q_ps], writes=[qs])
        yield

    def stage1b(hd):
        par = hd % 2
        eq, qtl, am, d1, emid = eqs[par], qtls[par], ams[par], d1s[par], emids[par]
        ktokA, ktokB = ktokAs[par], ktokBs[par]
        qs, u, la, lb_, Bt, Bm = Ws[par]
        Bsrc = Bt if bwd else la
        E.op("scalar", I("activation", out=eq.ap, in_=Bm.ap, func=AF.Exp), reads=[Bm], writes=[eq])
        E.op("scalar", I("activation", out=ek.ap, in_=Bm.ap, func=AF.Exp, scale=-1.0), reads=[Bm], writes=[ek])
        E.op("scalar", I("activation", out=d1.ap, in_=v3(Bsrc)[:, :, lidx], func=AF.Exp), reads=[Bsrc], writes=[d1])
        E.op("scalar", I("activation", out=emid.ap, in_=v3(Bsrc)[:, :, ridx], func=AF.Exp), reads=[Bsrc], writes=[emid])
        yield
        E.op("gpsimd", I("tensor_tensor", out=ktl.ap, in0=u.ap, in1=ek.ap, op=ALU.mult),
             reads=[u, ek], writes=[ktl])
        E.op("gpsimd", I("tensor_tensor", out=qtl.ap, in0=qs.ap, in1=eq.ap, op=ALU.mult),
             reads=[qs, eq], writes=[qtl])
        yield
        tpb = t_ps.ap.bitcast(BF16)
        for p in range(4):
            E.op("tensor", I("transpose", out=tpb[:, p * 128:(p + 1) * 128], in_=ktl.ap[:, p * 128:(p + 1) * 128],
                             identity=C.ident.ap), reads=[ktl, C.ident], writes=[t_ps])
        E.op("scalar", I("copy", out=ktokA.ap[0:64].rearrange("p a b -> p (a b)"), in_=tpb[0:64, 0:512]),
             reads=[t_ps], writes=[ktokA])
        E.op("scalar", I("copy", out=ktokB.ap[64:128].rearrange("p a b -> p (a b)"), in_=tpb[64:128, 0:512]),
             reads=[t_ps], writes=[ktokB])
        yield
        for p in range(4):
            ps_ = slice(p * 128, (p + 1) * 128)
            E.op("tensor", I("matmul", a_ps.ap[:, ps_], lhsT=ktl.ap[:, ps_], rhs=qtl.ap[:, ps_],
                             start=True, stop=True), reads=[ktl, qtl], writes=[a_ps])
        E.op("vector", I("tensor_tensor", out=am.ap, in0=a_ps.ap.rearrange("p (c l) -> p c l", c=4),
                         in1=mask3, op=ALU.mult), reads=[a_ps, mask], writes=[am])
        yield

    def stage2(hd, t, b):
        par = hd % 2
        hs = slice(hd * 128, (hd + 1) * 128)
        sl = slice(t * TN, (t + 1) * TN)
        eq, qtl, am, d1, emid = eqs[par], qtls[par], ams[par], d1s[par], emids[par]
        ktokA, ktokB = ktokAs[par], ktokBs[par]
        if bwd:
            E.dma("sync", [(ofw.ap, C.OB[hd, :, sl])], reads=[C.ob_trk[t]], writes=[ofw], sem="d_ofw")
        for p in porder:
            E.op("tensor", I("matmul", o_ps.ap[:, p * 128:(p + 1) * 128], lhsT=vtok.ap[:, p, hs], rhs=am.ap[:, p, :],
                             start=True, stop=False), reads=[vtok, am], writes=[o_ps])
            pair = [2 * p + 1, 2 * p] if bwd else [2 * p, 2 * p + 1]
            for ci, c in enumerate(pair):
                cs_ = slice(c * 64, (c + 1) * 64)
                kvp = kv_ps[cnt["kv"] % 2]
                cnt["kv"] += 1
                ktk = ktokA if c % 2 == 0 else ktokB
                E.op("tensor", I("matmul", kvp.ap[:, 0:128], lhsT=ktk.ap[:, p, :], rhs=vtok.ap[:, p, hs], start=True, stop=True),
                     reads=[ktk, vtok], writes=[kvp])
                E.op("vector", I("tensor_scalar", out=smid.ap, in0=state.ap[:, hd, :], scalar1=emid.ap[:, c:c + 1],
                                 scalar2=None, op0=ALU.mult), reads=[state, emid], writes=[smid])
                E.op("tensor", I("matmul", o_ps.ap[:, cs_], lhsT=smid.ap, rhs=qtl.ap[:, cs_], start=False, stop=(ci == 1)),
                     reads=[smid, qtl], writes=[o_ps])
                E.op("vector", I("tensor_scalar", out=tmp.ap, in0=kvp.ap[:, 0:128], scalar1=eq.ap[:, c * 64 + lidx:c * 64 + lidx + 1],
                                 scalar2=None, op0=ALU.mult), reads=[kvp, eq], writes=[tmp])
                E.op("vector", I("scalar_tensor_tensor", out=state.ap[:, hd, :], in0=state.ap[:, hd, :],
                                 scalar=d1.ap[:, c:c + 1], in1=tmp.ap, op0=ALU.mult, op1=ALU.add),
                     reads=[state, d1, tmp], writes=[state])
                yield
        if bwd:
            E.op("vector", I("tensor_tensor", out=osum.ap, in0=ofw.ap, in1=o_ps.ap, op=ALU.add),
                 reads=[ofw, o_ps], writes=[osum])
            yield
            return
        if not bwd:
            of = ofs[hd % 2]
            E.op("scalar", I("activation", out=of.ap, in_=o_ps.ap, func=AF.Identity), reads=[o_ps], writes=[of])
            E.dma("sync", [(C.OB[hd, :, sl], of.ap)], reads=[of], writes=[C.ob_trk[t]], sem=f"d_of{hd % 2}")
        yield

    def stage3(hd):
        hs = slice(hd * 128, (hd + 1) * 128)
        if True:
            E.op("scalar", I("activation", out=sqo.ap, in_=osum.ap, func=AF.Square), reads=[osum], writes=[sqo])
            E.op("tensor", I("matmul", v_ps.ap, lhsT=C.ones.ap, rhs=sqo.ap, start=True, stop=True),
                 reads=[sqo, C.ones], writes=[v_ps])
            yield
            E.op("scalar", I("activation", out=lno.ap, in_=v_ps.ap, func=AF.Ln, scale=1.0 / 128,
                             bias=C.eps.ap[:, 0:1]), reads=[v_ps, C.eps], writes=[lno])
            E.op("scalar", I("activation", out=rso.ap, in_=lno.ap, func=AF.Exp, scale=-0.5), reads=[lno], writes=[rso])
            E.op("vector", I("scalar_tensor_tensor", out=osum.ap, in0=osum.ap, scalar=g_o.ap[:, 0:1], in1=rso.ap,
                             op0=ALU.mult, op1=ALU.mult), reads=[osum, g_o, rso], writes=[osum])
            for dc in range(DC):
                E.op("tensor", I("matmul", v_ps.ap, lhsT=wg.ap[:, dc, hs], rhs=xn.ap[:, dc, :],
                                 start=(dc == 0), stop=(dc == DC - 1)), reads=[wg, xn], writes=[v_ps])
            yield
            E.op("scalar", I("activation", out=sgt.ap, in_=v_ps.ap, func=AF.Silu), reads=[v_ps], writes=[sgt])
            E.op("vector", I("tensor_tensor", out=ogT.ap[:, hd, :], in0=osum.ap, in1=sgt.ap, op=ALU.mult),
                 reads=[osum, sgt], writes=[ogT])
        yield

    load(0)
    for i, t in enumerate(torder):
        sl = slice(t * TN, (t + 1) * TN)
        if i + 1 < NT and len(hb) == 2:
            load(i + 1)
        b = hb[i % len(hb)]
        emit_rmsnorm_T(E, C, b, gain, sq, ss_ps, lnv, rstd, xn, TN)
        for p in range(4):
            for hg in range(2):
                for dc in range(DC):
                    E.op("tensor", I("matmul", v_ps.ap, lhsT=xn.ap[:, dc, p * 128:(p + 1) * 128],
                                     rhs=wi.ap[:, dc, hg * 512:(hg + 1) * 512], start=(dc == 0), stop=(dc == DC - 1)),
                         reads=[xn, wi], writes=[v_ps])
                if cnt["cp"] % 2 == 0:
                    E.op("vector", I("tensor_copy", out=vtok.ap[:, p, hg * 512:(hg + 1) * 512], in_=v_ps.ap),
                         reads=[v_ps], writes=[vtok])
                else:
                    E.op("scalar", I("copy", out=vtok.ap[:, p, hg * 512:(hg + 1) * 512], in_=v_ps.ap),
                         reads=[v_ps], writes=[vtok])
                cnt["cp"] += 1
        _interleave(stage1a(0))
        _interleave(stage1a(1), stage1b(0))
        for hd in range(8):
            _interleave(stage1a(hd + 2) if hd + 2 < 8 else None, stage1b(hd + 1) if hd + 1 < 8 else None,
                        stage2(hd, t, b), stage3(hd - 1) if (bwd and hd >= 1) else None)
        if bwd:
            _interleave(stage3(7))
        if bwd:
            for dm in range(DC):
                yp = ps[1 + dm % 2]
                for hd in range(8):
                    E.op("tensor", I("matmul", yp.ap, lhsT=w_out.ap[:, hd, dm * 128:(dm + 1) * 128], rhs=ogT.ap[:, hd, :],
                                     start=(hd == 0), stop=(hd == 7)), reads=[w_out, ogT], writes=[yp])
                E.op("vector", I("tensor_tensor", out=b.ap[:, dm, :], in0=b.ap[:, dm, :], in1=yp.ap, op=ALU.add),
                     reads=[b, yp], writes=[b])
            E.dma("sync", [(dstv[:, :, sl], b.ap)], reads=[b], writes=C.h_trk[2 * t:2 * t + 2], sem="d_st0")
            if i + 1 < NT:
                load(i + 1)
        elif len(hb) == 1 and i + 1 < NT:
            load(i + 1)
    E.barrier()


SMALL_SPECS = {
    "ffn_norm": [DEPTH, 128, DC],
    "attn_norm": [2, 128, DC],
    "gqa_q_norm": [2, 64, 1],
    "gqa_k_norm": [2, 64, 1],
    "mla_cq_norm": [2, 128, 2],
    "mla_ckv_norm": [2, 128, 1],
    "mla_q_norm": [2, 96, 1],
    "mla_k_norm": [2, 96, 1],
    "rec_norm": [2, 128, DC],
    "rec_lb": [2, 2, 128, DC],
    "rec_out_norm": [2, 128, 1],
    "c_ident": [128, 128],
    "c_PA": [64, 64],
    "c_PB": [96, 96],
    "c_CA": [64, S],
    "c_SA": [64, S],
    "c_CB": [96, S],
    "c_SB": [96, S],
    "c_maskf": [128, 128],
    "c_maskb": [128, 128],
    "c_rst": [128, 512],
}
WEIGHT_SPECS = {
    "ffn_w_gate": [DEPTH, D, FH],
    "ffn_w_up": [DEPTH, D, FH],
    "ffn_w_down": [DEPTH, FH, D],
    "attn_w_in": [2, D, 1184],
    "mla_w_uq": [2, 256, 768],
    "mla_w_ukv": [2, 128, 1024],
    "attn_w_out": [2, D, D],
    "rec_w_in": [2, D, 5120],
    "rec_w_out": [2, D, D],
}


def build_program(plan, debug=False, dbg=0):
    nc = bass.Bass("TRN2", target_bir_lowering=False)
    sck = "ExternalOutput" if debug else "Internal"
    E = Em()
    C = Ctx()
    C.dbg = dbg
    C.dram = {}
    xT = nc.dram_tensor("xT", [D, S], F32, kind="ExternalInput").ap()
    for name, shp in list(SMALL_SPECS.items()) + list(WEIGHT_SPECS.items()):
        C.dram[name] = nc.dram_tensor(name, shp, F32, kind="ExternalInput").ap()
    C.hT = nc.dram_tensor("yT", [D, S], F32, kind="ExternalOutput").ap()
    C.h_trk = [Trk(f"h{t}") for t in range(S // 256)]

    with ExitStack() as es:
        arena_h = es.enter_context(nc.sbuf_tensor("arena", [128, ARENA_BYTES // 4], F32))
        C.arena = Arena(arena_h, ARENA_BYTES)
        C.psum_all = es.enter_context(nc.psum_tensor("psall", [128, 4096], F32))[:, :]
        C.psum = [Trk(f"psb{i}", C.psum_all[:, i * 512:(i + 1) * 512]) for i in range(8)]
        C.ones = C.arena.alloc("ones", [128, 128], BF16)
        C.eps = C.arena.alloc("eps", [128, 1], F32)
        C.ident = C.arena.alloc("ident", [128, 128], BF16)
        C.PA = C.arena.alloc("PA", [64, 64], BF16)
        C.PB = C.arena.alloc("PB", [96, 96], BF16)
        C.arena_base = C.arena.top
        E.op("vector", I("memset", C.ones.ap, 1.0), writes=[C.ones])
        E.op("vector", I("memset", C.eps.ap, EPS), writes=[C.eps])
        E.dma("gpsimd", [(C.ident.ap, C.dram["c_ident"])], writes=[C.ident], sem="d_c0")
        E.dma("gpsimd", [(C.PA.ap, C.dram["c_PA"])], writes=[C.PA], sem="d_c1")
        E.dma("gpsimd", [(C.PB.ap, C.dram["c_PB"])], writes=[C.PB], sem="d_c2")
        C.QT = nc.dram_tensor("sc_QT", [16, 96, S], BF16, kind=sck).ap()
        C.KT = nc.dram_tensor("sc_KT", [10, 96, S], BF16, kind=sck).ap()
        C.V = nc.dram_tensor("sc_V", [S, 640], BF16, kind=sck).ap()
        C.OB = nc.dram_tensor("sc_OB", [8, 128, S], F32, kind=sck).ap()
        C.qt_trk = Trk("QT")
        C.kt_trk = Trk("KT")
        C.v_trk = Trk("V")
        C.ob_trk = [Trk(f"OB{t}") for t in range(8)]

        src = xT
        for (kind, layer) in plan:
            if kind == "ffn":
                phase_ffn(E, C, layer, src)
            elif kind == "attn":
                phase_attn_proj(E, C, layer, src)
                phase_attn_core(E, C, layer, src)
            elif kind == "attn_proj":
                phase_attn_proj(E, C, layer, src)
                continue
            elif kind == "attn_core":
                phase_attn_core(E, C, layer, src)
            elif kind == "rec_f":
                phase_rec_dir(E, C, layer, src, 0)
                continue
            elif kind == "rec_b":
                phase_rec_dir(E, C, layer, src, 1)
            elif kind == "rec":
                phase_rec_dir(E, C, layer, src, 0)
                phase_rec_dir(E, C, layer, src, 1)
            src = C.hT
        E.final_wait()

        sems = {}
        for k in E.sem_keys():
            sems[k] = es.enter_context(nc.semaphore(k))
        with nc.allow_low_precision("bf16 matmul operands, fp32 accumulation"):
            block = es.enter_context(nc.Block())
            E.replay(block, sems)
    return nc, E


def _rope_tables(rot_dim, lead):
    n_rows = S // 64
    row = np.repeat(np.arange(n_rows, dtype=np.float32), 64)
    col = np.tile(np.arange(64, dtype=np.float32), n_rows)
    sec = rot_dim // 2
    inv = (np.float32(10000.0) ** (-np.arange(0, sec, 2, dtype=np.float32) / np.float32(sec))).astype(np.float32)
    ang = np.concatenate([row[:, None] * inv, col[:, None] * inv], axis=-1).astype(np.float32)
    nf = rot_dim // 4
    Dh = lead + rot_dim
    Ct = np.ones((Dh, S), np.float32)
    St = np.zeros((Dh, S), np.float32)
    P = np.zeros((Dh, Dh), np.float32)
    for a in range(2):
        for j in range(nf):
            c = np.cos(ang[:, a * nf + j]).astype(np.float32)
            sn = np.sin(ang[:, a * nf + j]).astype(np.float32)
            i1 = lead + a * 2 * nf + j
            i2 = i1 + nf
            Ct[i1] = c
            Ct[i2] = c
            St[i1] = sn
            St[i2] = sn
            P[i2, i1] = -1.0
            P[i1, i2] = 1.0
    return Ct, St, P


def host_consts():
    out = {}
    out["c_ident"] = np.eye(128, dtype=np.float32)
    out["c_CA"], out["c_SA"], out["c_PA"] = _rope_tables(64, 0)
    out["c_CB"], out["c_SB"], out["c_PB"] = _rope_tables(32, 64)
    s_i = np.arange(128)[:, None]
    t_i = np.arange(128)[None, :]
    same = (s_i // 64) == (t_i // 64)
    out["c_maskf"] = ((s_i <= t_i) & same).astype(np.float32)
    out["c_maskb"] = ((s_i >= t_i) & same).astype(np.float32)
    rst = np.ones((128, 512), np.float32)
    rst[:, ::64] = 0.0
    out["c_rst"] = rst
    return out


def host_layout(inputs):
    out = dict(host_consts())
    f = lambda k: np.asarray(inputs[k], np.float32)
    out["ffn_norm"] = np.ascontiguousarray(f("ffn_norm").reshape(DEPTH, DC, 128).transpose(0, 2, 1))
    out["attn_norm"] = np.ascontiguousarray(f("attn_norm").reshape(2, DC, 128).transpose(0, 2, 1))
    out["rec_norm"] = np.ascontiguousarray(f("rec_norm").reshape(2, DC, 128).transpose(0, 2, 1))
    out["gqa_q_norm"] = np.ascontiguousarray(f("gqa_q_norm").reshape(2, 64, 1))
    out["gqa_k_norm"] = np.ascontiguousarray(f("gqa_k_norm").reshape(2, 64, 1))
    out["mla_cq_norm"] = np.ascontiguousarray(f("mla_cq_norm").reshape(2, 2, 128).transpose(0, 2, 1))
    out["mla_ckv_norm"] = np.ascontiguousarray(f("mla_ckv_norm").reshape(2, 128, 1))
    out["mla_q_norm"] = np.ascontiguousarray(f("mla_q_norm").reshape(2, 96, 1))
    out["mla_k_norm"] = np.ascontiguousarray(f("mla_k_norm").reshape(2, 96, 1))
    out["rec_lb"] = np.ascontiguousarray(f("rec_lower_bounds").reshape(2, 2, DC, 128).transpose(0, 1, 3, 2))
    out["rec_out_norm"] = np.ascontiguousarray(f("rec_out_norm").reshape(2, 128, 1))
    for k in WEIGHT_SPECS:
        out[k] = np.ascontiguousarray(np.asarray(inputs[k], np.float32))
    return out


FULL_PLAN = [("attn", 0), ("ffn", 0), ("rec", 0), ("ffn", 1), ("attn", 1), ("ffn", 2), ("rec", 1), ("ffn", 3)]


def kernel(**inputs):
    x = np.asarray(inputs["x"], np.float32)
    shared = host_layout(inputs)
    nc, _ = build_program(FULL_PLAN)
    in_maps = []
    for b in range(N_CORES):
        m = dict(shared)
        m["xT"] = np.ascontiguousarray(x[b].T)
        in_maps.append(m)
    res = run_bass_kernel_spmd(nc, in_maps, core_ids=list(range(N_CORES)))
    out = np.stack([np.asarray(res.results[b]["yT"]).T for b in range(N_CORES)])
    return np.ascontiguousarray(out.astype(np.float32))
```

```python
import numpy as np
from contextlib import ExitStack

import concourse.bass as bass
import concourse.mybir as mybir
from concourse.alu_op_type import AluOpType as ALU
from concourse.bass_utils import run_bass_kernel_spmd

F32 = mybir.dt.float32
BF16 = mybir.dt.bfloat16
AF = mybir.ActivationFunctionType
AX = mybir.AxisListType

S = 4096
D = 1024
DC = 8
FH = 2816
FC = 22
EPS = 1e-6
N_CORES = 8
DEPTH = 4
ARENA_BYTES = 204 * 1024


class Trk:
    __slots__ = ("name", "w", "r", "ap")

    def __init__(self, name="", ap=None):
        self.name = name
        self.w = None
        self.r = {}
        self.ap = ap

    def __getitem__(self, k):
        return self.ap[k]


class EngQ:
    def __init__(self, name):
        self.name = name
        self.n = 0
        self.waited = {}
        self.ops = []


ENGS = ("tensor", "vector", "scalar", "gpsimd", "sync")


class Em:
    def __init__(self):
        self.q = {e: EngQ(e) for e in ENGS}
        self.dval = {}
        self.groups = {}
        self.n_inst = 0

    def _wait(self, q, key, val):
        if q.waited.get(key, 0) < val:
            q.ops.append((0, key, val))
            q.waited[key] = val

    def _sync(self, q, reads, writes):
        own = q.name
        for t in reads:
            if t.w is not None:
                k, v = t.w
                if k == own and own == "tensor":
                    continue
                self._wait(q, k, v)
        for t in writes:
            if t.w is not None:
                k, v = t.w
                if not (k == own and own == "tensor"):
                    self._wait(q, k, v)
            for k, v in t.r.items():
                if k == own:
                    continue
                self._wait(q, k, v)

    @staticmethod
    def _mark(ev, reads, writes):
        k, v = ev
        for t in reads:
            if t.r.get(k, 0) < v:
                t.r[k] = v
        for t in writes:
            t.w = ev
            t.r = {}

    def op(self, eng, fn, reads=(), writes=()):
        q = self.q[eng]
        self._sync(q, reads, writes)
        q.n += 1
        q.ops.append((1, fn, eng, 1))
        self.n_inst += 1
        self._mark((eng, q.n), reads, writes)

    def dma(self, eng, pairs, reads=(), writes=(), sem=None, group=False, **kw):
        q = self.q[eng]
        self._sync(q, reads, writes)
        for (o, i) in pairs:
            self.dval[sem] = self.dval.get(sem, 0) + 16
            q.ops.append((1, I("dma_start", out=o, in_=i, **kw), sem, 16))
            self.n_inst += 1
        self._mark((sem, self.dval[sem]), reads, writes)
        if group:
            self.groups.setdefault(sem, []).extend(writes)

    def group_end(self, sem):
        for t in self.groups.pop(sem, []):
            if t.w is not None and t.w[0] == sem:
                t.w = (sem, self.dval[sem])

    def barrier(self):
        for q in self.q.values():
            for p in self.q.values():
                if p is not q and p.n > 0:
                    self._wait(q, p.name, p.n)
            for k, v in self.dval.items():
                self._wait(q, k, v)

    def final_wait(self):
        q = self.q["sync"]
        for k, v in self.dval.items():
            self._wait(q, k, v)
        for p in self.q.values():
            if p is not q and p.n > 0:
                self._wait(q, p.name, p.n)

    def sem_keys(self):
        return list(ENGS) + sorted(self.dval.keys())

    def replay(self, block, sems):
        for eng in ENGS:
            q = self.q[eng]

            def body(e, q=q):
                for o in q.ops:
                    if o[0] == 0:
                        e.wait_ge(sems[o[1]], o[2])
                    else:
                        o[1](e).then_inc(sems[o[2]], o[3])

            getattr(block, eng)(body)


class Arena:
    def __init__(self, handle, nbytes):
        self.h = handle
        self.cap = nbytes
        self.top = 0

    def reset(self, to=0):
        self.top = to

    def alloc(self, name, shape, dtype):
        n = 1
        for s in shape[1:]:
            n *= s
        esz = 4 if dtype == F32 else 2
        nb = (n * esz + 31) // 32 * 32
        off = self.top
        self.top += nb
        assert self.top <= self.cap, f"SBUF arena overflow at {name}: {self.top} > {self.cap}"
        ap = self.h[:, off // 4:(off + nb) // 4]
        if dtype != F32:
            ap = ap.bitcast(dtype)
        ap = ap[0:shape[0], 0:n]
        if len(shape) == 3:
            ap = ap.rearrange("p (a b) -> p a b", a=shape[1])
        elif len(shape) == 4:
            ap = ap.rearrange("p (a b c) -> p a b c", a=shape[1], b=shape[2])
        return Trk(name, ap)


class Ctx:
    pass


def I(method, *args, **kw):
    return lambda e: getattr(e, method)(*args, **kw)


def wload(E, dst, src, sem, eng="gpsimd"):
    C = dst.ap.shape[1]
    N = dst.ap.shape[2]
    srcv = src.rearrange("(c p) n -> p c n", p=128)
    pairs = []
    npieces = (N + 2047) // 2048
    step = (N + npieces - 1) // npieces
    for n0 in range(0, N, step):
        n1 = min(N, n0 + step)
        pairs.append((dst.ap[:, :, n0:n1], srcv[:, :, n0:n1]))
    E.dma(eng, pairs, writes=[dst], sem=sem)


def emit_rmsnorm_T(E, C, hbuf, gain, sq, ss_ps, lnv, rstd, xn, TN):
    E.op("scalar", I("activation", out=sq.ap, in_=hbuf.ap, func=AF.Square),
         reads=[hbuf], writes=[sq])
    for c in range(DC):
        E.op("tensor", I("matmul", ss_ps.ap[:, 0:TN], lhsT=C.ones.ap, rhs=sq.ap[:, c, :],
                                               start=(c == 0), stop=(c == DC - 1)),
             reads=[sq, C.ones], writes=[ss_ps])
    E.op("scalar", I("activation", out=lnv.ap, in_=ss_ps.ap[:, 0:TN], func=AF.Ln,
                                          scale=1.0 / D, bias=C.eps.ap[:, 0:1]),
         reads=[ss_ps, C.eps], writes=[lnv])
    E.op("scalar", I("activation", out=rstd.ap, in_=lnv.ap, func=AF.Exp, scale=-0.5),
         reads=[lnv], writes=[rstd])
    for c in range(DC):
        E.op("vector", I("scalar_tensor_tensor", out=xn.ap[:, c, :], in0=hbuf.ap[:, c, :], scalar=gain.ap[:, c:c + 1], in1=rstd.ap,
            op0=ALU.mult, op1=ALU.mult),
            reads=[hbuf, gain, rstd], writes=[xn])


def phase_ffn(E, C, layer, src, TN=256):
    A = C.arena
    A.reset(C.arena_base)
    NT = S // TN
    per = TN // 256
    HH = FH // 2
    wg = [A.alloc(f"wg{i}", [128, DC, HH], BF16) for i in range(2)]
    wu = [A.alloc(f"wu{i}", [128, DC, HH], BF16) for i in range(2)]
    wd = A.alloc("wd", [128, FC, D], BF16)
    gain = A.alloc("fgain", [128, DC], F32)
    hb = [A.alloc(f"hb{i}", [128, DC, TN], F32) for i in range(2)]
    sq = A.alloc("sq", [128, DC, TN], BF16)
    xn = A.alloc("xn", [128, DC, TN], BF16)
    lnv = A.alloc("lnv", [128, TN], F32)
    rstd = A.alloc("rstd", [128, TN], F32)
    sg = [A.alloc(f"sg{i}", [128, TN], F32) for i in range(2)]
    act = A.alloc("act", [128, FC, TN], BF16)
    ps = [Trk(f"ps{i}", C.psum[i].ap) for i in range(8)]
    ss_ps, g_ps, u_ps, y_ps = ps[0], ps[1:3], ps[3:5], ps[5:7]

    E.dma("sync", [(gain.ap, C.dram["ffn_norm"][layer])], writes=[gain], sem="d_small")
    wload(E, wg[0], C.dram["ffn_w_gate"][layer][:, 0:HH], "d_w0")
    wload(E, wu[0], C.dram["ffn_w_up"][layer][:, 0:HH], "d_w1")
    wload(E, wg[1], C.dram["ffn_w_gate"][layer][:, HH:FH], "d_w3")
    wload(E, wu[1], C.dram["ffn_w_up"][layer][:, HH:FH], "d_w4")
    wload(E, wd, C.dram["ffn_w_down"][layer], "d_w2")

    srcv = src.rearrange("(c p) s -> p c s", p=128)
    dstv = C.hT.rearrange("(c p) s -> p c s", p=128)

    def htrk(t):
        return C.h_trk[t * per:(t + 1) * per]

    def load(t):
        b = hb[t % 2]
        E.dma("sync", [(b.ap, srcv[:, :, t * TN:(t + 1) * TN])], reads=htrk(t), writes=[b],
              sem=f"d_ld{t % 2}")

    def prologue(t):
        emit_rmsnorm_T(E, C, hb[t % 2], gain, sq, ss_ps, lnv, rstd, xn, TN)

    def gateup(t):
        for f in range(FC):
            gp, up, sgt = g_ps[f % 2], u_ps[f % 2], sg[f % 2]
            wgt, wut, fo = wg[f // 11], wu[f // 11], (f % 11) * 128
            for c in range(DC):
                E.op("tensor", I("matmul", gp.ap[:, 0:TN], lhsT=wgt.ap[:, c, fo:fo + 128], rhs=xn.ap[:, c, :],
                    start=(c == 0), stop=(c == DC - 1)), reads=[wgt, xn], writes=[gp])
            for c in range(DC):
                E.op("tensor", I("matmul", up.ap[:, 0:TN], lhsT=wut.ap[:, c, fo:fo + 128], rhs=xn.ap[:, c, :],
                    start=(c == 0), stop=(c == DC - 1)), reads=[wut, xn], writes=[up])
            E.op("scalar", I("activation", out=sgt.ap, in_=gp.ap[:, 0:TN],
                                                                  func=AF.Silu),
                 reads=[gp], writes=[sgt])
            E.op("vector", I("tensor_tensor", out=act.ap[:, f, :], in0=sgt.ap, in1=up.ap[:, 0:TN], op=ALU.mult),
                reads=[sgt, up], writes=[act])

    def down(t):
        b = hb[t % 2]
        for dm in range(DC):
            yp = y_ps[dm % 2]
            for f in range(FC):
                E.op("tensor", I("matmul", yp.ap[:, 0:TN], lhsT=wd.ap[:, f, dm * 128:(dm + 1) * 128], rhs=act.ap[:, f, :],
                    start=(f == 0), stop=(f == FC - 1)), reads=[wd, act], writes=[yp])
            E.op("vector", I("tensor_tensor", out=b.ap[:, dm, :], in0=b.ap[:, dm, :], in1=yp.ap[:, 0:TN], op=ALU.add),
                reads=[b, yp], writes=[b])
        E.dma("sync", [(dstv[:, :, t * TN:(t + 1) * TN], b.ap)], reads=[b], writes=htrk(t),
              sem=f"d_st{t % 2}")

    load(0)
    prologue(0)
    for t in range(NT):
        if t + 1 < NT:
            load(t + 1)
        gateup(t)
        if t + 1 < NT:
            prologue(t + 1)
        down(t)
    E.barrier()


def phase_attn_proj(E, C, j, src):
    A = C.arena
    A.reset(C.arena_base)
    TN = 512
    NT = S // TN
    w_in = A.alloc("w_in", [128, DC, 1184], BF16)
    w_uq = A.alloc("w_uq", [128, 2, 768], BF16)
    w_ukv = A.alloc("w_ukv", [128, 1, 1024], BF16)
    wkr = A.alloc("wkr", [128, DC, 96], BF16)
    wuk = A.alloc("wuk", [128, 8, 96], BF16)
    gain = A.alloc("again", [128, DC], F32)
    g_qa = A.alloc("g_qa", [64, 1], F32)
    g_ka = A.alloc("g_ka", [64, 1], F32)
    g_cq = A.alloc("g_cq", [128, 2], F32)
    g_ckv = A.alloc("g_ckv", [128, 1], F32)
    g_qb = A.alloc("g_qb", [96, 1], F32)
    g_kb = A.alloc("g_kb", [96, 1], F32)
    tabs = [[A.alloc(f"tab{i}_{k}", [96, TN], F32) for k in range(4)] for i in range(2)]
    hb = [A.alloc(f"hb{i}", [128, DC, TN], F32) for i in range(2)]
    sq = A.alloc("sq", [128, DC, TN], BF16)
    xn = A.alloc("xn", [128, DC, TN], BF16)
    lnv = A.alloc("lnv", [128, TN], F32)
    rstd = A.alloc("rstd", [128, TN], F32)
    cq_raw = A.alloc("cq_raw", [128, 2, TN], F32)
    sq2 = A.alloc("sq2", [128, 2, TN], BF16)
    ln2 = A.alloc("ln2", [128, TN], F32)
    rstd2 = A.alloc("rstd2", [128, TN], F32)
    cqn = A.alloc("cqn", [128, 2, TN], BF16)
    ckvn = A.alloc("ckvn", [128, TN], BF16)
    va = A.alloc("va", [128, 4, 128], BF16)
    vb = A.alloc("vb", [128, 4, 512], BF16)
    ws = []
    for i in range(4):
        w = Ctx()
        w.usq = A.alloc(f"usq{i}", [96, TN], BF16)
        w.uln = A.alloc(f"uln{i}", [96, TN], F32)
        w.urstd = A.alloc(f"urstd{i}", [96, TN], F32)
        w.uxn = A.alloc(f"uxn{i}", [96, TN], BF16)
        w.ut1 = A.alloc(f"ut1{i}", [96, TN], F32)
        w.ut2 = A.alloc(f"ut2{i}", [96, TN], F32)
        w.uout = A.alloc(f"uout{i}", [96, TN], BF16)
        ws.append(w)
    ps = [Trk(f"ps{i}", C.psum[i].ap) for i in range(8)]
    ss_ps = ps[0]
    pj, uss, upx, pv = ps[1:4], ps[4:6], ps[6:8], ps[0]

    sm = "d_small"
    dbg = getattr(C, "dbg", 0)
    if dbg == 12:
        wload(E, w_in, C.dram["attn_w_in"][j], "d_w0")
        wload(E, w_uq, C.dram["mla_w_uq"][j], "d_w1")
        wload(E, w_ukv, C.dram["mla_w_ukv"][j], "d_w2")
        E.barrier()
        return
    if dbg == 13:
        E.op("vector", I("memset", wkr.ap, 0.0), writes=[wkr])
        E.op("vector", I("memset", wuk.ap, 0.0), writes=[wuk])
        E.op("vector", I("tensor_copy", out=wkr.ap[:, :, 64:96], in_=w_in.ap[:, :, 1152:1184]),
             reads=[w_in], writes=[wkr])
        E.barrier()
        return
    E.dma("sync", [(gain.ap, C.dram["attn_norm"][j])], writes=[gain], sem=sm, group=True)
    E.dma("sync", [(g_qa.ap, C.dram["gqa_q_norm"][j])], writes=[g_qa], sem=sm, group=True)
    E.dma("sync", [(g_ka.ap, C.dram["gqa_k_norm"][j])], writes=[g_ka], sem=sm, group=True)
    E.dma("sync", [(g_cq.ap, C.dram["mla_cq_norm"][j])], writes=[g_cq], sem=sm, group=True)
    E.dma("sync", [(g_ckv.ap, C.dram["mla_ckv_norm"][j])], writes=[g_ckv], sem=sm, group=True)
    E.dma("sync", [(g_qb.ap, C.dram["mla_q_norm"][j])], writes=[g_qb], sem=sm, group=True)
    E.dma("sync", [(g_kb.ap, C.dram["mla_k_norm"][j])], writes=[g_kb], sem=sm, group=True)
    E.group_end(sm)
    if dbg == 11:
        E.barrier()
        return
    wload(E, w_in, C.dram["attn_w_in"][j], "d_w0")
    wload(E, w_uq, C.dram["mla_w_uq"][j], "d_w1")
    wload(E, w_ukv, C.dram["mla_w_ukv"][j], "d_w2")
    E.op("vector", I("memset", wkr.ap, 0.0), writes=[wkr])
    E.op("vector", I("memset", wuk.ap, 0.0), writes=[wuk])
    E.op("vector", I("tensor_copy", out=wkr.ap[:, :, 64:96], in_=w_in.ap[:, :, 1152:1184]),
         reads=[w_in], writes=[wkr])
    ukv_h = w_ukv.ap[:, 0, :].rearrange("p (h x) -> p h x", h=8)
    if dbg != 14:
        E.op("vector", I("tensor_copy", out=wuk.ap[:, :, 0:64], in_=ukv_h[:, :, 0:64]),
             reads=[w_ukv], writes=[wuk])
    if dbg == 14 or dbg == 15:
        E.barrier()
        return

    srcv = src.rearrange("(c p) s -> p c s", p=128)
    dbg = getattr(C, "dbg", 0)
    if dbg == 1:
        E.barrier()
        return

    def load(t):
        b = hb[t % 2]
        sl = slice(t * TN, (t + 1) * TN)
        E.dma("sync", [(b.ap, srcv[:, :, sl])], reads=C.h_trk[2 * t:2 * t + 2], writes=[b], sem=f"d_ld{t % 2}")
        tb = tabs[t % 2]
        E.dma("sync", [(tb[0].ap[0:64, :], C.dram["c_CA"][:, sl])], writes=[tb[0]], sem=f"d_tb{t % 2}", group=True)
        E.dma("sync", [(tb[1].ap[0:64, :], C.dram["c_SA"][:, sl])], writes=[tb[1]], sem=f"d_tb{t % 2}", group=True)
        E.dma("sync", [(tb[2].ap, C.dram["c_CB"][:, sl])], writes=[tb[2]], sem=f"d_tb{t % 2}", group=True)
        E.dma("sync", [(tb[3].ap, C.dram["c_SB"][:, sl])], writes=[tb[3]], sem=f"d_tb{t % 2}", group=True)
        E.group_end(f"d_tb{t % 2}")

    ucount = [0]
    active = []

    def tick():
        for g in list(active):
            try:
                next(g)
            except StopIteration:
                active.remove(g)

    def drain():
        while active:
            tick()

    def push(g):
        active.append(g)
        for _ in range(3):
            tick()

    def unit(projfn, Dh, g, Pm, Ct, St, dst_ap, dst_trk):
        k = ucount[0]
        ucount[0] += 1
        w = ws[k % 4]
        pst = pj[k % 3]
        ssp, pxp = uss[k % 2], upx[k % 2]
        projfn(pst)
        yield
        E.op("scalar", I("activation", out=w.usq.ap[0:Dh, :], in_=pst.ap[0:Dh, :], func=AF.Square),
             reads=[pst], writes=[w.usq])
        yield
        E.op("tensor", I("matmul", ssp.ap[0:Dh, :], lhsT=C.ones.ap[0:Dh, 0:Dh], rhs=w.usq.ap[0:Dh, :],
                         start=True, stop=True), reads=[w.usq, C.ones], writes=[ssp])
        yield
        E.op("scalar", I("activation", out=w.uln.ap[0:Dh, :], in_=ssp.ap[0:Dh, :], func=AF.Ln,
                         scale=1.0 / Dh, bias=C.eps.ap[0:Dh, 0:1]),
             reads=[ssp, C.eps], writes=[w.uln])
        yield
        E.op("scalar", I("activation", out=w.urstd.ap[0:Dh, :], in_=w.uln.ap[0:Dh, :], func=AF.Exp,
                         scale=-0.5), reads=[w.uln], writes=[w.urstd])
        yield
        E.op("vector", I("scalar_tensor_tensor", out=w.uxn.ap[0:Dh, :], in0=pst.ap[0:Dh, :], scalar=g.ap[0:Dh, 0:1],
                         in1=w.urstd.ap[0:Dh, :], op0=ALU.mult, op1=ALU.mult),
             reads=[pst, g, w.urstd], writes=[w.uxn])
        yield
        E.op("tensor", I("matmul", pxp.ap[0:Dh, :], lhsT=Pm.ap[0:Dh, 0:Dh], rhs=w.uxn.ap[0:Dh, :],
                         start=True, stop=True), reads=[w.uxn, Pm], writes=[pxp])
        E.op("gpsimd", I("tensor_tensor", out=w.ut1.ap[0:Dh, :], in0=w.uxn.ap[0:Dh, :],
                         in1=Ct.ap[0:Dh, :], op=ALU.mult),
             reads=[w.uxn, Ct], writes=[w.ut1])
        yield
        E.op("vector", I("tensor_tensor", out=w.ut2.ap[0:Dh, :], in0=pxp.ap[0:Dh, :],
                         in1=St.ap[0:Dh, :], op=ALU.mult),
             reads=[pxp, St], writes=[w.ut2])
        yield
        E.op("gpsimd", I("tensor_tensor", out=w.uout.ap[0:Dh, :], in0=w.ut1.ap[0:Dh, :],
                         in1=w.ut2.ap[0:Dh, :], op=ALU.add),
             reads=[w.ut1, w.ut2], writes=[w.uout])
        yield
        E.dma("sync", [(dst_ap, w.uout.ap[0:Dh, :])], reads=[w.uout], writes=[dst_trk], sem=f"d_u{k % 4}")

    def win_proj(M, col0):
        def f(pst):
            for c in range(DC):
                E.op("tensor", I("matmul", pst.ap[0:M, :], lhsT=w_in.ap[:, c, col0:col0 + M], rhs=xn.ap[:, c, :],
                                 start=(c == 0), stop=(c == DC - 1)), reads=[w_in, xn], writes=[pst])
        return f

    pcount = [0]

    def proj(M, col0, cols=None):
        pst = pj[pcount[0] % 2]
        pcount[0] += 1
        for c in range(DC):
            E.op("tensor", I("matmul", pst.ap[0:M, :], lhsT=w_in.ap[:, c, col0:col0 + M],
                                                   rhs=xn.ap[:, c, :], start=(c == 0), stop=(c == DC - 1)),
                 reads=[w_in, xn], writes=[pst])
        return pst

    load(0)
    for t in range(NT):
        if t + 1 < NT:
            load(t + 1)
        sl = slice(t * TN, (t + 1) * TN)
        tb = tabs[t % 2]
        emit_rmsnorm_T(E, C, hb[t % 2], gain, sq, ss_ps, lnv, rstd, xn, TN)
        for h in range(8):
            push(unit(win_proj(64, h * 64), 64, g_qa, C.PA, tb[0], tb[1], C.QT[h, 0:64, sl], C.qt_trk))
        for g in range(2):
            push(unit(win_proj(64, 512 + g * 64), 64, g_ka, C.PA, tb[0], tb[1], C.KT[g, 0:64, sl], C.kt_trk))
        drain()
        if dbg == 5:
            E.barrier()
            return
        for sub in range(4):
            for c in range(DC):
                E.op("tensor", I("matmul", pv.ap[:, 0:128], lhsT=xn.ap[:, c, sub * 128:(sub + 1) * 128], rhs=w_in.ap[:, c, 640:768],
                    start=(c == 0), stop=(c == DC - 1)), reads=[xn, w_in], writes=[pv])
            E.op("vector", I("tensor_copy", out=va.ap[:, sub, :], in_=pv.ap[:, 0:128]),
                 reads=[pv], writes=[va])
        E.dma("sync", [(C.V[sl, 0:128].rearrange("(s p) n -> p s n", p=128), va.ap)],
              reads=[va], writes=[C.v_trk], sem="d_va")
        if dbg == 6:
            E.barrier()
            return
        cq_ps = []
        for jj in range(2):
            pst = proj(128, 768 + jj * 128)
            cq_ps.append(pst)
            E.op("scalar", I("activation", out=sq2.ap[:, jj, :], in_=pst.ap, func=AF.Square),
                 reads=[pst], writes=[sq2])
        if dbg == 71:
            E.barrier()
            return
        ssp = uss[0]
        for jj in range(2):
            E.op("tensor", I("matmul", ssp.ap, lhsT=C.ones.ap, rhs=sq2.ap[:, jj, :],
                                                     start=(jj == 0), stop=(jj == 1)),
                 reads=[sq2, C.ones], writes=[ssp])
        E.op("scalar", I("activation", out=ln2.ap, in_=ssp.ap, func=AF.Ln, scale=1.0 / 256,
                                              bias=C.eps.ap[:, 0:1]), reads=[ssp, C.eps], writes=[ln2])
        E.op("scalar", I("activation", out=rstd2.ap, in_=ln2.ap, func=AF.Exp, scale=-0.5),
             reads=[ln2], writes=[rstd2])
        if dbg == 72:
            E.barrier()
            return
        for jj in range(2):
            E.op("vector", I("scalar_tensor_tensor", out=cqn.ap[:, jj, :], in0=cq_ps[jj].ap, scalar=g_cq.ap[:, jj:jj + 1], in1=rstd2.ap,
                op0=ALU.mult, op1=ALU.mult), reads=[cq_ps[jj], g_cq, rstd2], writes=[cqn])
        if dbg == 7:
            E.barrier()
            return
        pst = proj(128, 1024)
        E.op("scalar", I("activation", out=sq2.ap[:, 0, :], in_=pst.ap, func=AF.Square),
             reads=[pst], writes=[sq2])
        ssp = uss[1]
        E.op("tensor", I("matmul", ssp.ap, lhsT=C.ones.ap, rhs=sq2.ap[:, 0, :], start=True, stop=True),
             reads=[sq2, C.ones], writes=[ssp])
        E.op("scalar", I("activation", out=ln2.ap, in_=ssp.ap, func=AF.Ln, scale=1.0 / 128,
                                              bias=C.eps.ap[:, 0:1]), reads=[ssp, C.eps], writes=[ln2])
        E.op("scalar", I("activation", out=rstd2.ap, in_=ln2.ap, func=AF.Exp, scale=-0.5),
             reads=[ln2], writes=[rstd2])
        E.op("vector", I("scalar_tensor_tensor", out=ckvn.ap, in0=pst.ap, scalar=g_ckv.ap[:, 0:1], in1=rstd2.ap, op0=ALU.mult, op1=ALU.mult),
            reads=[pst, g_ckv, rstd2], writes=[ckvn])
        if dbg == 8:
            E.barrier()
            return
        def uq_proj(h):
            def f(pst):
                for jj in range(2):
                    E.op("tensor", I("matmul", pst.ap[0:96, :], lhsT=w_uq.ap[:, jj, h * 96:(h + 1) * 96], rhs=cqn.ap[:, jj, :],
                                     start=(jj == 0), stop=(jj == 1)), reads=[w_uq, cqn], writes=[pst])
            return f

        def kb_proj(h):
            def f(pst):
                E.op("tensor", I("matmul", pst.ap[0:96, :], lhsT=wuk.ap[:, h, :], rhs=ckvn.ap, start=True, stop=False),
                     reads=[wuk, ckvn], writes=[pst])
                for c in range(DC):
                    E.op("tensor", I("matmul", pst.ap[0:96, :], lhsT=wkr.ap[:, c, :], rhs=xn.ap[:, c, :],
                                     start=False, stop=(c == DC - 1)), reads=[wkr, xn], writes=[pst])
            return f

        for h in range(8):
            push(unit(uq_proj(h), 96, g_qb, C.PB, tb[2], tb[3], C.QT[8 + h, 0:96, sl], C.qt_trk))
        for h in range(8):
            push(unit(kb_proj(h), 96, g_kb, C.PB, tb[2], tb[3], C.KT[2 + h, 0:96, sl], C.kt_trk))
        drain()
        for sub in range(4):
            E.op("tensor", I("matmul", pv.ap, lhsT=ckvn.ap[:, sub * 128:(sub + 1) * 128],
                                                       rhs=ukv_h[:, :, 64:128], start=True, stop=True),
                 reads=[ckvn, w_ukv], writes=[pv])
            E.op("vector", I("tensor_copy", out=vb.ap[:, sub, :], in_=pv.ap),
                 reads=[pv], writes=[vb])
        E.dma("sync", [(C.V[sl, 128:640].rearrange("(s p) n -> p s n", p=128), vb.ap)],
              reads=[vb], writes=[C.v_trk], sem="d_vb")
    E.barrier()


def phase_attn_core(E, C, j, src):
    A = C.arena
    A.reset(C.arena_base)
    oT_all = A.alloc("oT_all", [128, 8, S], BF16)
    w_out = A.alloc("w_out", [128, DC, D], BF16)
    ktb = [A.alloc(f"ktb{i}", [128, S], BF16) for i in range(2)]
    qtb = [A.alloc(f"qtb{i}", [128, S], BF16) for i in range(2)]
    vtb = [A.alloc(f"vtb{i}", [128, 32, 128], BF16) for i in range(2)]
    pt = [A.alloc(f"pt{i}", [128, 3, 512], BF16) for i in range(2)]
    rd = [A.alloc(f"rd{i}", [128, 512], F32) for i in range(2)]
    mark = A.top
    s_ps = [Trk(f"s_ps{i}", C.psum_all[:, i * 1536:(i + 1) * 1536]) for i in range(2)]
    o_ps = [Trk(f"o_ps{i}", C.psum[6 + i].ap) for i in range(2)]
    groups = [(k0, 3) for k0 in range(0, 30, 3)] + [(30, 2)]
    wload(E, w_out, C.dram["attn_w_out"][j], "d_w0")
    for i in range(2):
        E.op("vector", I("memset", vtb[i].ap[:, :, 64:128], 1.0), writes=[vtb[i]])
        E.op("vector", I("memset", ktb[i].ap[64:128, :], 0.0), writes=[ktb[i]])
        E.op("gpsimd", I("memset", qtb[i].ap[64:128, :], 0.0), writes=[qtb[i]])
    kv_loaded = [None, None]
    fin = 0
    for hh in range(16):
        b = hh % 2
        if hh < 8:
            Dh, kv, vc0 = 64, hh // 4, (hh // 4) * 64
        else:
            Dh, kv, vc0 = 96, 2 + (hh - 8), 128 + (hh - 8) * 64
        scale = float(Dh) ** -0.5
        E.dma("sync", [(qtb[b].ap[0:Dh, :], C.QT[hh, 0:Dh, :])], reads=[C.qt_trk], writes=[qtb[b]], sem=f"d_q{b}")
        if kv_loaded[b] != kv:
            E.dma("sync", [(ktb[b].ap[0:Dh, :], C.KT[kv, 0:Dh, :])], reads=[C.kt_trk], writes=[ktb[b]],
                  sem=f"d_k{b}")
            vsrc = C.V[:, vc0:vc0 + 64].rearrange("(k p) n -> p k n", p=128)
            E.dma("sync", [(vtb[b].ap[:, k0:k0 + 8, 0:64], vsrc[:, k0:k0 + 8, :]) for k0 in range(0, 32, 8)],
                  reads=[C.v_trk], writes=[vtb[b]], sem=f"d_v{b}")
            kv_loaded[b] = kv
        kt_, qt_, vt_ = ktb[b], qtb[b], vtb[b]
        pb = (hh % 2) * 64
        for qb in range(8):
            def smm(gi):
                sp = s_ps[gi % 2]
                k0, n = groups[gi]
                for jj in range(n):
                    kt = k0 + jj
                    E.op("tensor", I("matmul", sp.ap[:, jj * 512:(jj + 1) * 512], lhsT=kt_.ap[:, kt * 128:(kt + 1) * 128],
                                     rhs=qt_.ap[:, qb * 512:(qb + 1) * 512], start=True, stop=True),
                         reads=[kt_, qt_], writes=[sp])
            op_ = o_ps[fin % 2]
            rdt = rd[fin % 2]
            fin += 1
            smm(0)
            for gi, (k0, n) in enumerate(groups):
                if gi + 1 < len(groups):
                    smm(gi + 1)
                sp, pp = s_ps[gi % 2], pt[gi % 2]
                E.op("scalar", I("activation", out=pp.ap.rearrange("p a b -> p (a b)")[:, 0:n * 512], in_=sp.ap[:, 0:n * 512],
                                 func=AF.Exp, scale=scale), reads=[sp], writes=[pp])
                for jj in range(n):
                    kt = k0 + jj
                    E.op("tensor", I("matmul", op_.ap, lhsT=vt_.ap[:, kt, :], rhs=pp.ap[:, jj, :],
                                     start=(kt == 0), stop=(kt == 31)), reads=[pp, vt_], writes=[op_])
            E.op("vector", I("reciprocal", out=rdt.ap[64:128, :], in_=op_.ap[64:128, :]), reads=[op_], writes=[rdt])
            E.op("vector", I("tensor_tensor", out=oT_all.ap[pb:pb + 64, hh // 2, qb * 512:(qb + 1) * 512],
                             in0=op_.ap[0:64, :], in1=rdt.ap[64:128, :], op=ALU.mult),
                 reads=[op_, rdt], writes=[oT_all])
    E.barrier()

    A.reset(mark)
    TN = 512
    hb = [A.alloc(f"hb{i}", [128, DC, TN], F32) for i in range(2)]
    ps = [Trk(f"ps{i}", C.psum[i].ap) for i in range(8)]
    srcv = src.rearrange("(c p) s -> p c s", p=128)
    dstv = C.hT.rearrange("(c p) s -> p c s", p=128)

    def load(t):
        E.dma("sync", [(hb[t % 2].ap, srcv[:, :, t * TN:(t + 1) * TN])], reads=C.h_trk[2 * t:2 * t + 2],
              writes=[hb[t % 2]], sem=f"d_ld{t % 2}")

    load(0)
    for t in range(S // TN):
        if t + 1 < S // TN:
            load(t + 1)
        b = hb[t % 2]
        for dm in range(DC):
            yp = ps[dm % 4]
            for c in range(DC):
                E.op("tensor", I("matmul", yp.ap, lhsT=w_out.ap[:, c, dm * 128:(dm + 1) * 128],
                                 rhs=oT_all.ap[:, c, t * TN:(t + 1) * TN], start=(c == 0), stop=(c == DC - 1)),
                     reads=[w_out, oT_all], writes=[yp])
            E.op("vector", I("tensor_tensor", out=b.ap[:, dm, :], in0=b.ap[:, dm, :], in1=yp.ap, op=ALU.add),
                 reads=[b, yp], writes=[b])
        E.dma("sync", [(dstv[:, :, t * TN:(t + 1) * TN], b.ap)], reads=[b], writes=C.h_trk[2 * t:2 * t + 2],
              sem=f"d_st{t % 2}")
    E.barrier()


LN_LO = float(np.log(np.float32(1e-6)))
LN_HI = float(np.log(np.float32(1.0) - np.float32(1e-6)))


def _interleave(*gens):
    act = [g for g in gens if g is not None]
    while act:
        for g in list(act):
            try:
                next(g)
            except StopIteration:
                act.remove(g)


def phase_rec_dir(E, C, j, src, dirn):
    A = C.arena
    A.reset(C.arena_base)
    TN = 512
    NT = S // TN
    bwd = dirn == 1
    win = C.dram["rec_w_in"][j]
    wq = A.alloc("wq", [128, DC, 1024], BF16)
    wz = A.alloc("wz", [128, DC, 1024], BF16)
    wi = A.alloc("wi", [128, DC, 1024], BF16)
    if bwd:
        wg = A.alloc("wg", [128, DC, 1024], BF16)
        w_out = A.alloc("w_out", [128, DC, D], BF16)
        g_o = A.alloc("g_o", [128, 1], F32)
    gain = A.alloc("rgain", [128, DC], F32)
    lbv = A.alloc("lbv", [128, 8], F32)
    l0 = A.alloc("l0", [128, 8], F32)
    l1 = A.alloc("l1", [128, 8], F32)
    one = A.alloc("one", [128, 1], F32)
    mask = A.alloc("mask", [128, 128], F32)
    rst = A.alloc("rst", [128, TN], F32)
    hb = [A.alloc(f"hb{i}", [128, DC, TN], F32) for i in range(2 if not bwd else 1)]
    sq = A.alloc("sq", [128, DC, TN], BF16)
    xn = A.alloc("xn", [128, DC, TN], BF16)
    lnv = A.alloc("lnv", [128, TN], F32)
    rstd = A.alloc("rstd", [128, TN], F32)
    vtok = A.alloc("vtok", [128, 4, 1024], BF16)
    Ws = [[A.alloc(f"W{k}_{i}", [128, TN], F32) for i in range(6)] for k in range(2)]
    ek = A.alloc("ek", [128, TN], F32)
    eqs = [A.alloc(f"eq{i}", [128, TN], F32) for i in range(2)]
    qtls = [A.alloc(f"qtl{i}", [128, TN], BF16) for i in range(2)]
    ktl = A.alloc("ktl", [128, TN], BF16)
    ams = [A.alloc(f"am{i}", [128, 4, 128], BF16) for i in range(2)]
    ktokAs = [A.alloc(f"ktokA{i}", [128, 4, 128], BF16) for i in range(2)]
    ktokBs = [A.alloc(f"ktokB{i}", [128, 4, 128], BF16) for i in range(2)]
    state = A.alloc("state", [128, 8, 128], F32)
    smid = A.alloc("smid", [128, 128], BF16)
    tmp = A.alloc("tmp", [128, 128], F32)
    d1s = [A.alloc(f"d1{i}", [128, 8], F32) for i in range(2)]
    emids = [A.alloc(f"emid{i}", [128, 8], F32) for i in range(2)]
    if bwd:
        ofw = A.alloc("ofw", [128, TN], F32)
        osum = A.alloc("osum", [128, TN], F32)
        sqo = A.alloc("sqo", [128, TN], BF16)
        lno = A.alloc("lno", [128, TN], F32)
        rso = A.alloc("rso", [128, TN], F32)
        sgt = A.alloc("sgt", [128, TN], F32)
        ogT = A.alloc("ogT", [128, 8, TN], BF16)
    else:
        ofs = [A.alloc(f"ofs{i}", [128, TN], F32) for i in range(2)]
    ps = [Trk(f"ps{i}", C.psum[i].ap) for i in range(8)]
    ss_ps, q_ps, z_ps, a_ps, t_ps, o_ps = ps[0], ps[1], ps[2], ps[4], ps[5], ps[6]
    v_ps = ps[0]
    kv_ps = [ps[3], ps[7]]

    sm = "d_small"
    E.dma("sync", [(gain.ap, C.dram["rec_norm"][j])], writes=[gain], sem=sm, group=True)
    E.dma("sync", [(l0.ap, C.dram["rec_lb"][dirn, 0])], writes=[l0], sem=sm, group=True)
    E.dma("sync", [(l1.ap, C.dram["rec_lb"][dirn, 1])], writes=[l1], sem=sm, group=True)
    E.dma("sync", [(mask.ap, C.dram["c_maskb" if bwd else "c_maskf"])], writes=[mask], sem=sm, group=True)
    E.dma("sync", [(rst.ap, C.dram["c_rst"])], writes=[rst], sem=sm, group=True)
    wload(E, wq, win[:, 0:1024], "d_w0")
    wload(E, wz, win[:, (2048 if bwd else 1024):(3072 if bwd else 2048)], "d_w1")
    wload(E, wi, win[:, 3072:4096], "d_w2")
    if bwd:
        wload(E, wg, win[:, 4096:5120], "d_w3")
        wload(E, w_out, C.dram["rec_w_out"][j], "d_w4")
        E.dma("sync", [(g_o.ap, C.dram["rec_out_norm"][j])], writes=[g_o], sem=sm, group=True)
    E.group_end(sm)
    E.op("vector", I("memset", one.ap, 1.0), writes=[one])
    E.op("vector", I("memset", state.ap, 0.0), writes=[state])
    for i in range(2):
        E.op("vector", I("memset", ktokAs[i].ap, 0.0), writes=[ktokAs[i]])
        E.op("gpsimd", I("memset", ktokBs[i].ap, 0.0), writes=[ktokBs[i]])
    if j == 0:
        E.op("vector", I("memset", lbv.ap, 0.0), writes=[lbv])
    else:
        E.op("vector", I("tensor_tensor", out=l0.ap, in0=l0.ap, in1=l1.ap, op=ALU.subtract), reads=[l0, l1], writes=[l0])
        E.op("scalar", I("activation", out=l1.ap, in_=l0.ap, func=AF.Exp), reads=[l0], writes=[l1])
        E.op("scalar", I("activation", out=l0.ap, in_=l1.ap, func=AF.Ln, bias=one.ap[:, 0:1]), reads=[l1, one], writes=[l0])
        E.op("scalar", I("activation", out=lbv.ap, in_=l0.ap, func=AF.Exp, scale=-1.0), reads=[l0], writes=[lbv])

    srcv = src.rearrange("(c p) s -> p c s", p=128)
    dstv = C.hT.rearrange("(c p) s -> p c s", p=128)
    torder = list(range(NT - 1, -1, -1)) if bwd else list(range(NT))
    corder = list(range(7, -1, -1)) if bwd else list(range(8))
    ridx = 32 if bwd else 31
    lidx = 0 if bwd else 63
    mask3 = mask.ap.rearrange("p (a b) -> p a b", a=1).to_broadcast([128, 4, 128])
    porder = list(range(3, -1, -1)) if bwd else list(range(4))

    def v3(t):
        return t.ap.rearrange("p (c l) -> p c l", c=8)

    def load(i):
        t = torder[i]
        b = hb[i % len(hb)]
        E.dma("sync", [(b.ap, srcv[:, :, t * TN:(t + 1) * TN])], reads=C.h_trk[2 * t:2 * t + 2], writes=[b],
              sem=f"d_ld{i % len(hb)}")

    cnt = {"kv": 0, "cp": 0}

    def stage1a(hd):
        par = hd % 2
        hs = slice(hd * 128, (hd + 1) * 128)
        qs, u, la, lb_, Bt, Bm = Ws[par]
        for dc in range(DC):
            E.op("tensor", I("matmul", q_ps.ap, lhsT=wq.ap[:, dc, hs], rhs=xn.ap[:, dc, :],
                             start=(dc == 0), stop=(dc == DC - 1)), reads=[wq, xn], writes=[q_ps])
        for dc in range(DC):
            E.op("tensor", I("matmul", z_ps.ap, lhsT=wz.ap[:, dc, hs], rhs=xn.ap[:, dc, :],
                             start=(dc == 0), stop=(dc == DC - 1)), reads=[wz, xn], writes=[z_ps])
        yield
        E.op("scalar", I("activation", out=u.ap, in_=z_ps.ap, func=AF.Exp, scale=-1.0), reads=[z_ps], writes=[u])
        E.op("scalar", I("activation", out=la.ap, in_=u.ap, func=AF.Ln, bias=one.ap[:, 0:1]),
             reads=[u, one], writes=[la])
        E.op("scalar", I("activation", out=lb_.ap, in_=u.ap, func=AF.Ln, bias=one.ap[:, 0:1],
                         scale=lbv.ap[:, hd:hd + 1]), reads=[u, one, lbv], writes=[lb_])
        yield
        E.op("vector", I("tensor_tensor", out=lb_.ap, in0=lb_.ap, in1=la.ap, op=ALU.subtract),
             reads=[lb_, la], writes=[lb_])
        E.op("vector", I("tensor_scalar", out=lb_.ap, in0=lb_.ap, scalar1=LN_HI, scalar2=LN_LO,
                         op0=ALU.min, op1=ALU.max), reads=[lb_], writes=[lb_])
        E.op("scalar", I("activation", out=u.ap, in_=lb_.ap, func=AF.Exp), reads=[lb_], writes=[u])
        yield
        E.op("vector", I("tensor_tensor_scan", out=la.ap, data0=rst.ap, data1=lb_.ap, initial=0.0,
                         op0=ALU.mult, op1=ALU.add), reads=[rst, lb_], writes=[la])
        E.op("vector", I("tensor_scalar", out=u.ap, in0=u.ap, scalar1=-1.0, scalar2=1.0,
                         op0=ALU.mult, op1=ALU.add), reads=[u], writes=[u])
        yield
        if bwd:
            E.op("vector", I("tensor_tensor", out=Bt.ap, in0=lb_.ap, in1=la.ap, op=ALU.subtract),
                 reads=[lb_, la], writes=[Bt])
            E.op("vector", I("tensor_tensor", out=v3(Bt), in0=v3(Bt),
                             in1=v3(la)[:, :, 63:64].to_broadcast([128, 8, 64]), op=ALU.add),
                 reads=[Bt, la], writes=[Bt])
            Bsrc = Bt
        else:
            Bsrc = la
        E.op("vector", I("tensor_tensor", out=v3(Bm), in0=v3(Bsrc),
                         in1=v3(Bsrc)[:, :, ridx:ridx + 1].to_broadcast([128, 8, 64]), op=ALU.subtract),
             reads=[Bsrc], writes=[Bm])
        yield
        E.op("scalar", I("activation", out=qs.ap, in_=q_ps.ap, func=AF.Silu), reads=[q_ps], writes=[qs])
        yield

    def stage1b(hd):
        par = hd % 2
        eq, qtl, am, d1, emid = eqs[par], qtls[par], ams[par], d1s[par], emids[par]
        ktokA, ktokB = ktokAs[par], ktokBs[par]
        qs, u, la, lb_, Bt, Bm = Ws[par]
        Bsrc = Bt if bwd else la
        E.op("scalar", I("activation", out=eq.ap, in_=Bm.ap, func=AF.Exp), reads=[Bm], writes=[eq])
        E.op("scalar", I("activation", out=ek.ap, in_=Bm.ap, func=AF.Exp, scale=-1.0), reads=[Bm], writes=[ek])
        E.op("scalar", I("activation", out=d1.ap, in_=v3(Bsrc)[:, :, lidx], func=AF.Exp), reads=[Bsrc], writes=[d1])
        E.op("scalar", I("activation", out=emid.ap, in_=v3(Bsrc)[:, :, ridx], func=AF.Exp), reads=[Bsrc], writes=[emid])
        yield
        E.op("gpsimd", I("tensor_tensor", out=ktl.ap, in0=u.ap, in1=ek.ap, op=ALU.mult),
             reads=[u, ek], writes=[ktl])
        E.op("gpsimd", I("tensor_tensor", out=qtl.ap, in0=qs.ap, in1=eq.ap, op=ALU.mult),
             reads=[qs, eq], writes=[qtl])
        yield
        tpb = t_ps.ap.bitcast(BF16)
        for p in range(4):
            E.op("tensor", I("transpose", out=tpb[:, p * 128:(p + 1) * 128], in_=ktl.ap[:, p * 128:(p + 1) * 128],
                             identity=C.ident.ap), reads=[ktl, C.ident], writes=[t_ps])
        E.op("scalar", I("copy", out=ktokA.ap[0:64].rearrange("p a b -> p (a b)"), in_=tpb[0:64, 0:512]),
             reads=[t_ps], writes=[ktokA])
        E.op("scalar", I("copy", out=ktokB.ap[64:128].rearrange("p a b -> p (a b)"), in_=tpb[64:128, 0:512]),
             reads=[t_ps], writes=[ktokB])
        yield
        for p in range(4):
            ps_ = slice(p * 128, (p + 1) * 128)
            E.op("tensor", I("matmul", a_ps.ap[:, ps_], lhsT=ktl.ap[:, ps_], rhs=qtl.ap[:, ps_],
                             start=True, stop=True), reads=[ktl, qtl], writes=[a_ps])
        E.op("vector", I("tensor_tensor", out=am.ap, in0=a_ps.ap.rearrange("p (c l) -> p c l", c=4),
                         in1=mask3, op=ALU.mult), reads=[a_ps, mask], writes=[am])
        yield

    def stage2(hd, t, b):
        par = hd % 2
        hs = slice(hd * 128, (hd + 1) * 128)
        sl = slice(t * TN, (t + 1) * TN)
        eq, qtl, am, d1, emid = eqs[par], qtls[par], ams[par], d1s[par], emids[par]
        ktokA, ktokB = ktokAs[par], ktokBs[par]
        if bwd:
            E.dma("sync", [(ofw.ap, C.OB[hd, :, sl])], reads=[C.ob_trk[t]], writes=[ofw], sem="d_ofw")
        for p in porder:
            E.op("tensor", I("matmul", o_ps.ap[:, p * 128:(p + 1) * 128], lhsT=vtok.ap[:, p, hs], rhs=am.ap[:, p, :],
                             start=True, stop=False), reads=[vtok, am], writes=[o_ps])
            pair = [2 * p + 1, 2 * p] if bwd else [2 * p, 2 * p + 1]
            for ci, c in enumerate(pair):
                cs_ = slice(c * 64, (c + 1) * 64)
                kvp = kv_ps[cnt["kv"] % 2]
                cnt["kv"] += 1
                ktk = ktokA if c % 2 == 0 else ktokB
                E.op("tensor", I("matmul", kvp.ap[:, 0:128], lhsT=ktk.ap[:, p, :], rhs=vtok.ap[:, p, hs], start=True, stop=True),
                     reads=[ktk, vtok], writes=[kvp])
                E.op("vector", I("tensor_scalar", out=smid.ap, in0=state.ap[:, hd, :], scalar1=emid.ap[:, c:c + 1],
                                 scalar2=None, op0=ALU.mult), reads=[state, emid], writes=[smid])
                E.op("tensor", I("matmul", o_ps.ap[:, cs_], lhsT=smid.ap, rhs=qtl.ap[:, cs_], start=False, stop=(ci == 1)),
                     reads=[smid, qtl], writes=[o_ps])
                E.op("vector", I("tensor_scalar", out=tmp.ap, in0=kvp.ap[:, 0:128], scalar1=eq.ap[:, c * 64 + lidx:c * 64 + lidx + 1],
                                 scalar2=None, op0=ALU.mult), reads=[kvp, eq], writes=[tmp])
                E.op("vector", I("scalar_tensor_tensor", out=state.ap[:, hd, :], in0=state.ap[:, hd, :],
                                 scalar=d1.ap[:, c:c + 1], in1=tmp.ap, op0=ALU.mult, op1=ALU.add),
                     reads=[state, d1, tmp], writes=[state])
                yield
        if bwd:
            E.op("vector", I("tensor_tensor", out=osum.ap, in0=ofw.ap, in1=o_ps.ap, op=ALU.add),
                 reads=[ofw, o_ps], writes=[osum])
            yield
            return
        if not bwd:
            of = ofs[hd % 2]
            E.op("scalar", I("activation", out=of.ap, in_=o_ps.ap, func=AF.Identity), reads=[o_ps], writes=[of])
            E.dma("sync", [(C.OB[hd, :, sl], of.ap)], reads=[of], writes=[C.ob_trk[t]], sem=f"d_of{hd % 2}")
        yield

    def stage3(hd):
        hs = slice(hd * 128, (hd + 1) * 128)
        if True:
            E.op("scalar", I("activation", out=sqo.ap, in_=osum.ap, func=AF.Square), reads=[osum], writes=[sqo])
            E.op("tensor", I("matmul", v_ps.ap, lhsT=C.ones.ap, rhs=sqo.ap, start=True, stop=True),
                 reads=[sqo, C.ones], writes=[v_ps])
            yield
            E.op("scalar", I("activation", out=lno.ap, in_=v_ps.ap, func=AF.Ln, scale=1.0 / 128,
                             bias=C.eps.ap[:, 0:1]), reads=[v_ps, C.eps], writes=[lno])
            E.op("scalar", I("activation", out=rso.ap, in_=lno.ap, func=AF.Exp, scale=-0.5), reads=[lno], writes=[rso])
            E.op("vector", I("scalar_tensor_tensor", out=osum.ap, in0=osum.ap, scalar=g_o.ap[:, 0:1], in1=rso.ap,
                             op0=ALU.mult, op1=ALU.mult), reads=[osum, g_o, rso], writes=[osum])
            for dc in range(DC):
                E.op("tensor", I("matmul", v_ps.ap, lhsT=wg.ap[:, dc, hs], rhs=xn.ap[:, dc, :],
                                 start=(dc == 0), stop=(dc == DC - 1)), reads=[wg, xn], writes=[v_ps])
            yield
            E.op("scalar", I("activation", out=sgt.ap, in_=v_ps.ap, func=AF.Silu), reads=[v_ps], writes=[sgt])
            E.op("vector", I("tensor_tensor", out=ogT.ap[:, hd, :], in0=osum.ap, in1=sgt.ap, op=ALU.mult),
                 reads=[osum, sgt], writes=[ogT])
        yield

    load(0)
    for i, t in enumerate(torder):
        sl = slice(t * TN, (t + 1) * TN)
        if i + 1 < NT and len(hb) == 2:
            load(i + 1)
        b = hb[i % len(hb)]
        emit_rmsnorm_T(E, C, b, gain, sq, ss_ps, lnv, rstd, xn, TN)
        for p in range(4):
            for hg in range(2):
                for dc in range(DC):
                    E.op("tensor", I("matmul", v_ps.ap, lhsT=xn.ap[:, dc, p * 128:(p + 1) * 128],
                                     rhs=wi.ap[:, dc, hg * 512:(hg + 1) * 512], start=(dc == 0), stop=(dc == DC - 1)),
                         reads=[xn, wi], writes=[v_ps])
                if cnt["cp"] % 2 == 0:
                    E.op("vector", I("tensor_copy", out=vtok.ap[:, p, hg * 512:(hg + 1) * 512], in_=v_ps.ap),
                         reads=[v_ps], writes=[vtok])
                else:
                    E.op("scalar", I("copy", out=vtok.ap[:, p, hg * 512:(hg + 1) * 512], in_=v_ps.ap),
                         reads=[v_ps], writes=[vtok])
                cnt["cp"] += 1
        _interleave(stage1a(0))
        _interleave(stage1a(1), stage1b(0))
        for hd in range(8):
            _interleave(stage1a(hd + 2) if hd + 2 < 8 else None, stage1b(hd + 1) if hd + 1 < 8 else None,
                        stage2(hd, t, b), stage3(hd - 1) if (bwd and hd >= 1) else None)
        if bwd:
            _interleave(stage3(7))
        if bwd:
            for dm in range(DC):
                yp = ps[1 + dm % 2]
                for hd in range(8):
                    E.op("tensor", I("matmul", yp.ap, lhsT=w_out.ap[:, hd, dm * 128:(dm + 1) * 128], rhs=ogT.ap[:, hd, :],
                                     start=(hd == 0), stop=(hd == 7)), reads=[w_out, ogT], writes=[yp])
                E.op("vector", I("tensor_tensor", out=b.ap[:, dm, :], in0=b.ap[:, dm, :], in1=yp.ap, op=ALU.add),
                     reads=[b, yp], writes=[b])
            E.dma("sync", [(dstv[:, :, sl], b.ap)], reads=[b], writes=C.h_trk[2 * t:2 * t + 2], sem="d_st0")
            if i + 1 < NT:
                load(i + 1)
        elif len(hb) == 1 and i + 1 < NT:
            load(i + 1)
    E.barrier()


SMALL_SPECS = {
    "ffn_norm": [DEPTH, 128, DC],
    "attn_norm": [2, 128, DC],
    "gqa_q_norm": [2, 64, 1],
    "gqa_k_norm": [2, 64, 1],
    "mla_cq_norm": [2, 128, 2],
    "mla_ckv_norm": [2, 128, 1],
    "mla_q_norm": [2, 96, 1],
    "mla_k_norm": [2, 96, 1],
    "rec_norm": [2, 128, DC],
    "rec_lb": [2, 2, 128, DC],
    "rec_out_norm": [2, 128, 1],
    "c_ident": [128, 128],
    "c_PA": [64, 64],
    "c_PB": [96, 96],
    "c_CA": [64, S],
    "c_SA": [64, S],
    "c_CB": [96, S],
    "c_SB": [96, S],
    "c_maskf": [128, 128],
    "c_maskb": [128, 128],
    "c_rst": [128, 512],
}
WEIGHT_SPECS = {
    "ffn_w_gate": [DEPTH, D, FH],
    "ffn_w_up": [DEPTH, D, FH],
    "ffn_w_down": [DEPTH, FH, D],
    "attn_w_in": [2, D, 1184],
    "mla_w_uq": [2, 256, 768],
    "mla_w_ukv": [2, 128, 1024],
    "attn_w_out": [2, D, D],
    "rec_w_in": [2, D, 5120],
    "rec_w_out": [2, D, D],
}


def build_program(plan, debug=False, dbg=0):
    nc = bass.Bass("TRN2", target_bir_lowering=False)
    sck = "ExternalOutput" if debug else "Internal"
    E = Em()
    C = Ctx()
    C.dbg = dbg
    C.dram = {}
    xT = nc.dram_tensor("xT", [D, S], F32, kind="ExternalInput").ap()
    for name, shp in list(SMALL_SPECS.items()) + list(WEIGHT_SPECS.items()):
        C.dram[name] = nc.dram_tensor(name, shp, F32, kind="ExternalInput").ap()
    C.hT = nc.dram_tensor("yT", [D, S], F32, kind="ExternalOutput").ap()
    C.h_trk = [Trk(f"h{t}") for t in range(S // 256)]

    with ExitStack() as es:
        arena_h = es.enter_context(nc.sbuf_tensor("arena", [128, ARENA_BYTES // 4], F32))
        C.arena = Arena(arena_h, ARENA_BYTES)
        C.psum_all = es.enter_context(nc.psum_tensor("psall", [128, 4096], F32))[:, :]
        C.psum = [Trk(f"psb{i}", C.psum_all[:, i * 512:(i + 1) * 512]) for i in range(8)]
        C.ones = C.arena.alloc("ones", [128, 128], BF16)
        C.eps = C.arena.alloc("eps", [128, 1], F32)
        C.ident = C.arena.alloc("ident", [128, 128], BF16)
        C.PA = C.arena.alloc("PA", [64, 64], BF16)
        C.PB = C.arena.alloc("PB", [96, 96], BF16)
        C.arena_base = C.arena.top
        E.op("vector", I("memset", C.ones.ap, 1.0), writes=[C.ones])
        E.op("vector", I("memset", C.eps.ap, EPS), writes=[C.eps])
        E.dma("gpsimd", [(C.ident.ap, C.dram["c_ident"])], writes=[C.ident], sem="d_c0")
        E.dma("gpsimd", [(C.PA.ap, C.dram["c_PA"])], writes=[C.PA], sem="d_c1")
        E.dma("gpsimd", [(C.PB.ap, C.dram["c_PB"])], writes=[C.PB], sem="d_c2")
        C.QT = nc.dram_tensor("sc_QT", [16, 96, S], BF16, kind=sck).ap()
        C.KT = nc.dram_tensor("sc_KT", [10, 96, S], BF16, kind=sck).ap()
        C.V = nc.dram_tensor("sc_V", [S, 640], BF16, kind=sck).ap()
        C.OB = nc.dram_tensor("sc_OB", [8, 128, S], F32, kind=sck).ap()
        C.qt_trk = Trk("QT")
        C.kt_trk = Trk("KT")
        C.v_trk = Trk("V")
        C.ob_trk = [Trk(f"OB{t}") for t in range(8)]

        src = xT
        for (kind, layer) in plan:
            if kind == "ffn":
                phase_ffn(E, C, layer, src)
            elif kind == "attn":
                phase_attn_proj(E, C, layer, src)
                phase_attn_core(E, C, layer, src)
            elif kind == "attn_proj":
                phase_attn_proj(E, C, layer, src)
                continue
            elif kind == "attn_core":
                phase_attn_core(E, C, layer, src)
            elif kind == "rec_f":
                phase_rec_dir(E, C, layer, src, 0)
                continue
            elif kind == "rec_b":
                phase_rec_dir(E, C, layer, src, 1)
            elif kind == "rec":
                phase_rec_dir(E, C, layer, src, 0)
                phase_rec_dir(E, C, layer, src, 1)
            src = C.hT
        E.final_wait()

        sems = {}
        for k in E.sem_keys():
            sems[k] = es.enter_context(nc.semaphore(k))
        with nc.allow_low_precision("bf16 matmul operands, fp32 accumulation"):
            block = es.enter_context(nc.Block())
            E.replay(block, sems)
    return nc, E


def _rope_tables(rot_dim, lead):
    n_rows = S // 64
    row = np.repeat(np.arange(n_rows, dtype=np.float32), 64)
    col = np.tile(np.arange(64, dtype=np.float32), n_rows)
    sec = rot_dim // 2
    inv = (np.float32(10000.0) ** (-np.arange(0, sec, 2, dtype=np.float32) / np.float32(sec))).astype(np.float32)
    ang = np.concatenate([row[:, None] * inv, col[:, None] * inv], axis=-1).astype(np.float32)
    nf = rot_dim // 4
    Dh = lead + rot_dim
    Ct = np.ones((Dh, S), np.float32)
    St = np.zeros((Dh, S), np.float32)
    P = np.zeros((Dh, Dh), np.float32)
    for a in range(2):
        for j in range(nf):
            c = np.cos(ang[:, a * nf + j]).astype(np.float32)
            sn = np.sin(ang[:, a * nf + j]).astype(np.float32)
            i1 = lead + a * 2 * nf + j
            i2 = i1 + nf
            Ct[i1] = c
            Ct[i2] = c
            St[i1] = sn
            St[i2] = sn
            P[i2, i1] = -1.0
            P[i1, i2] = 1.0
    return Ct, St, P


def host_consts():
    out = {}
    out["c_ident"] = np.eye(128, dtype=np.float32)
    out["c_CA"], out["c_SA"], out["c_PA"] = _rope_tables(64, 0)
    out["c_CB"], out["c_SB"], out["c_PB"] = _rope_tables(32, 64)
    s_i = np.arange(128)[:, None]
    t_i = np.arange(128)[None, :]
    same = (s_i // 64) == (t_i // 64)
    out["c_maskf"] = ((s_i <= t_i) & same).astype(np.float32)
    out["c_maskb"] = ((s_i >= t_i) & same).astype(np.float32)
    rst = np.ones((128, 512), np.float32)
    rst[:, ::64] = 0.0
    out["c_rst"] = rst
    return out


def host_layout(inputs):
    out = dict(host_consts())
    f = lambda k: np.asarray(inputs[k], np.float32)
    out["ffn_norm"] = np.ascontiguousarray(f("ffn_norm").reshape(DEPTH, DC, 128).transpose(0, 2, 1))
    out["attn_norm"] = np.ascontiguousarray(f("attn_norm").reshape(2, DC, 128).transpose(0, 2, 1))
    out["rec_norm"] = np.ascontiguousarray(f("rec_norm").reshape(2, DC, 128).transpose(0, 2, 1))
    out["gqa_q_norm"] = np.ascontiguousarray(f("gqa_q_norm").reshape(2, 64, 1))
    out["gqa_k_norm"] = np.ascontiguousarray(f("gqa_k_norm").reshape(2, 64, 1))
    out["mla_cq_norm"] = np.ascontiguousarray(f("mla_cq_norm").reshape(2, 2, 128).transpose(0, 2, 1))
    out["mla_ckv_norm"] = np.ascontiguousarray(f("mla_ckv_norm").reshape(2, 128, 1))
    out["mla_q_norm"] = np.ascontiguousarray(f("mla_q_norm").reshape(2, 96, 1))
    out["mla_k_norm"] = np.ascontiguousarray(f("mla_k_norm").reshape(2, 96, 1))
    out["rec_lb"] = np.ascontiguousarray(f("rec_lower_bounds").reshape(2, 2, DC, 128).transpose(0, 1, 3, 2))
    out["rec_out_norm"] = np.ascontiguousarray(f("rec_out_norm").reshape(2, 128, 1))
    for k in WEIGHT_SPECS:
        out[k] = np.ascontiguousarray(np.asarray(inputs[k], np.float32))
    return out


FULL_PLAN = [("attn", 0), ("ffn", 0), ("rec", 0), ("ffn", 1), ("attn", 1), ("ffn", 2), ("rec", 1), ("ffn", 3)]


def kernel(**inputs):
    x = np.asarray(inputs["x"], np.float32)
    shared = host_layout(inputs)
    nc, _ = build_program(FULL_PLAN)
    in_maps = []
    for b in range(N_CORES):
        m = dict(shared)
        m["xT"] = np.ascontiguousarray(x[b].T)
        in_maps.append(m)
    res = run_bass_kernel_spmd(nc, in_maps, core_ids=list(range(N_CORES)))
    out = np.stack([np.asarray(res.results[b]["yT"]).T for b in range(N_CORES)])
    return np.ascontiguousarray(out.astype(np.float32))
```

```python
import numpy as np
from contextlib import ExitStack

import concourse.bass as bass
import concourse.mybir as mybir
from concourse.alu_op_type import AluOpType as ALU
from concourse.bass_utils import run_bass_kernel_spmd

F32 = mybir.dt.float32
BF16 = mybir.dt.bfloat16
AF = mybir.ActivationFunctionType
AX = mybir.AxisListType

S = 4096
D = 1024
DC = 8
FH = 2816
FC = 22
EPS = 1e-6
N_CORES = 8
DEPTH = 4
ARENA_BYTES = 204 * 1024


class Trk:
    __slots__ = ("name", "w", "r", "ap")

    def __init__(self, name="", ap=None):
        self.name = name
        self.w = None
        self.r = {}
        self.ap = ap

    def __getitem__(self, k):
        return self.ap[k]


class EngQ:
    def __init__(self, name):
        self.name = name
        self.n = 0
        self.waited = {}
        self.ops = []


ENGS = ("tensor", "vector", "scalar", "gpsimd", "sync")


class Em:
    def __init__(self):
        self.q = {e: EngQ(e) for e in ENGS}
        self.dval = {}
        self.groups = {}
        self.n_inst = 0

    def _wait(self, q, key, val):
        if q.waited.get(key, 0) < val:
            q.ops.append((0, key, val))
            q.waited[key] = val

    def _sync(self, q, reads, writes):
        own = q.name
        for t in reads:
            if t.w is not None:
                k, v = t.w
                if k == own and own == "tensor":
                    continue
                self._wait(q, k, v)
        for t in writes:
            if t.w is not None:
                k, v = t.w
                if not (k == own and own == "tensor"):
                    self._wait(q, k, v)
            for k, v in t.r.items():
                if k == own:
                    continue
                self._wait(q, k, v)

    @staticmethod
    def _mark(ev, reads, writes):
        k, v = ev
        for t in reads:
            if t.r.get(k, 0) < v:
                t.r[k] = v
        for t in writes:
            t.w = ev
            t.r = {}

    def op(self, eng, fn, reads=(), writes=()):
        q = self.q[eng]
        self._sync(q, reads, writes)
        q.n += 1
        q.ops.append((1, fn, eng, 1))
        self.n_inst += 1
        self._mark((eng, q.n), reads, writes)

    def dma(self, eng, pairs, reads=(), writes=(), sem=None, group=False, **kw):
        q = self.q[eng]
        self._sync(q, reads, writes)
        for (o, i) in pairs:
            self.dval[sem] = self.dval.get(sem, 0) + 16
            q.ops.append((1, I("dma_start", out=o, in_=i, **kw), sem, 16))
            self.n_inst += 1
        self._mark((sem, self.dval[sem]), reads, writes)
        if group:
            self.groups.setdefault(sem, []).extend(writes)

    def group_end(self, sem):
        for t in self.groups.pop(sem, []):
            if t.w is not None and t.w[0] == sem:
                t.w = (sem, self.dval[sem])

    def barrier(self):
        for q in self.q.values():
            for p in self.q.values():
                if p is not q and p.n > 0:
                    self._wait(q, p.name, p.n)
            for k, v in self.dval.items():
                self._wait(q, k, v)

    def final_wait(self):
        q = self.q["sync"]
        for k, v in self.dval.items():
            self._wait(q, k, v)
        for p in self.q.values():
            if p is not q and p.n > 0:
                self._wait(q, p.name, p.n)

    def sem_keys(self):
        return list(ENGS) + sorted(self.dval.keys())

    def replay(self, block, sems):
        for eng in ENGS:
            q = self.q[eng]

            def body(e, q=q):
                for o in q.ops:
                    if o[0] == 0:
                        e.wait_ge(sems[o[1]], o[2])
                    else:
                        o[1](e).then_inc(sems[o[2]], o[3])

            getattr(block, eng)(body)


class Arena:
    def __init__(self, handle, nbytes):
        self.h = handle
        self.cap = nbytes
        self.top = 0

    def reset(self, to=0):
        self.top = to

    def alloc(self, name, shape, dtype):
        n = 1
        for s in shape[1:]:
            n *= s
        esz = 4 if dtype == F32 else 2
        nb = (n * esz + 31) // 32 * 32
        off = self.top
        self.top += nb
        assert self.top <= self.cap, f"SBUF arena overflow at {name}: {self.top} > {self.cap}"
        ap = self.h[:, off // 4:(off + nb) // 4]
        if dtype != F32:
            ap = ap.bitcast(dtype)
        ap = ap[0:shape[0], 0:n]
        if len(shape) == 3:
            ap = ap.rearrange("p (a b) -> p a b", a=shape[1])
        elif len(shape) == 4:
            ap = ap.rearrange("p (a b c) -> p a b c", a=shape[1], b=shape[2])
        return Trk(name, ap)


class Ctx:
    pass


def I(method, *args, **kw):
    return lambda e: getattr(e, method)(*args, **kw)


def wload(E, dst, src, sem, eng="gpsimd"):
    C = dst.ap.shape[1]
    N = dst.ap.shape[2]
    srcv = src.rearrange("(c p) n -> p c n", p=128)
    pairs = []
    npieces = (N + 2047) // 2048
    step = (N + npieces - 1) // npieces
    for n0 in range(0, N, step):
        n1 = min(N, n0 + step)
        pairs.append((dst.ap[:, :, n0:n1], srcv[:, :, n0:n1]))
    E.dma(eng, pairs, writes=[dst], sem=sem)


def emit_rmsnorm_T(E, C, hbuf, gain, sq, ss_ps, lnv, rstd, xn, TN):
    E.op("scalar", I("activation", out=sq.ap, in_=hbuf.ap, func=AF.Square),
         reads=[hbuf], writes=[sq])
    for c in range(DC):
        E.op("tensor", I("matmul", ss_ps.ap[:, 0:TN], lhsT=C.ones.ap, rhs=sq.ap[:, c, :],
                                               start=(c == 0), stop=(c == DC - 1)),
             reads=[sq, C.ones], writes=[ss_ps])
    E.op("scalar", I("activation", out=lnv.ap, in_=ss_ps.ap[:, 0:TN], func=AF.Ln,
                                          scale=1.0 / D, bias=C.eps.ap[:, 0:1]),
         reads=[ss_ps, C.eps], writes=[lnv])
    E.op("scalar", I("activation", out=rstd.ap, in_=lnv.ap, func=AF.Exp, scale=-0.5),
         reads=[lnv], writes=[rstd])
    for c in range(DC):
        E.op("vector", I("scalar_tensor_tensor", out=xn.ap[:, c, :], in0=hbuf.ap[:, c, :], scalar=gain.ap[:, c:c + 1], in1=rstd.ap,
            op0=ALU.mult, op1=ALU.mult),
            reads=[hbuf, gain, rstd], writes=[xn])


def phase_ffn(E, C, layer, src, TN=256):
    A = C.arena
    A.reset(C.arena_base)
    NT = S // TN
    per = TN // 256
    HH = FH // 2
    wg = [A.alloc(f"wg{i}", [128, DC, HH], BF16) for i in range(2)]
    wu = [A.alloc(f"wu{i}", [128, DC, HH], BF16) for i in range(2)]
    wd = A.alloc("wd", [128, FC, D], BF16)
    gain = A.alloc("fgain", [128, DC], F32)
    hb = [A.alloc(f"hb{i}", [128, DC, TN], F32) for i in range(2)]
    sq = A.alloc("sq", [128, DC, TN], BF16)
    xn = A.alloc("xn", [128, DC, TN], BF16)
    lnv = A.alloc("lnv", [128, TN], F32)
    rstd = A.alloc("rstd", [128, TN], F32)
    sg = [A.alloc(f"sg{i}", [128, TN], F32) for i in range(2)]
    act = A.alloc("act", [128, FC, TN], BF16)
    ps = [Trk(f"ps{i}", C.psum[i].ap) for i in range(8)]
    ss_ps, g_ps, u_ps, y_ps = ps[0], ps[1:3], ps[3:5], ps[5:7]

    E.dma("sync", [(gain.ap, C.dram["ffn_norm"][layer])], writes=[gain], sem="d_small")
    wload(E, wg[0], C.dram["ffn_w_gate"][layer][:, 0:HH], "d_w0")
    wload(E, wu[0], C.dram["ffn_w_up"][layer][:, 0:HH], "d_w1")
    wload(E, wg[1], C.dram["ffn_w_gate"][layer][:, HH:FH], "d_w3")
    wload(E, wu[1], C.dram["ffn_w_up"][layer][:, HH:FH], "d_w4")
    wload(E, wd, C.dram["ffn_w_down"][layer], "d_w2")

    srcv = src.rearrange("(c p) s -> p c s", p=128)
    dstv = C.hT.rearrange("(c p) s -> p c s", p=128)

    def htrk(t):
        return C.h_trk[t * per:(t + 1) * per]

    def load(t):
        b = hb[t % 2]
        E.dma("sync", [(b.ap, srcv[:, :, t * TN:(t + 1) * TN])], reads=htrk(t), writes=[b],
              sem=f"d_ld{t % 2}")

    def prologue(t):
        emit_rmsnorm_T(E, C, hb[t % 2], gain, sq, ss_ps, lnv, rstd, xn, TN)

    def gateup(t):
        for f in range(FC):
            gp, up, sgt = g_ps[f % 2], u_ps[f % 2], sg[f % 2]
            wgt, wut, fo = wg[f // 11], wu[f // 11], (f % 11) * 128
            for c in range(DC):
                E.op("tensor", I("matmul", gp.ap[:, 0:TN], lhsT=wgt.ap[:, c, fo:fo + 128], rhs=xn.ap[:, c, :],
                    start=(c == 0), stop=(c == DC - 1)), reads=[wgt, xn], writes=[gp])
            for c in range(DC):
                E.op("tensor", I("matmul", up.ap[:, 0:TN], lhsT=wut.ap[:, c, fo:fo + 128], rhs=xn.ap[:, c, :],
                    start=(c == 0), stop=(c == DC - 1)), reads=[wut, xn], writes=[up])
            E.op("scalar", I("activation", out=sgt.ap, in_=gp.ap[:, 0:TN],
                                                                  func=AF.Silu),
                 reads=[gp], writes=[sgt])
            E.op("vector", I("tensor_tensor", out=act.ap[:, f, :], in0=sgt.ap, in1=up.ap[:, 0:TN], op=ALU.mult),
                reads=[sgt, up], writes=[act])

    def down(t):
        b = hb[t % 2]
        for dm in range(DC):
            yp = y_ps[dm % 2]
            for f in range(FC):
                E.op("tensor", I("matmul", yp.ap[:, 0:TN], lhsT=wd.ap[:, f, dm * 128:(dm + 1) * 128], rhs=act.ap[:, f, :],
                    start=(f == 0), stop=(f == FC - 1)), reads=[wd, act], writes=[yp])
            E.op("vector", I("tensor_tensor", out=b.ap[:, dm, :], in0=b.ap[:, dm, :], in1=yp.ap[:, 0:TN], op=ALU.add),
                reads=[b, yp], writes=[b])
        E.dma("sync", [(dstv[:, :, t * TN:(t + 1) * TN], b.ap)], reads=[b], writes=htrk(t),
              sem=f"d_st{t % 2}")

    load(0)
    prologue(0)
    for t in range(NT):
        if t + 1 < NT:
            load(t + 1)
        gateup(t)
        if t + 1 < NT:
            prologue(t + 1)
        down(t)
    E.barrier()


def phase_attn_proj(E, C, j, src):
    A = C.arena
    A.reset(C.arena_base)
    TN = 512
    NT = S // TN
    w_in = A.alloc("w_in", [128, DC, 1184], BF16)
    w_uq = A.alloc("w_uq", [128, 2, 768], BF16)
    w_ukv = A.alloc("w_ukv", [128, 1, 1024], BF16)
    wkr = A.alloc("wkr", [128, DC, 96], BF16)
    wuk = A.alloc("wuk", [128, 8, 96], BF16)
    gain = A.alloc("again", [128, DC], F32)
    g_qa = A.alloc("g_qa", [64, 1], F32)
    g_ka = A.alloc("g_ka", [64, 1], F32)
    g_cq = A.alloc("g_cq", [128, 2], F32)
    g_ckv = A.alloc("g_ckv", [128, 1], F32)
    g_qb = A.alloc("g_qb", [96, 1], F32)
    g_kb = A.alloc("g_kb", [96, 1], F32)
    tabs = [[A.alloc(f"tab{i}_{k}", [96, TN], F32) for k in range(4)] for i in range(2)]
    hb = [A.alloc(f"hb{i}", [128, DC, TN], F32) for i in range(2)]
    sq = A.alloc("sq", [128, DC, TN], BF16)
    xn = A.alloc("xn", [128, DC, TN], BF16)
    lnv = A.alloc("lnv", [128, TN], F32)
    rstd = A.alloc("rstd", [128, TN], F32)
    cq_raw = A.alloc("cq_raw", [128, 2, TN], F32)
    sq2 = A.alloc("sq2", [128, 2, TN], BF16)
    ln2 = A.alloc("ln2", [128, TN], F32)
    rstd2 = A.alloc("rstd2", [128, TN], F32)
    cqn = A.alloc("cqn", [128, 2, TN], BF16)
    ckvn = A.alloc("ckvn", [128, TN], BF16)
    va = A.alloc("va", [128, 4, 128], BF16)
    vb = A.alloc("vb", [128, 4, 512], BF16)
    ws = []
    for i in range(4):
        w = Ctx()
        w.usq = A.alloc(f"usq{i}", [96, TN], BF16)
        w.uln = A.alloc(f"uln{i}", [96, TN], F32)
        w.urstd = A.alloc(f"urstd{i}", [96, TN], F32)
        w.uxn = A.alloc(f"uxn{i}", [96, TN], BF16)
        w.ut1 = A.alloc(f"ut1{i}", [96, TN], F32)
        w.ut2 = A.alloc(f"ut2{i}", [96, TN], F32)
        w.uout = A.alloc(f"uout{i}", [96, TN], BF16)
        ws.append(w)
    ps = [Trk(f"ps{i}", C.psum[i].ap) for i in range(8)]
    ss_ps = ps[0]
    pj, uss, upx, pv = ps[1:4], ps[4:6], ps[6:8], ps[0]

    sm = "d_small"
    dbg = getattr(C, "dbg", 0)
    if dbg == 12:
        wload(E, w_in, C.dram["attn_w_in"][j], "d_w0")
        wload(E, w_uq, C.dram["mla_w_uq"][j], "d_w1")
        wload(E, w_ukv, C.dram["mla_w_ukv"][j], "d_w2")
        E.barrier()
        return
    if dbg == 13:
        E.op("vector", I("memset", wkr.ap, 0.0), writes=[wkr])
        E.op("vector", I("memset", wuk.ap, 0.0), writes=[wuk])
        E.op("vector", I("tensor_copy", out=wkr.ap[:, :, 64:96], in_=w_in.ap[:, :, 1152:1184]),
             reads=[w_in], writes=[wkr])
        E.barrier()
        return
    E.dma("sync", [(gain.ap, C.dram["attn_norm"][j])], writes=[gain], sem=sm, group=True)
    E.dma("sync", [(g_qa.ap, C.dram["gqa_q_norm"][j])], writes=[g_qa], sem=sm, group=True)
    E.dma("sync", [(g_ka.ap, C.dram["gqa_k_norm"][j])], writes=[g_ka], sem=sm, group=True)
    E.dma("sync", [(g_cq.ap, C.dram["mla_cq_norm"][j])], writes=[g_cq], sem=sm, group=True)
    E.dma("sync", [(g_ckv.ap, C.dram["mla_ckv_norm"][j])], writes=[g_ckv], sem=sm, group=True)
    E.dma("sync", [(g_qb.ap, C.dram["mla_q_norm"][j])], writes=[g_qb], sem=sm, group=True)
    E.dma("sync", [(g_kb.ap, C.dram["mla_k_norm"][j])], writes=[g_kb], sem=sm, group=True)
    E.group_end(sm)
    if dbg == 11:
        E.barrier()
        return
    wload(E, w_in, C.dram["attn_w_in"][j], "d_w0")
    wload(E, w_uq, C.dram["mla_w_uq"][j], "d_w1")
    wload(E, w_ukv, C.dram["mla_w_ukv"][j], "d_w2")
    E.op("vector", I("memset", wkr.ap, 0.0), writes=[wkr])
    E.op("vector", I("memset", wuk.ap, 0.0), writes=[wuk])
    E.op("vector", I("tensor_copy", out=wkr.ap[:, :, 64:96], in_=w_in.ap[:, :, 1152:1184]),
         reads=[w_in], writes=[wkr])
    ukv_h = w_ukv.ap[:, 0, :].rearrange("p (h x) -> p h x", h=8)
    if dbg != 14:
        E.op("vector", I("tensor_copy", out=wuk.ap[:, :, 0:64], in_=ukv_h[:, :, 0:64]),
             reads=[w_ukv], writes=[wuk])
    if dbg == 14 or dbg == 15:
        E.barrier()
        return

    srcv = src.rearrange("(c p) s -> p c s", p=128)
    dbg = getattr(C, "dbg", 0)
    if dbg == 1:
        E.barrier()
        return

    def load(t):
        b = hb[t % 2]
        sl = slice(t * TN, (t + 1) * TN)
        E.dma("sync", [(b.ap, srcv[:, :, sl])], reads=C.h_trk[2 * t:2 * t + 2], writes=[b], sem=f"d_ld{t % 2}")
        tb = tabs[t % 2]
        E.dma("sync", [(tb[0].ap[0:64, :], C.dram["c_CA"][:, sl])], writes=[tb[0]], sem=f"d_tb{t % 2}", group=True)
        E.dma("sync", [(tb[1].ap[0:64, :], C.dram["c_SA"][:, sl])], writes=[tb[1]], sem=f"d_tb{t % 2}", group=True)
        E.dma("sync", [(tb[2].ap, C.dram["c_CB"][:, sl])], writes=[tb[2]], sem=f"d_tb{t % 2}", group=True)
        E.dma("sync", [(tb[3].ap, C.dram["c_SB"][:, sl])], writes=[tb[3]], sem=f"d_tb{t % 2}", group=True)
        E.group_end(f"d_tb{t % 2}")

    ucount = [0]
    active = []

    def tick():
        for g in list(active):
            try:
                next(g)
            except StopIteration:
                active.remove(g)

    def drain():
        while active:
            tick()

    def push(g):
        active.append(g)
        for _ in range(3):
            tick()

    def unit(projfn, Dh, g, Pm, Ct, St, dst_ap, dst_trk):
        k = ucount[0]
        ucount[0] += 1
        w = ws[k % 4]
        pst = pj[k % 3]
        ssp, pxp = uss[k % 2], upx[k % 2]
        projfn(pst)
        yield
        E.op("scalar", I("activation", out=w.usq.ap[0:Dh, :], in_=pst.ap[0:Dh, :], func=AF.Square),
             reads=[pst], writes=[w.usq])
        yield
        E.op("tensor", I("matmul", ssp.ap[0:Dh, :], lhsT=C.ones.ap[0:Dh, 0:Dh], rhs=w.usq.ap[0:Dh, :],
                         start=True, stop=True), reads=[w.usq, C.ones], writes=[ssp])
        yield
        E.op("scalar", I("activation", out=w.uln.ap[0:Dh, :], in_=ssp.ap[0:Dh, :], func=AF.Ln,
                         scale=1.0 / Dh, bias=C.eps.ap[0:Dh, 0:1]),
             reads=[ssp, C.eps], writes=[w.uln])
        yield
        E.op("scalar", I("activation", out=w.urstd.ap[0:Dh, :], in_=w.uln.ap[0:Dh, :], func=AF.Exp,
                         scale=-0.5), reads=[w.uln], writes=[w.urstd])
        yield
        E.op("vector", I("scalar_tensor_tensor", out=w.uxn.ap[0:Dh, :], in0=pst.ap[0:Dh, :], scalar=g.ap[0:Dh, 0:1],
                         in1=w.urstd.ap[0:Dh, :], op0=ALU.mult, op1=ALU.mult),
             reads=[pst, g, w.urstd], writes=[w.uxn])
        yield
        E.op("tensor", I("matmul", pxp.ap[0:Dh, :], lhsT=Pm.ap[0:Dh, 0:Dh], rhs=w.uxn.ap[0:Dh, :],
                         start=True, stop=True), reads=[w.uxn, Pm], writes=[pxp])
        E.op("gpsimd", I("tensor_tensor", out=w.ut1.ap[0:Dh, :], in0=w.uxn.ap[0:Dh, :],
                         in1=Ct.ap[0:Dh, :], op=ALU.mult),
             reads=[w.uxn, Ct], writes=[w.ut1])
        yield
        E.op("vector", I("tensor_tensor", out=w.ut2.ap[0:Dh, :], in0=pxp.ap[0:Dh, :],
                         in1=St.ap[0:Dh, :], op=ALU.mult),
             reads=[pxp, St], writes=[w.ut2])
        yield
        E.op("gpsimd", I("tensor_tensor", out=w.uout.ap[0:Dh, :], in0=w.ut1.ap[0:Dh, :],
                         in1=w.ut2.ap[0:Dh, :], op=ALU.add),
             reads=[w.ut1, w.ut2], writes=[w.uout])
        yield
        E.dma("sync", [(dst_ap, w.uout.ap[0:Dh, :])], reads=[w.uout], writes=[dst_trk], sem=f"d_u{k % 4}")

    def win_proj(M, col0):
        def f(pst):
            for c in range(DC):
                E.op("tensor", I("matmul", pst.ap[0:M, :], lhsT=w_in.ap[:, c, col0:col0 + M], rhs=xn.ap[:, c, :],
                                 start=(c == 0), stop=(c == DC - 1)), reads=[w_in, xn], writes=[pst])
        return f

    pcount = [0]

    def proj(M, col0, cols=None):
        pst = pj[pcount[0] % 2]
        pcount[0] += 1
        for c in range(DC):
            E.op("tensor", I("matmul", pst.ap[0:M, :], lhsT=w_in.ap[:, c, col0:col0 + M],
                                                   rhs=xn.ap[:, c, :], start=(c == 0), stop=(c == DC - 1)),
                 reads=[w_in, xn], writes=[pst])
        return pst

    load(0)
    for t in range(NT):
        if t + 1 < NT:
            load(t + 1)
        sl = slice(t * TN, (t + 1) * TN)
        tb = tabs[t % 2]
        emit_rmsnorm_T(E, C, hb[t % 2], gain, sq, ss_ps, lnv, rstd, xn, TN)
        for h in range(8):
            push(unit(win_proj(64, h * 64), 64, g_qa, C.PA, tb[0], tb[1], C.QT[h, 0:64, sl], C.qt_trk))
        for g in range(2):
            push(unit(win_proj(64, 512 + g * 64), 64, g_ka, C.PA, tb[0], tb[1], C.KT[g, 0:64, sl], C.kt_trk))
        drain()
        if dbg == 5:
            E.barrier()
            return
        for sub in range(4):
            for c in range(DC):
                E.op("tensor", I("matmul", pv.ap[:, 0:128], lhsT=xn.ap[:, c, sub * 128:(sub + 1) * 128], rhs=w_in.ap[:, c, 640:768],
                    start=(c == 0), stop=(c == DC - 1)), reads=[xn, w_in], writes=[pv])
            E.op("vector", I("tensor_copy", out=va.ap[:, sub, :], in_=pv.ap[:, 0:128]),
                 reads=[pv], writes=[va])
        E.dma("sync", [(C.V[sl, 0:128].rearrange("(s p) n -> p s n", p=128), va.ap)],
              reads=[va], writes=[C.v_trk], sem="d_va")
        if dbg == 6:
            E.barrier()
            return
        cq_ps = []
        for jj in range(2):
            pst = proj(128, 768 + jj * 128)
            cq_ps.append(pst)
            E.op("scalar", I("activation", out=sq2.ap[:, jj, :], in_=pst.ap, func=AF.Square),
                 reads=[pst], writes=[sq2])
        if dbg == 71:
            E.barrier()
            return
        ssp = uss[0]
        for jj in range(2):
            E.op("tensor", I("matmul", ssp.ap, lhsT=C.ones.ap, rhs=sq2.ap[:, jj, :],
                                                     start=(jj == 0), stop=(jj == 1)),
                 reads=[sq2, C.ones], writes=[ssp])
        E.op("scalar", I("activation", out=ln2.ap, in_=ssp.ap, func=AF.Ln, scale=1.0 / 256,
                                              bias=C.eps.ap[:, 0:1]), reads=[ssp, C.eps], writes=[ln2])
        E.op("scalar", I("activation", out=rstd2.ap, in_=ln2.ap, func=AF.Exp, scale=-0.5),
             reads=[ln2], writes=[rstd2])
        if dbg == 72:
            E.barrier()
            return
        for jj in range(2):
            E.op("vector", I("scalar_tensor_tensor", out=cqn.ap[:, jj, :], in0=cq_ps[jj].ap, scalar=g_cq.ap[:, jj:jj + 1], in1=rstd2.ap,
                op0=ALU.mult, op1=ALU.mult), reads=[cq_ps[jj], g_cq, rstd2], writes=[cqn])
        if dbg == 7:
            E.barrier()
            return
        pst = proj(128, 1024)
        E.op("scalar", I("activation", out=sq2.ap[:, 0, :], in_=pst.ap, func=AF.Square),
             reads=[pst], writes=[sq2])
        ssp = uss[1]
        E.op("tensor", I("matmul", ssp.ap, lhsT=C.ones.ap, rhs=sq2.ap[:, 0, :], start=True, stop=True),
             reads=[sq2, C.ones], writes=[ssp])
        E.op("scalar", I("activation", out=ln2.ap, in_=ssp.ap, func=AF.Ln, scale=1.0 / 128,
                                              bias=C.eps.ap[:, 0:1]), reads=[ssp, C.eps], writes=[ln2])
        E.op("scalar", I("activation", out=rstd2.ap, in_=ln2.ap, func=AF.Exp, scale=-0.5),
             reads=[ln2], writes=[rstd2])
        E.op("vector", I("scalar_tensor_tensor", out=ckvn.ap, in0=pst.ap, scalar=g_ckv.ap[:, 0:1], in1=rstd2.ap, op0=ALU.mult, op1=ALU.mult),
            reads=[pst, g_ckv, rstd2], writes=[ckvn])
        if dbg == 8:
            E.barrier()
            return
        def uq_proj(h):
            def f(pst):
                for jj in range(2):
                    E.op("tensor", I("matmul", pst.ap[0:96, :], lhsT=w_uq.ap[:, jj, h * 96:(h + 1) * 96], rhs=cqn.ap[:, jj, :],
                                     start=(jj == 0), stop=(jj == 1)), reads=[w_uq, cqn], writes=[pst])
            return f

        def kb_proj(h):
            def f(pst):
                E.op("tensor", I("matmul", pst.ap[0:96, :], lhsT=wuk.ap[:, h, :], rhs=ckvn.ap, start=True, stop=False),
                     reads=[wuk, ckvn], writes=[pst])
                for c in range(DC):
                    E.op("tensor", I("matmul", pst.ap[0:96, :], lhsT=wkr.ap[:, c, :], rhs=xn.ap[:, c, :],
                                     start=False, stop=(c == DC - 1)), reads=[wkr, xn], writes=[pst])
            return f

        for h in range(8):
            push(unit(uq_proj(h), 96, g_qb, C.PB, tb[2], tb[3], C.QT[8 + h, 0:96, sl], C.qt_trk))
        for h in range(8):
            push(unit(kb_proj(h), 96, g_kb, C.PB, tb[2], tb[3], C.KT[2 + h, 0:96, sl], C.kt_trk))
        drain()
        for sub in range(4):
            E.op("tensor", I("matmul", pv.ap, lhsT=ckvn.ap[:, sub * 128:(sub + 1) * 128],
                                                       rhs=ukv_h[:, :, 64:128], start=True, stop=True),
                 reads=[ckvn, w_ukv], writes=[pv])
            E.op("vector", I("tensor_copy", out=vb.ap[:, sub, :], in_=pv.ap),
                 reads=[pv], writes=[vb])
        E.dma("sync", [(C.V[sl, 128:640].rearrange("(s p) n -> p s n", p=128), vb.ap)],
              reads=[vb], writes=[C.v_trk], sem="d_vb")
    E.barrier()


def phase_attn_core(E, C, j, src):
    A = C.arena
    A.reset(C.arena_base)
    oT_all = A.alloc("oT_all", [128, 8, S], BF16)
    w_out = A.alloc("w_out", [128, DC, D], BF16)
    ktb = [A.alloc(f"ktb{i}", [128, S], BF16) for i in range(2)]
    qtb = [A.alloc(f"qtb{i}", [128, S], BF16) for i in range(2)]
    vtb = [A.alloc(f"vtb{i}", [128, 32, 128], BF16) for i in range(2)]
    pt = [A.alloc(f"pt{i}", [128, 3, 512], BF16) for i in range(2)]
    rd = [A.alloc(f"rd{i}", [128, 512], F32) for i in range(2)]
    mark = A.top
    s_ps = [Trk(f"s_ps{i}", C.psum_all[:, i * 1536:(i + 1) * 1536]) for i in range(2)]
    o_ps = [Trk(f"o_ps{i}", C.psum[6 + i].ap) for i in range(2)]
    groups = [(k0, 3) for k0 in range(0, 30, 3)] + [(30, 2)]
    wload(E, w_out, C.dram["attn_w_out"][j], "d_w0")
    for i in range(2):
        E.op("vector", I("memset", vtb[i].ap[:, :, 64:128], 1.0), writes=[vtb[i]])
        E.op("vector", I("memset", ktb[i].ap[64:128, :], 0.0), writes=[ktb[i]])
        E.op("gpsimd", I("memset", qtb[i].ap[64:128, :], 0.0), writes=[qtb[i]])
    kv_loaded = [None, None]
    fin = 0
    for hh in range(16):
        b = hh % 2
        if hh < 8:
            Dh, kv, vc0 = 64, hh // 4, (hh // 4) * 64
        else:
            Dh, kv, vc0 = 96, 2 + (hh - 8), 128 + (hh - 8) * 64
        scale = float(Dh) ** -0.5
        E.dma("sync", [(qtb[b].ap[0:Dh, :], C.QT[hh, 0:Dh, :])], reads=[C.qt_trk], writes=[qtb[b]], sem=f"d_q{b}")
        if kv_loaded[b] != kv:
            E.dma("sync", [(ktb[b].ap[0:Dh, :], C.KT[kv, 0:Dh, :])], reads=[C.kt_trk], writes=[ktb[b]],
                  sem=f"d_k{b}")
            vsrc = C.V[:, vc0:vc0 + 64].rearrange("(k p) n -> p k n", p=128)
            E.dma("sync", [(vtb[b].ap[:, k0:k0 + 8, 0:64], vsrc[:, k0:k0 + 8, :]) for k0 in range(0, 32, 8)],
                  reads=[C.v_trk], writes=[vtb[b]], sem=f"d_v{b}")
            kv_loaded[b] = kv
        kt_, qt_, vt_ = ktb[b], qtb[b], vtb[b]
        pb = (hh % 2) * 64
        for qb in range(8):
            def smm(gi):
                sp = s_ps[gi % 2]
                k0, n = groups[gi]
                for jj in range(n):
                    kt = k0 + jj
                    E.op("tensor", I("matmul", sp.ap[:, jj * 512:(jj + 1) * 512], lhsT=kt_.ap[:, kt * 128:(kt + 1) * 128],
                                     rhs=qt_.ap[:, qb * 512:(qb + 1) * 512], start=True, stop=True),
                         reads=[kt_, qt_], writes=[sp])
            op_ = o_ps[fin % 2]
            rdt = rd[fin % 2]
            fin += 1
            smm(0)
            for gi, (k0, n) in enumerate(groups):
                if gi + 1 < len(groups):
                    smm(gi + 1)
                sp, pp = s_ps[gi % 2], pt[gi % 2]
                E.op("scalar", I("activation", out=pp.ap.rearrange("p a b -> p (a b)")[:, 0:n * 512], in_=sp.ap[:, 0:n * 512],
                                 func=AF.Exp, scale=scale), reads=[sp], writes=[pp])
                for jj in range(n):
                    kt = k0 + jj
                    E.op("tensor", I("matmul", op_.ap, lhsT=vt_.ap[:, kt, :], rhs=pp.ap[:, jj, :],
                                     start=(kt == 0), stop=(kt == 31)), reads=[pp, vt_], writes=[op_])
            E.op("vector", I("reciprocal", out=rdt.ap[64:128, :], in_=op_.ap[64:128, :]), reads=[op_], writes=[rdt])
            E.op("vector", I("tensor_tensor", out=oT_all.ap[pb:pb + 64, hh // 2, qb * 512:(qb + 1) * 512],
                             in0=op_.ap[0:64, :], in1=rdt.ap[64:128, :], op=ALU.mult),
                 reads=[op_, rdt], writes=[oT_all])
    E.barrier()

    A.reset(mark)
    TN = 512
    hb = [A.alloc(f"hb{i}", [128, DC, TN], F32) for i in range(2)]
    ps = [Trk(f"ps{i}", C.psum[i].ap) for i in range(8)]
    srcv = src.rearrange("(c p) s -> p c s", p=128)
    dstv = C.hT.rearrange("(c p) s -> p c s", p=128)

    def load(t):
        E.dma("sync", [(hb[t % 2].ap, srcv[:, :, t * TN:(t + 1) * TN])], reads=C.h_trk[2 * t:2 * t + 2],
              writes=[hb[t % 2]], sem=f"d_ld{t % 2}")

    load(0)
    for t in range(S // TN):
        if t + 1 < S // TN:
            load(t + 1)
        b = hb[t % 2]
        for dm in range(DC):
            yp = ps[dm % 4]
            for c in range(DC):
                E.op("tensor", I("matmul", yp.ap, lhsT=w_out.ap[:, c, dm * 128:(dm + 1) * 128],
                                 rhs=oT_all.ap[:, c, t * TN:(t + 1) * TN], start=(c == 0), stop=(c == DC - 1)),
                     reads=[w_out, oT_all], writes=[yp])
            E.op("vector", I("tensor_tensor", out=b.ap[:, dm, :], in0=b.ap[:, dm, :], in1=yp.ap, op=ALU.add),
                 reads=[b, yp], writes=[b])
        E.dma("sync", [(dstv[:, :, t * TN:(t + 1) * TN], b.ap)], reads=[b], writes=C.h_trk[2 * t:2 * t + 2],
              sem=f"d_st{t % 2}")
    E.barrier()


LN_LO = float(np.log(np.float32(1e-6)))
LN_HI = float(np.log(np.float32(1.0) - np.float32(1e-6)))


def _interleave(*gens):
    act = [g for g in gens if g is not None]
    while act:
        for g in list(act):
            try:
                next(g)
            except StopIteration:
                act.remove(g)


def phase_rec_dir(E, C, j, src, dirn):
    A = C.arena
    A.reset(C.arena_base)
    TN = 512
    NT = S // TN
    bwd = dirn == 1
    win = C.dram["rec_w_in"][j]
    wq = A.alloc("wq", [128, DC, 1024], BF16)
    wz = A.alloc("wz", [128, DC, 1024], BF16)
    wi = A.alloc("wi", [128, DC, 1024], BF16)
    if bwd:
        wg = A.alloc("wg", [128, DC, 1024], BF16)
        w_out = A.alloc("w_out", [128, DC, D], BF16)
        g_o = A.alloc("g_o", [128, 1], F32)
    gain = A.alloc("rgain", [128, DC], F32)
    lbv = A.alloc("lbv", [128, 8], F32)
    l0 = A.alloc("l0", [128, 8], F32)
    l1 = A.alloc("l1", [128, 8], F32)
    one = A.alloc("one", [128, 1], F32)
    mask = A.alloc("mask", [128, 128], F32)
    rst = A.alloc("rst", [128, TN], F32)
    hb = [A.alloc(f"hb{i}", [128, DC, TN], F32) for i in range(2 if not bwd else 1)]
    sq = A.alloc("sq", [128, DC, TN], BF16)
    xn = A.alloc("xn", [128, DC, TN], BF16)
    lnv = A.alloc("lnv", [128, TN], F32)
    rstd = A.alloc("rstd", [128, TN], F32)
    vtok = A.alloc("vtok", [128, 4, 1024], BF16)
    Ws = [[A.alloc(f"W{k}_{i}", [128, TN], F32) for i in range(6)] for k in range(2)]
    ek = A.alloc("ek", [128, TN], F32)
    eqs = [A.alloc(f"eq{i}", [128, TN], F32) for i in range(2)]
    qtls = [A.alloc(f"qtl{i}", [128, TN], BF16) for i in range(2)]
    ktl = A.alloc("ktl", [128, TN], BF16)
    ams = [A.alloc(f"am{i}", [128, 4, 128], BF16) for i in range(2)]
    ktokAs = [A.alloc(f"ktokA{i}", [128, 4, 128], BF16) for i in range(2)]
    ktokBs = [A.alloc(f"ktokB{i}", [128, 4, 128], BF16) for i in range(2)]
    state = A.alloc("state", [128, 8, 128], F32)
    smid = A.alloc("smid", [128, 128], BF16)
    tmp = A.alloc("tmp", [128, 128], F32)
    d1s = [A.alloc(f"d1{i}", [128, 8], F32) for i in range(2)]
    emids = [A.alloc(f"emid{i}", [128, 8], F32) for i in range(2)]
    if bwd:
        ofw = A.alloc("ofw", [128, TN], F32)
        osum = A.alloc("osum", [128, TN], F32)
        sqo = A.alloc("sqo", [128, TN], BF16)
        lno = A.alloc("lno", [128, TN], F32)
        rso = A.alloc("rso", [128, TN], F32)
        sgt = A.alloc("sgt", [128, TN], F32)
        ogT = A.alloc("ogT", [128, 8, TN], BF16)
    else:
        ofs = [A.alloc(f"ofs{i}", [128, TN], F32) for i in range(2)]
    ps = [Trk(f"ps{i}", C.psum[i].ap) for i in range(8)]
    ss_ps, q_ps, z_ps, a_ps, t_ps, o_ps = ps[0], ps[1], ps[2], ps[4], ps[5], ps[6]
    v_ps = ps[0]
    kv_ps = [ps[3], ps[7]]

    sm = "d_small"
    E.dma("sync", [(gain.ap, C.dram["rec_norm"][j])], writes=[gain], sem=sm, group=True)
    E.dma("sync", [(l0.ap, C.dram["rec_lb"][dirn, 0])], writes=[l0], sem=sm, group=True)
    E.dma("sync", [(l1.ap, C.dram["rec_lb"][dirn, 1])], writes=[l1], sem=sm, group=True)
    E.dma("sync", [(mask.ap, C.dram["c_maskb" if bwd else "c_maskf"])], writes=[mask], sem=sm, group=True)
    E.dma("sync", [(rst.ap, C.dram["c_rst"])], writes=[rst], sem=sm, group=True)
    wload(E, wi, win[:, 3072:4096], "d_w2")
    wload(E, wq, win[:, 0:1024], "d_w0")
    wload(E, wz, win[:, (2048 if bwd else 1024):(3072 if bwd else 2048)], "d_w1")
    if bwd:
        wload(E, wg, win[:, 4096:5120], "d_w3")
        wload(E, w_out, C.dram["rec_w_out"][j], "d_w4")
        E.dma("sync", [(g_o.ap, C.dram["rec_out_norm"][j])], writes=[g_o], sem=sm, group=True)
    E.group_end(sm)
    E.op("vector", I("memset", one.ap, 1.0), writes=[one])
    E.op("vector", I("memset", state.ap, 0.0), writes=[state])
    for i in range(2):
        E.op("vector", I("memset", ktokAs[i].ap, 0.0), writes=[ktokAs[i]])
        E.op("gpsimd", I("memset", ktokBs[i].ap, 0.0), writes=[ktokBs[i]])
    if j == 0:
        E.op("vector", I("memset", lbv.ap, 0.0), writes=[lbv])
    else:
        E.op("vector", I("tensor_tensor", out=l0.ap, in0=l0.ap, in1=l1.ap, op=ALU.subtract), reads=[l0, l1], writes=[l0])
        E.op("scalar", I("activation", out=l1.ap, in_=l0.ap, func=AF.Exp), reads=[l0], writes=[l1])
        E.op("scalar", I("activation", out=l0.ap, in_=l1.ap, func=AF.Ln, bias=one.ap[:, 0:1]), reads=[l1, one], writes=[l0])
        E.op("scalar", I("activation", out=lbv.ap, in_=l0.ap, func=AF.Exp, scale=-1.0), reads=[l0], writes=[lbv])

    srcv = src.rearrange("(c p) s -> p c s", p=128)
    dstv = C.hT.rearrange("(c p) s -> p c s", p=128)
    torder = list(range(NT - 1, -1, -1)) if bwd else list(range(NT))
    corder = list(range(7, -1, -1)) if bwd else list(range(8))
    ridx = 32 if bwd else 31
    lidx = 0 if bwd else 63
    mask3 = mask.ap.rearrange("p (a b) -> p a b", a=1).to_broadcast([128, 4, 128])
    porder = list(range(3, -1, -1)) if bwd else list(range(4))

    def v3(t):
        return t.ap.rearrange("p (c l) -> p c l", c=8)

    def load(i):
        t = torder[i]
        b = hb[i % len(hb)]
        E.dma("sync", [(b.ap, srcv[:, :, t * TN:(t + 1) * TN])], reads=C.h_trk[2 * t:2 * t + 2], writes=[b],
              sem=f"d_ld{i % len(hb)}")

    cnt = {"kv": 0, "cp": 0}

    def stage1a(hd):
        par = hd % 2
        hs = slice(hd * 128, (hd + 1) * 128)
        qs, u, la, lb_, Bt, Bm = Ws[par]
        for dc in range(DC):
            E.op("tensor", I("matmul", q_ps.ap, lhsT=wq.ap[:, dc, hs], rhs=xn.ap[:, dc, :],
                             start=(dc == 0), stop=(dc == DC - 1)), reads=[wq, xn], writes=[q_ps])
        for dc in range(DC):
            E.op("tensor", I("matmul", z_ps.ap, lhsT=wz.ap[:, dc, hs], rhs=xn.ap[:, dc, :],
                             start=(dc == 0), stop=(dc == DC - 1)), reads=[wz, xn], writes=[z_ps])
        yield
        E.op("scalar", I("activation", out=u.ap, in_=z_ps.ap, func=AF.Exp, scale=-1.0), reads=[z_ps], writes=[u])
        E.op("scalar", I("activation", out=la.ap, in_=u.ap, func=AF.Ln, bias=one.ap[:, 0:1]),
             reads=[u, one], writes=[la])
        E.op("scalar", I("activation", out=lb_.ap, in_=u.ap, func=AF.Ln, bias=one.ap[:, 0:1],
                         scale=lbv.ap[:, hd:hd + 1]), reads=[u, one, lbv], writes=[lb_])
        yield
        E.op("vector", I("tensor_tensor", out=lb_.ap, in0=lb_.ap, in1=la.ap, op=ALU.subtract),
             reads=[lb_, la], writes=[lb_])
        E.op("vector", I("tensor_scalar", out=lb_.ap, in0=lb_.ap, scalar1=LN_HI, scalar2=LN_LO,
                         op0=ALU.min, op1=ALU.max), reads=[lb_], writes=[lb_])
        E.op("scalar", I("activation", out=u.ap, in_=lb_.ap, func=AF.Exp), reads=[lb_], writes=[u])
        yield
        E.op("vector", I("tensor_tensor_scan", out=la.ap, data0=rst.ap, data1=lb_.ap, initial=0.0,
                         op0=ALU.mult, op1=ALU.add), reads=[rst, lb_], writes=[la])
        E.op("vector", I("tensor_scalar", out=u.ap, in0=u.ap, scalar1=-1.0, scalar2=1.0,
                         op0=ALU.mult, op1=ALU.add), reads=[u], writes=[u])
        yield
        if bwd:
            E.op("vector", I("tensor_tensor", out=Bt.ap, in0=lb_.ap, in1=la.ap, op=ALU.subtract),
                 reads=[lb_, la], writes=[Bt])
            E.op("vector", I("tensor_tensor", out=v3(Bt), in0=v3(Bt),
                             in1=v3(la)[:, :, 63:64].to_broadcast([128, 8, 64]), op=ALU.add),
                 reads=[Bt, la], writes=[Bt])
            Bsrc = Bt
        else:
            Bsrc = la
        E.op("vector", I("tensor_tensor", out=v3(Bm), in0=v3(Bsrc),
                         in1=v3(Bsrc)[:, :, ridx:ridx + 1].to_broadcast([128, 8, 64]), op=ALU.subtract),
             reads=[Bsrc], writes=[Bm])
        yield
        E.op("scalar", I("activation", out=qs.ap, in_=q_ps.ap, func=AF.Silu), reads=[q_ps], writes=[qs])
        yield

    def stage1b(hd):
        par = hd % 2
        eq, qtl, am, d1, emid = eqs[par], qtls[par], ams[par], d1s[par], emids[par]
        ktokA, ktokB = ktokAs[par], ktokBs[par]
        qs, u, la, lb_, Bt, Bm = Ws[par]
        Bsrc = Bt if bwd else la
        E.op("scalar", I("activation", out=eq.ap, in_=Bm.ap, func=AF.Exp), reads=[Bm], writes=[eq])
        E.op("scalar", I("activation", out=ek.ap, in_=Bm.ap, func=AF.Exp, scale=-1.0), reads=[Bm], writes=[ek])
        E.op("scalar", I("activation", out=d1.ap, in_=v3(Bsrc)[:, :, lidx], func=AF.Exp), reads=[Bsrc], writes=[d1])
        E.op("scalar", I("activation", out=emid.ap, in_=v3(Bsrc)[:, :, ridx], func=AF.Exp), reads=[Bsrc], writes=[emid])
        yield
        E.op("gpsimd", I("tensor_tensor", out=ktl.ap, in0=u.ap, in1=ek.ap, op=ALU.mult),
             reads=[u, ek], writes=[ktl])
        E.op("gpsimd", I("tensor_tensor", out=qtl.ap, in0=qs.ap, in1=eq.ap, op=ALU.mult),
             reads=[qs, eq], writes=[qtl])
        yield
        tpb = t_ps.ap.bitcast(BF16)
        for p in range(4):
            E.op("tensor", I("transpose", out=tpb[:, p * 128:(p + 1) * 128], in_=ktl.ap[:, p * 128:(p + 1) * 128],
                             identity=C.ident.ap), reads=[ktl, C.ident], writes=[t_ps])
        E.op("scalar", I("copy", out=ktokA.ap[0:64].rearrange("p a b -> p (a b)"), in_=tpb[0:64, 0:512]),
             reads=[t_ps], writes=[ktokA])
        E.op("scalar", I("copy", out=ktokB.ap[64:128].rearrange("p a b -> p (a b)"), in_=tpb[64:128, 0:512]),
             reads=[t_ps], writes=[ktokB])
        yield
        for p in range(4):
            ps_ = slice(p * 128, (p + 1) * 128)
            E.op("tensor", I("matmul", a_ps.ap[:, ps_], lhsT=ktl.ap[:, ps_], rhs=qtl.ap[:, ps_],
                             start=True, stop=True), reads=[ktl, qtl], writes=[a_ps])
        E.op("vector", I("tensor_tensor", out=am.ap, in0=a_ps.ap.rearrange("p (c l) -> p c l", c=4),
                         in1=mask3, op=ALU.mult), reads=[a_ps, mask], writes=[am])
        yield

    def stage2(hd, t, b):
        par = hd % 2
        hs = slice(hd * 128, (hd + 1) * 128)
        sl = slice(t * TN, (t + 1) * TN)
        eq, qtl, am, d1, emid = eqs[par], qtls[par], ams[par], d1s[par], emids[par]
        ktokA, ktokB = ktokAs[par], ktokBs[par]
        if bwd:
            E.dma("sync", [(ofw.ap, C.OB[hd, :, sl])], reads=[C.ob_trk[t]], writes=[ofw], sem="d_ofw")
        for p in porder:
            E.op("tensor", I("matmul", o_ps.ap[:, p * 128:(p + 1) * 128], lhsT=vtok.ap[:, p, hs], rhs=am.ap[:, p, :],
                             start=True, stop=False), reads=[vtok, am], writes=[o_ps])
            pair = [2 * p + 1, 2 * p] if bwd else [2 * p, 2 * p + 1]
            for ci, c in enumerate(pair):
                cs_ = slice(c * 64, (c + 1) * 64)
                kvp = kv_ps[cnt["kv"] % 2]
                cnt["kv"] += 1
                ktk = ktokA if c % 2 == 0 else ktokB
                E.op("tensor", I("matmul", kvp.ap[:, 0:128], lhsT=ktk.ap[:, p, :], rhs=vtok.ap[:, p, hs], start=True, stop=True),
                     reads=[ktk, vtok], writes=[kvp])
                E.op("vector", I("tensor_scalar", out=smid.ap, in0=state.ap[:, hd, :], scalar1=emid.ap[:, c:c + 1],
                                 scalar2=None, op0=ALU.mult), reads=[state, emid], writes=[smid])
                E.op("tensor", I("matmul", o_ps.ap[:, cs_], lhsT=smid.ap, rhs=qtl.ap[:, cs_], start=False, stop=(ci == 1)),
                     reads=[smid, qtl], writes=[o_ps])
                E.op("vector", I("tensor_scalar", out=tmp.ap, in0=kvp.ap[:, 0:128], scalar1=eq.ap[:, c * 64 + lidx:c * 64 + lidx + 1],
                                 scalar2=None, op0=ALU.mult), reads=[kvp, eq], writes=[tmp])
                E.op("vector", I("scalar_tensor_tensor", out=state.ap[:, hd, :], in0=state.ap[:, hd, :],
                                 scalar=d1.ap[:, c:c + 1], in1=tmp.ap, op0=ALU.mult, op1=ALU.add),
                     reads=[state, d1, tmp], writes=[state])
                yield
        if bwd:
            E.op("vector", I("tensor_tensor", out=osum.ap, in0=ofw.ap, in1=o_ps.ap, op=ALU.add),
                 reads=[ofw, o_ps], writes=[osum])
            yield
            return
        if not bwd:
            of = ofs[hd % 2]
            E.op("scalar", I("activation", out=of.ap, in_=o_ps.ap, func=AF.Identity), reads=[o_ps], writes=[of])
            E.dma("sync", [(C.OB[hd, :, sl], of.ap)], reads=[of], writes=[C.ob_trk[t]], sem=f"d_of{hd % 2}")
        yield

    def stage3(hd):
        hs = slice(hd * 128, (hd + 1) * 128)
        if True:
            E.op("scalar", I("activation", out=sqo.ap, in_=osum.ap, func=AF.Square), reads=[osum], writes=[sqo])
            E.op("tensor", I("matmul", v_ps.ap, lhsT=C.ones.ap, rhs=sqo.ap, start=True, stop=True),
                 reads=[sqo, C.ones], writes=[v_ps])
            yield
            E.op("scalar", I("activation", out=lno.ap, in_=v_ps.ap, func=AF.Ln, scale=1.0 / 128,
                             bias=C.eps.ap[:, 0:1]), reads=[v_ps, C.eps], writes=[lno])
            E.op("scalar", I("activation", out=rso.ap, in_=lno.ap, func=AF.Exp, scale=-0.5), reads=[lno], writes=[rso])
            E.op("vector", I("scalar_tensor_tensor", out=osum.ap, in0=osum.ap, scalar=g_o.ap[:, 0:1], in1=rso.ap,
                             op0=ALU.mult, op1=ALU.mult), reads=[osum, g_o, rso], writes=[osum])
            for dc in range(DC):
                E.op("tensor", I("matmul", v_ps.ap, lhsT=wg.ap[:, dc, hs], rhs=xn.ap[:, dc, :],
                                 start=(dc == 0), stop=(dc == DC - 1)), reads=[wg, xn], writes=[v_ps])
            yield
            E.op("scalar", I("activation", out=sgt.ap, in_=v_ps.ap, func=AF.Silu), reads=[v_ps], writes=[sgt])
            E.op("vector", I("tensor_tensor", out=ogT.ap[:, hd, :], in0=osum.ap, in1=sgt.ap, op=ALU.mult),
                 reads=[osum, sgt], writes=[ogT])
        yield

    load(0)
    for i, t in enumerate(torder):
        sl = slice(t * TN, (t + 1) * TN)
        if i + 1 < NT and len(hb) == 2:
            load(i + 1)
        b = hb[i % len(hb)]
        emit_rmsnorm_T(E, C, b, gain, sq, ss_ps, lnv, rstd, xn, TN)
        for p in range(4):
            for hg in range(2):
                for dc in range(DC):
                    E.op("tensor", I("matmul", v_ps.ap, lhsT=xn.ap[:, dc, p * 128:(p + 1) * 128],
                                     rhs=wi.ap[:, dc, hg * 512:(hg + 1) * 512], start=(dc == 0), stop=(dc == DC - 1)),
                         reads=[xn, wi], writes=[v_ps])
                if cnt["cp"] % 2 == 0:
                    E.op("vector", I("tensor_copy", out=vtok.ap[:, p, hg * 512:(hg + 1) * 512], in_=v_ps.ap),
                         reads=[v_ps], writes=[vtok])
                else:
                    E.op("scalar", I("copy", out=vtok.ap[:, p, hg * 512:(hg + 1) * 512], in_=v_ps.ap),
                         reads=[v_ps], writes=[vtok])
                cnt["cp"] += 1
        _interleave(stage1a(0))
        _interleave(stage1a(1), stage1b(0))
        for hd in range(8):
            _interleave(stage1a(hd + 2) if hd + 2 < 8 else None, stage1b(hd + 1) if hd + 1 < 8 else None,
                        stage2(hd, t, b), stage3(hd - 1) if (bwd and hd >= 1) else None)
        if bwd:
            _interleave(stage3(7))
        if bwd:
            for dm in range(DC):
                yp = ps[1 + dm % 2]
                for hd in range(8):
                    E.op("tensor", I("matmul", yp.ap, lhsT=w_out.ap[:, hd, dm * 128:(dm + 1) * 128], rhs=ogT.ap[:, hd, :],
                                     start=(hd == 0), stop=(hd == 7)), reads=[w_out, ogT], writes=[yp])
                E.op("vector", I("tensor_tensor", out=b.ap[:, dm, :], in0=b.ap[:, dm, :], in1=yp.ap, op=ALU.add),
                     reads=[b, yp], writes=[b])
            E.dma("sync", [(dstv[:, :, sl], b.ap)], reads=[b], writes=C.h_trk[2 * t:2 * t + 2], sem="d_st0")
            if i + 1 < NT:
                load(i + 1)
        elif len(hb) == 1 and i + 1 < NT:
            load(i + 1)
    E.barrier()


SMALL_SPECS = {
    "ffn_norm": [DEPTH, 128, DC],
    "attn_norm": [2, 128, DC],
    "gqa_q_norm": [2, 64, 1],
    "gqa_k_norm": [2, 64, 1],
    "mla_cq_norm": [2, 128, 2],
    "mla_ckv_norm": [2, 128, 1],
    "mla_q_norm": [2, 96, 1],
    "mla_k_norm": [2, 96, 1],
    "rec_norm": [2, 128, DC],
    "rec_lb": [2, 2, 128, DC],
    "rec_out_norm": [2, 128, 1],
    "c_ident": [128, 128],
    "c_PA": [64, 64],
    "c_PB": [96, 96],
    "c_CA": [64, S],
    "c_SA": [64, S],
    "c_CB": [96, S],
    "c_SB": [96, S],
    "c_maskf": [128, 128],
    "c_maskb": [128, 128],
    "c_rst": [128, 512],
}
WEIGHT_SPECS = {
    "ffn_w_gate": [DEPTH, D, FH],
    "ffn_w_up": [DEPTH, D, FH],
    "ffn_w_down": [DEPTH, FH, D],
    "attn_w_in": [2, D, 1184],
    "mla_w_uq": [2, 256, 768],
    "mla_w_ukv": [2, 128, 1024],
    "attn_w_out": [2, D, D],
    "rec_w_in": [2, D, 5120],
    "rec_w_out": [2, D, D],
}


def build_program(plan, debug=False, dbg=0):
    nc = bass.Bass("TRN2", target_bir_lowering=False)
    sck = "ExternalOutput" if debug else "Internal"
    E = Em()
    C = Ctx()
    C.dbg = dbg
    C.dram = {}
    xT = nc.dram_tensor("xT", [D, S], F32, kind="ExternalInput").ap()
    for name, shp in list(SMALL_SPECS.items()) + list(WEIGHT_SPECS.items()):
        C.dram[name] = nc.dram_tensor(name, shp, F32, kind="ExternalInput").ap()
    C.hT = nc.dram_tensor("yT", [D, S], F32, kind="ExternalOutput").ap()
    C.h_trk = [Trk(f"h{t}") for t in range(S // 256)]

    with ExitStack() as es:
        arena_h = es.enter_context(nc.sbuf_tensor("arena", [128, ARENA_BYTES // 4], F32))
        C.arena = Arena(arena_h, ARENA_BYTES)
        C.psum_all = es.enter_context(nc.psum_tensor("psall", [128, 4096], F32))[:, :]
        C.psum = [Trk(f"psb{i}", C.psum_all[:, i * 512:(i + 1) * 512]) for i in range(8)]
        C.ones = C.arena.alloc("ones", [128, 128], BF16)
        C.eps = C.arena.alloc("eps", [128, 1], F32)
        C.ident = C.arena.alloc("ident", [128, 128], BF16)
        C.PA = C.arena.alloc("PA", [64, 64], BF16)
        C.PB = C.arena.alloc("PB", [96, 96], BF16)
        C.arena_base = C.arena.top
        E.op("vector", I("memset", C.ones.ap, 1.0), writes=[C.ones])
        E.op("vector", I("memset", C.eps.ap, EPS), writes=[C.eps])
        E.dma("gpsimd", [(C.ident.ap, C.dram["c_ident"])], writes=[C.ident], sem="d_c0")
        E.dma("gpsimd", [(C.PA.ap, C.dram["c_PA"])], writes=[C.PA], sem="d_c1")
        E.dma("gpsimd", [(C.PB.ap, C.dram["c_PB"])], writes=[C.PB], sem="d_c2")
        C.QT = nc.dram_tensor("sc_QT", [16, 96, S], BF16, kind=sck).ap()
        C.KT = nc.dram_tensor("sc_KT", [10, 96, S], BF16, kind=sck).ap()
        C.V = nc.dram_tensor("sc_V", [S, 640], BF16, kind=sck).ap()
        C.OB = nc.dram_tensor("sc_OB", [8, 128, S], F32, kind=sck).ap()
        C.qt_trk = Trk("QT")
        C.kt_trk = Trk("KT")
        C.v_trk = Trk("V")
        C.ob_trk = [Trk(f"OB{t}") for t in range(8)]

        src = xT
        for (kind, layer) in plan:
            if kind == "ffn":
                phase_ffn(E, C, layer, src)
            elif kind == "attn":
                phase_attn_proj(E, C, layer, src)
                phase_attn_core(E, C, layer, src)
            elif kind == "attn_proj":
                phase_attn_proj(E, C, layer, src)
                continue
            elif kind == "attn_core":
                phase_attn_core(E, C, layer, src)
            elif kind == "rec_f":
                phase_rec_dir(E, C, layer, src, 0)
                continue
            elif kind == "rec_b":
                phase_rec_dir(E, C, layer, src, 1)
            elif kind == "rec":
                phase_rec_dir(E, C, layer, src, 0)
                phase_rec_dir(E, C, layer, src, 1)
            src = C.hT
        E.final_wait()

        sems = {}
        for k in E.sem_keys():
            sems[k] = es.enter_context(nc.semaphore(k))
        with nc.allow_low_precision("bf16 matmul operands, fp32 accumulation"):
            block = es.enter_context(nc.Block())
            E.replay(block, sems)
    return nc, E


def _rope_tables(rot_dim, lead):
    n_rows = S // 64
    row = np.repeat(np.arange(n_rows, dtype=np.float32), 64)
    col = np.tile(np.arange(64, dtype=np.float32), n_rows)
    sec = rot_dim // 2
    inv = (np.float32(10000.0) ** (-np.arange(0, sec, 2, dtype=np.float32) / np.float32(sec))).astype(np.float32)
    ang = np.concatenate([row[:, None] * inv, col[:, None] * inv], axis=-1).astype(np.float32)
    nf = rot_dim // 4
    Dh = lead + rot_dim
    Ct = np.ones((Dh, S), np.float32)
    St = np.zeros((Dh, S), np.float32)
    P = np.zeros((Dh, Dh), np.float32)
    for a in range(2):
        for j in range(nf):
            c = np.cos(ang[:, a * nf + j]).astype(np.float32)
            sn = np.sin(ang[:, a * nf + j]).astype(np.float32)
            i1 = lead + a * 2 * nf + j
            i2 = i1 + nf
            Ct[i1] = c
            Ct[i2] = c
            St[i1] = sn
            St[i2] = sn
            P[i2, i1] = -1.0
            P[i1, i2] = 1.0
    return Ct, St, P


def host_consts():
    out = {}
    out["c_ident"] = np.eye(128, dtype=np.float32)
    out["c_CA"], out["c_SA"], out["c_PA"] = _rope_tables(64, 0)
    out["c_CB"], out["c_SB"], out["c_PB"] = _rope_tables(32, 64)
    s_i = np.arange(128)[:, None]
    t_i = np.arange(128)[None, :]
    same = (s_i // 64) == (t_i // 64)
    out["c_maskf"] = ((s_i <= t_i) & same).astype(np.float32)
    out["c_maskb"] = ((s_i >= t_i) & same).astype(np.float32)
    rst = np.ones((128, 512), np.float32)
    rst[:, ::64] = 0.0
    out["c_rst"] = rst
    return out


def host_layout(inputs):
    out = dict(host_consts())
    f = lambda k: np.asarray(inputs[k], np.float32)
    out["ffn_norm"] = np.ascontiguousarray(f("ffn_norm").reshape(DEPTH, DC, 128).transpose(0, 2, 1))
    out["attn_norm"] = np.ascontiguousarray(f("attn_norm").reshape(2, DC, 128).transpose(0, 2, 1))
    out["rec_norm"] = np.ascontiguousarray(f("rec_norm").reshape(2, DC, 128).transpose(0, 2, 1))
    out["gqa_q_norm"] = np.ascontiguousarray(f("gqa_q_norm").reshape(2, 64, 1))
    out["gqa_k_norm"] = np.ascontiguousarray(f("gqa_k_norm").reshape(2, 64, 1))
    out["mla_cq_norm"] = np.ascontiguousarray(f("mla_cq_norm").reshape(2, 2, 128).transpose(0, 2, 1))
    out["mla_ckv_norm"] = np.ascontiguousarray(f("mla_ckv_norm").reshape(2, 128, 1))
    out["mla_q_norm"] = np.ascontiguousarray(f("mla_q_norm").reshape(2, 96, 1))
    out["mla_k_norm"] = np.ascontiguousarray(f("mla_k_norm").reshape(2, 96, 1))
    out["rec_lb"] = np.ascontiguousarray(f("rec_lower_bounds").reshape(2, 2, DC, 128).transpose(0, 1, 3, 2))
    out["rec_out_norm"] = np.ascontiguousarray(f("rec_out_norm").reshape(2, 128, 1))
    for k in WEIGHT_SPECS:
        out[k] = np.ascontiguousarray(np.asarray(inputs[k], np.float32))
    return out


FULL_PLAN = [("attn", 0), ("ffn", 0), ("rec", 0), ("ffn", 1), ("attn", 1), ("ffn", 2), ("rec", 1), ("ffn", 3)]


def kernel(**inputs):
    x = np.asarray(inputs["x"], np.float32)
    shared = host_layout(inputs)
    nc, _ = build_program(FULL_PLAN)
    in_maps = []
    for b in range(N_CORES):
        m = dict(shared)
        m["xT"] = np.ascontiguousarray(x[b].T)
        in_maps.append(m)
    res = run_bass_kernel_spmd(nc, in_maps, core_ids=list(range(N_CORES)))
    out = np.stack([np.asarray(res.results[b]["yT"]).T for b in range(N_CORES)])
    return np.ascontiguousarray(out.astype(np.float32))
```
